# Optimizing a Trainium2 kernel written in Bass

```python
import math
import jax, jax.numpy as jnp
from jax import lax
import numpy as np

D_MODEL = 1024
BATCH = 16
SEQ = 256
DEPTH = 1
DEC_BATCH = 2
DEC_SEQ = 4096
PAST_LEN = 256

GRID_W = 64
S5_WIDTH = 512
S5_GROUP = 16
S5_GROUPS = S5_WIDTH // S5_GROUP
S5_STATE = 64
FFT_WIDTH = D_MODEL - S5_WIDTH
FFT_GROUPS = 4
FFT_GROUP = FFT_WIDTH // FFT_GROUPS
N_BRANCH = 2
IN_WIDTH = S5_WIDTH + FFT_WIDTH + N_BRANCH * D_MODEL
_FF_RAW = -(-8 * D_MODEL // 3)
D_FF = -(-_FF_RAW // 256) * 256
N_MOD = 6
EPS = 1e-6

kernel_name = "hybrid_s5_fnet_flow_step"


def rmsnorm(x, g):
    x32 = x.astype(jnp.float32)
    y = x32 * lax.rsqrt(jnp.mean(x32 * x32, axis=-1, keepdims=True) + EPS)
    return (y * g.astype(jnp.float32)).astype(x.dtype)


def adaln(cvec, w_ada, b_ada):
    m = jax.nn.silu(cvec) @ w_ada + b_ada
    return m.reshape(cvec.shape[0], N_MOD, D_MODEL)


def _cmul(ar, ai, br, bi):
    return ar * br - ai * bi, ar * bi + ai * br


def _scan_op(e1, e2):
    a1r, a1i, b1r, b1i = e1
    a2r, a2i, b2r, b2i = e2
    ar, ai = _cmul(a1r, a1i, a2r, a2i)
    br, bi = _cmul(a2r, a2i, b1r, b1i)
    return ar, ai, br + b2r, bi + b2i


def s5_discretise(lam_re, lam_im, log_step, b_re, b_im):
    step = jnp.exp(log_step)[:, None]
    mag = jnp.exp(lam_re * step)
    ab_re = mag * jnp.cos(lam_im * step)
    ab_im = mag * jnp.sin(lam_im * step)
    nr = ab_re - 1.0
    ni = ab_im
    den = lam_re * lam_re + lam_im * lam_im
    fr = (nr * lam_re + ni * lam_im) / den
    fi = (ni * lam_re - nr * lam_im) / den
    bb_re = fr[..., None] * b_re - fi[..., None] * b_im
    bb_im = fr[..., None] * b_im + fi[..., None] * b_re
    return ab_re, ab_im, bb_re, bb_im


def s5_direction(u, lam_re, lam_im, log_step, b_re, b_im, c_re, c_im, h0_re, h0_im, reverse):
    ab_re, ab_im, bb_re, bb_im = s5_discretise(lam_re, lam_im, log_step, b_re, b_im)
    bu_re = jnp.einsum("nlgh,gph->nlgp", u, bb_re)
    bu_im = jnp.einsum("nlgh,gph->nlgp", u, bb_im)
    a_re = jnp.broadcast_to(ab_re, bu_re.shape)
    a_im = jnp.broadcast_to(ab_im, bu_im.shape)
    acum_re, acum_im, s_re, s_im = lax.associative_scan(
        _scan_op, (a_re, a_im, bu_re, bu_im), reverse=reverse, axis=1)
    hr, hi = _cmul(acum_re, acum_im, h0_re[:, None], h0_im[:, None])
    s_re = s_re + hr
    s_im = s_im + hi
    y = jnp.einsum("nlgp,ghp->nlgh", s_re, c_re) - jnp.einsum("nlgp,ghp->nlgh", s_im, c_im)
    end = 0 if reverse else -1
    return y, s_re[:, end], s_im[:, end]


def s5_branch(u, h0, p):
    n, l, _ = u.shape
    f32 = jnp.float32
    v = u.astype(f32).reshape(n, l, S5_GROUPS, S5_GROUP)
    h0 = h0.astype(f32)
    y = p["s5_d"].astype(f32).reshape(S5_GROUPS, S5_GROUP) * v
    finals = []
    for d in range(2):
        y_d, f_re, f_im = s5_direction(
            v,
            p["s5_lambda_re"][d].astype(f32), p["s5_lambda_im"][d].astype(f32),
            p["s5_log_step"][d].astype(f32),
            p["s5_b_re"][d].astype(f32), p["s5_b_im"][d].astype(f32),
            p["s5_c_re"][d].astype(f32), p["s5_c_im"][d].astype(f32),
            h0[:, d, 0], h0[:, d, 1], reverse=(d == 1))
        y = y + y_d
        finals.append(jnp.stack([f_re, f_im], axis=1))
    y = y.reshape(n, l, S5_WIDTH).astype(u.dtype)
    z = jax.nn.gelu(y)
    z = z * jax.nn.sigmoid(z @ p["w_glu"] + p["b_glu"])
    return z, jnp.stack(finals, axis=1)


def fourier_branch(u):
    n, l, _ = u.shape
    v = u.astype(jnp.float32).reshape(n, l, FFT_GROUPS, FFT_GROUP)
    f = jnp.fft.fft2(v, axes=(1, 3), norm="ortho").real
    return f.reshape(n, l, FFT_WIDTH).astype(u.dtype)


def layer(x, mod, h0, p):
    shift1, scale1, gate1, shift2, scale2, gate2 = [mod[:, k][:, None] for k in range(N_MOD)]
    h = rmsnorm(x, p["norm1_g"]) * (1.0 + scale1) + shift1
    z = h @ p["w_in"]
    o = 0
    u_s5 = z[..., o:o + S5_WIDTH]; o += S5_WIDTH
    u_f = z[..., o:o + FFT_WIDTH]; o += FFT_WIDTH
    g_s5 = z[..., o:o + D_MODEL]; o += D_MODEL
    g_f = z[..., o:o + D_MODEL]
    y_s5, finals = s5_branch(u_s5, h0, p)
    y_f = fourier_branch(u_f)
    m = (jax.nn.sigmoid(g_s5) * (y_s5 @ p["w_proj_s5"])
         + jax.nn.sigmoid(g_f) * (y_f @ p["w_proj_fft"]))
    x = x + gate1 * (m @ p["w_out"])
    h2 = rmsnorm(x, p["norm2_g"]) * (1.0 + scale2) + shift2
    ff = (jax.nn.silu(h2 @ p["w_ffn_gate"]) * (h2 @ p["w_ffn_up"])) @ p["w_ffn_down"]
    x = x + gate2 * ff
    return x, finals


def setup_inputs(seed: int = 0) -> dict:
    key = jax.random.key(seed)
    ks = iter(jax.random.split(key, 40))
    f32 = jnp.float32
    D, G, P, H = D_MODEL, S5_GROUPS, S5_STATE, S5_GROUP

    def nrm(shape, scale):
        return jax.random.normal(next(ks), shape, f32) * scale

    lam_im_base = jnp.pi * jnp.arange(P, dtype=f32)
    return {
        "x_prompt": nrm((BATCH, SEQ, D), 1.0),
        "x_sample": nrm((DEC_BATCH, DEC_SEQ, D), 1.0),
        "state_s5": nrm((DEC_BATCH, DEPTH, 2, 2, G, P), 0.5),
        "c": nrm((DEC_BATCH, D), 1.0),
        "c_ctx": nrm((D,), 1.0),
        "norm1_g": 1.0 + nrm((DEPTH, D), 0.02),
        "norm2_g": 1.0 + nrm((DEPTH, D), 0.02),
        "w_ada": nrm((DEPTH, D, N_MOD * D), 0.5 * D ** -0.5),
        "b_ada": nrm((DEPTH, N_MOD * D), 0.01),
        "w_in": nrm((DEPTH, D, IN_WIDTH), D ** -0.5),
        "s5_lambda_re": -0.5 + nrm((DEPTH, 2, G, P), 0.01),
        "s5_lambda_im": lam_im_base + nrm((DEPTH, 2, G, P), 0.01),
        "s5_log_step": jax.random.uniform(next(ks), (DEPTH, 2, G), f32,
                                          math.log(1e-3), math.log(1e-1)),
        "s5_b_re": nrm((DEPTH, 2, G, P, H), (2 * H) ** -0.5),
        "s5_b_im": nrm((DEPTH, 2, G, P, H), (2 * H) ** -0.5),
        "s5_c_re": nrm((DEPTH, 2, G, H, P), P ** -0.5),
        "s5_c_im": nrm((DEPTH, 2, G, H, P), P ** -0.5),
        "s5_d": nrm((DEPTH, S5_WIDTH), 1.0),
        "w_glu": nrm((DEPTH, S5_WIDTH, S5_WIDTH), S5_WIDTH ** -0.5),
        "b_glu": nrm((DEPTH, S5_WIDTH), 0.01),
        "w_proj_s5": nrm((DEPTH, S5_WIDTH, D), S5_WIDTH ** -0.5),
        "w_proj_fft": nrm((DEPTH, FFT_WIDTH, D), FFT_WIDTH ** -0.5),
        "w_out": nrm((DEPTH, D, D), D ** -0.5),
        "w_ffn_gate": nrm((DEPTH, D, D_FF), D ** -0.5),
        "w_ffn_up": nrm((DEPTH, D, D_FF), D ** -0.5),
        "w_ffn_down": nrm((DEPTH, D_FF, D), D_FF ** -0.5),
        "final_norm_g": 1.0 + nrm((D,), 0.02),
    }


def reference(x_prompt, x_sample, state_s5, c, c_ctx, norm1_g, norm2_g, w_ada, b_ada, w_in,
              s5_lambda_re, s5_lambda_im, s5_log_step, s5_b_re, s5_b_im, s5_c_re, s5_c_im,
              s5_d, w_glu, b_glu, w_proj_s5, w_proj_fft, w_out, w_ffn_gate, w_ffn_up,
              w_ffn_down, final_norm_g):
    xp = x_prompt
    xs = x_sample
    n_ctx = x_prompt.shape[0]
    zero_state = jnp.zeros((n_ctx, 2, 2, S5_GROUPS, S5_STATE), jnp.float32)
    new_states = []
    for i in range(DEPTH):
        p = {
            "norm1_g": norm1_g[i], "norm2_g": norm2_g[i], "w_in": w_in[i],
            "s5_lambda_re": s5_lambda_re[i], "s5_lambda_im": s5_lambda_im[i],
            "s5_log_step": s5_log_step[i], "s5_b_re": s5_b_re[i], "s5_b_im": s5_b_im[i],
            "s5_c_re": s5_c_re[i], "s5_c_im": s5_c_im[i], "s5_d": s5_d[i],
            "w_glu": w_glu[i], "b_glu": b_glu[i], "w_proj_s5": w_proj_s5[i],
            "w_proj_fft": w_proj_fft[i], "w_out": w_out[i], "w_ffn_gate": w_ffn_gate[i],
            "w_ffn_up": w_ffn_up[i], "w_ffn_down": w_ffn_down[i],
        }
        mod_ctx = adaln(c_ctx[None], w_ada[i], b_ada[i])
        mod_lat = adaln(c, w_ada[i], b_ada[i])
        xp, finals_ctx = layer(xp, mod_ctx, zero_state, p)
        new_states.append(finals_ctx)
        xs, _ = layer(xs, mod_lat, state_s5[:, i], p)
    y_prompt = rmsnorm(xp, final_norm_g)
    y_sample = rmsnorm(xs, final_norm_g)
    new_state_s5 = jnp.stack(new_states, axis=1)
    return (y_prompt, y_sample, new_state_s5)
```

```python
import math
from contextlib import ExitStack
import numpy as np
import ml_dtypes
import concourse.bass as bass
import concourse.mybir as mybir
from concourse.bass_utils import run_bass_kernel_spmd

F32 = mybir.dt.float32
BF16 = mybir.dt.bfloat16
I32 = mybir.dt.int32
AF = mybir.ActivationFunctionType
ALU = mybir.AluOpType

D = 1024
KT = 8
LL = 4096
LS = 256
DFF = 2816
FT = DFF // 128
EPS = 1e-6
PI = math.pi
NB = 256
SB = 64
LF = 3072
LO = 1024
NT = LO + 2 * LS
TL = 4
FB = 128
NFB = LF // FB
NK = FB // TL


class Tl:
    def __init__(self, t, name="", psum=False):
        self.t = t
        self.name = name
        self.psum = psum
        self.w = None
        self.r = []

    def __getitem__(self, idx):
        return self.t[idx]


class _Sub:
    def __init__(self, parent, n, nd):
        self.p = parent
        self.n = n
        self.nd = nd
        self.psum = parent.psum

    @property
    def w(self):
        return self.p.w

    @w.setter
    def w(self, v):
        self.p.w = v

    @property
    def r(self):
        return self.p.r

    @r.setter
    def r(self, v):
        self.p.r = v

    def __getitem__(self, idx):
        if self.nd == 2:
            base = self.p.t[:, 0:self.n]
        else:
            base = self.p.t[:, :, 0:self.n]
        return base[idx]


class FW:
    def __init__(self, nc, es, ndma=32):
        self.nc = nc
        self.engs = {"pe": nc.tensor, "act": nc.scalar, "dve": nc.vector, "pool": nc.gpsimd, "sp": nc.sync}
        self.sems = {}
        self.cnt = {}
        self.waited = {}
        for k in self.engs:
            self.sems[k] = es.enter_context(nc.semaphore("s_" + k))
            self.cnt[k] = 0
        self.dsem = []
        for i in range(ndma):
            k = "d%d" % i
            self.sems[k] = es.enter_context(nc.semaphore(k))
            self.cnt[k] = 0
            self.dsem.append(k)
        self.di = 0
        self.asem = []
        for i in range(4):
            k = "a%d" % i
            self.sems[k] = es.enter_context(nc.semaphore(k))
            self.cnt[k] = 0
            self.asem.append(k)
        self.ai = 0
        self.gsem = []
        for i in range(8):
            k = "g%d" % i
            self.sems[k] = es.enter_context(nc.semaphore(k))
            self.cnt[k] = 0
            self.gsem.append(k)
        self.gi = 0
        self.pending = {k: 0 for k in self.engs}
        self.nosync = {"pe", "pool", "act"}

    def _need(self, e, deps):
        mx = {}
        for d in deps:
            if d is None:
                continue
            k, c = d
            if e == k and e in self.nosync:
                continue
            if c > mx.get(k, 0):
                mx[k] = c
        for k, c in mx.items():
            if self.waited.get((e, k), 0) >= c:
                continue
            self.engs[e].wait_ge(self.sems[k], c)
            self.waited[(e, k)] = c

    def op(self, e, fn, outs=(), ins=(), inc=True):
        deps = []
        outs = list(outs) + [t for t in ins if t.psum and t not in outs]
        for t in ins:
            deps.append(t.w)
        for t in outs:
            deps.append(t.w)
            deps.extend(t.r)
        self._need(e, deps)
        inst = fn(self.engs[e])
        tag = (e, self.cnt[e] + 1)
        if inc:
            self.cnt[e] += 1
            inst.then_inc(self.sems[e], 1)
        for t in outs:
            t.w = tag
            t.r = []
        for t in ins:
            t.r.append(tag)
        return inst

    def dma(self, out_ap, in_ap, outs=(), ins=(), e="sp", detached=False):
        if detached:
            k = self.asem[self.ai % len(self.asem)]
            self.ai += 1
        elif e == "pool":
            k = self.gsem[self.gi % len(self.gsem)]
            self.gi += 1
        else:
            k = self.dsem[self.di % len(self.dsem)]
            self.di += 1
        deps = [(k, self.cnt[k])] if self.cnt[k] else []
        for t in ins:
            deps.append(t.w)
        for t in outs:
            deps.append(t.w)
            deps.extend(t.r)
        self._need(e, deps)
        inst = self.engs[e].dma_start(out=out_ap, in_=in_ap)
        self.cnt[k] += 16
        inst.then_inc(self.sems[k], 16)
        tag = (k, self.cnt[k])
        for t in outs:
            t.w = tag
            t.r = []
        for t in ins:
            t.r.append(tag)

    def barrier(self, skip=()):
        allc = [(k, c) for k, c in self.cnt.items() if c > 0 and k not in skip and not (skip and k[0] == "a")]
        for e in self.engs:
            if e in skip:
                continue
            self._need(e, [d for d in allc if d[0] != e])


def build(stop=None, debug=False):
    nc = bass.Bass("TRN2", target_bir_lowering=False)
    es = ExitStack()
    fw = FW(nc, es)

    def din(name, shape, dt=F32):
        return nc.dram_tensor(name, shape, dt, kind="ExternalInput").ap()

    def dout(name, shape, dt=F32):
        return nc.dram_tensor(name, shape, dt, kind="ExternalOutput").ap()

    def dscr(name, shape, dt):
        return nc.dram_tensor(name, shape, dt, kind="Internal").ap()

    xf_d = din("xf", [128, KT, LF])
    xo_d = din("xo", [128, KT, NT])
    cT_d = din("cT", [128, KT, 2])
    wada_d = din("w_ada", [128, KT, 6 * D])
    bada_d = din("b_ada", [128, 48])
    g1_d = din("g1", [128, KT]); g2_d = din("g2", [128, KT]); gf_d = din("gf", [128, KT])
    win_d = din("w_in", [128, KT, 3 * D])
    wglu_d = din("w_glu", [128, 4, 512]); bglu_d = din("b_glu", [128, 4]); s5d_d = din("s5d", [128, 4])
    wp5_d = din("w_p5", [128, 4, D]); wpf_d = din("w_pf", [128, 4, D]); wout_d = din("w_out", [128, KT, D])
    wg_d = din("w_g", [128, KT, DFF]); wu_d = din("w_u", [128, KT, DFF]); wd_d = din("w_d", [128, FT, D])
    lrs_d = din("lrs", [128, 32]); lis_d = din("lis", [128, 32]); lss_d = din("lss", [128, 32])
    lrq_d = din("lrq", [128, 128]); liq_d = din("liq", [128, 128]); lsq_d = din("lsq", [128, 128])
    brp_d = din("brp", [128, 4096]); bip_d = din("bip", [128, 4096])
    crp_d = din("crp", [128, 4096]); cip_d = din("cip", [128, 4096])
    h0l_d = din("h0l", [128, 64]); h0s_d = din("h0s", [128, 64])
    rampb_d = din("rampb", [128, NK + 1]); rampf_d = din("rampf", [128, NK])
    capf_d = din("capf", [128, NFB + 1]); capb_d = din("capb", [128, NFB + 1])
    cml_d = din("cml", [LO // NB, 128, LL // 128, NB], BF16); sml_d = din("sml", [LO // NB, 128, LL // 128, NB], BF16)
    cms_d = din("cms", [1, 128, LS // 128, NB], BF16); sms_d = din("sms", [1, 128, LS // 128, NB], BF16)
    ccl_d = din("ccl", [128, 128], BF16); nscl_d = din("nscl", [128, 128], BF16)
    ccs_d = din("ccs", [128, 128], BF16); nscs_d = din("nscs", [128, 128], BF16)
    ones_d = din("ones", [128, 128], BF16)
    yo_d = dout("yo", [128, KT, NT])
    st_d = dout("st", [2, 128, 64])
    ya_d = dscr("ya_s", [128, 4, NT], BF16); yb_d = dscr("yb_s", [128, 4, NT], BF16); yf_d = dscr("yf_s", [128, 4, NT], BF16)
    x1_d = dscr("x1_s", [128, KT, NT], F32)
    bp_d = dscr("bp_s", [128, TL * 32 * 2 * 128], BF16); cpd_d = dscr("cp_s", [128, 32 * 2 * 128], BF16)

    uid = [0]

    def sb(stack, name, shape, dt):
        uid[0] += 1
        name = "%s_%d" % (name, uid[0])
        return Tl(stack.enter_context(nc.sbuf_tensor(name, shape, dt)), name)

    def ps(stack, name, shape, dt=F32):
        uid[0] += 1
        name = "%s_%d" % (name, uid[0])
        return Tl(stack.enter_context(nc.psum_tensor(name, shape, dt)), name, psum=True)

    def tt(e, out, oap, a, aap, b, bap, op):
        fw.op(e, lambda g: g.tensor_tensor(out=oap, in0=aap, in1=bap, op=op), outs=[out], ins=[a, b])

    def tsc(e, out, oap, a, aap, s1, op0, s2=None, op1=None):
        if op1 is None:
            fw.op(e, lambda g: g.tensor_scalar(out=oap, in0=aap, scalar1=s1, scalar2=None, op0=op0),
                  outs=[out], ins=[a])
        else:
            fw.op(e, lambda g: g.tensor_scalar(out=oap, in0=aap, scalar1=s1, scalar2=s2, op0=op0, op1=op1),
                  outs=[out], ins=[a])

    def stt(out, oap, a, aap, scal, b, bap, op0, op1, extra=()):
        fw.op("dve", lambda g: g.scalar_tensor_tensor(out=oap, in0=aap, scalar=scal, in1=bap, op0=op0, op1=op1),
              outs=[out], ins=[a, b] + list(extra))

    def act(out, oap, a, aap, func, bias=None, scale=None, extra=()):
        kw = {}
        if bias is not None:
            kw["bias"] = bias
        if scale is not None:
            kw["scale"] = scale
        fw.op("act", lambda g: g.activation(out=oap, in_=aap, func=func, **kw), outs=[out], ins=[a] + list(extra))

    def cp(e, out, oap, a, aap):
        fw.op(e, lambda g: g.tensor_copy(out=oap, in_=aap), outs=[out], ins=[a])

    def mm(out, oap, l, lap, r, rap, start, stop):
        fw.op("pe", lambda g: g.matmul(oap, lhsT=lap, rhs=rap, start=start, stop=stop),
              outs=[out], ins=[l, r], inc=stop)


    P = es
    ones = sb(P, "ones", [128, 128], BF16)
    fw.dma(ones[:], ones_d[:, :], outs=[ones])
    A1 = sb(P, "A1", [128, KT, 2], F32); B1 = sb(P, "B1", [128, KT, 2], F32); G1 = sb(P, "G1", [128, KT, 2], F32)
    A2 = sb(P, "A2", [128, KT, 2], F32); B2 = sb(P, "B2", [128, KT, 2], F32); G2 = sb(P, "G2", [128, KT, 2], F32)
    gft = sb(P, "gft", [128, KT], F32)
    epst = sb(P, "epst", [128, 1], F32)
    fw.op("pool", lambda g: g.memset(epst[:], EPS), outs=[epst])
    fw.dma(gft[:], gf_d[:, :], outs=[gft])
    bglu = sb(P, "bglu", [128, 4], F32); s5d = sb(P, "s5dt", [128, 4], F32)
    fw.dma(bglu[:], bglu_d[:, :], outs=[bglu]); fw.dma(s5d[:], s5d_d[:, :], outs=[s5d])
    capF = sb(P, "capF", [128, NFB + 1], F32); capB = sb(P, "capB", [128, NFB + 1], F32)
    fw.dma(capF[:], capf_d[:, :], outs=[capF]); fw.dma(capB[:], capb_d[:, :], outs=[capB])
    PWR = sb(P, "PWR", [128, 32, 5], F32); PWI = sb(P, "PWI", [128, 32, 5], F32); NPWI = sb(P, "NPWI", [128, 32, 5], F32)
    PWRr = sb(P, "PWRr", [128, 32, 4], F32); PWIr = sb(P, "PWIr", [128, 32, 4], F32); NPWIr = sb(P, "NPWIr", [128, 32, 4], F32)
    PIS4 = sb(P, "PIS4", [128, 32, 2], F32)
    PISf = sb(P, "PISf", [128, 32, 2, 4], F32)
    PISr = sb(P, "PISr", [128, 32, 2, 4], F32)
    ACC = [sb(P, "ACC%d" % d, [128, 16, 2], F32) for d in range(2)]
    ZST = [sb(P, "ZST%d" % d, [128, 16, 2], F32) for d in range(2)]
    H0L = [sb(P, "H0L%d" % d, [128, 16, 2], F32) for d in range(2)]
    for d in range(2):
        fw.dma(H0L[d][:], h0l_d[:, d * 32:(d + 1) * 32].rearrange("p (a b) -> p a b", b=2), outs=[H0L[d]])
        fw.dma(ZST[d][:], h0s_d[:, d * 32:(d + 1) * 32].rearrange("p (a b) -> p a b", b=2), outs=[ZST[d]])

    def discretize(S, n, LR, LI, LSt, pre):
        t = {}
        for nm in ["step", "dr", "di", "mag", "q", "r", "m", "s", "c", "are", "aim", "nr", "den", "fr", "fi", "tmp"]:
            t[nm] = sb(S, pre + nm, [128, n], F32)
        qi = sb(S, pre + "qi", [128, n], I32)
        act(t["step"], t["step"][:], LSt, LSt[:], AF.Exp)
        tt("dve", t["dr"], t["dr"][:], LR, LR[:], t["step"], t["step"][:], ALU.mult)
        tt("dve", t["di"], t["di"][:], LI, LI[:], t["step"], t["step"][:], ALU.mult)
        act(t["mag"], t["mag"][:], t["dr"], t["dr"][:], AF.Exp)
        tsc("dve", t["q"], t["q"][:], t["di"], t["di"][:], 1.0 / (2 * PI), ALU.mult)
        cp("dve", qi, qi[:], t["q"], t["q"][:])
        cp("dve", t["q"], t["q"][:], qi, qi[:])
        stt(t["r"], t["r"][:], t["q"], t["q"][:], -2 * PI, t["di"], t["di"][:], ALU.mult, ALU.add)
        tsc("dve", t["m"], t["m"][:], t["r"], t["r"][:], PI, ALU.is_gt)
        stt(t["r"], t["r"][:], t["m"], t["m"][:], -2 * PI, t["r"], t["r"][:], ALU.mult, ALU.add)
        tsc("dve", t["m"], t["m"][:], t["r"], t["r"][:], -PI, ALU.is_lt)
        stt(t["r"], t["r"][:], t["m"], t["m"][:], 2 * PI, t["r"], t["r"][:], ALU.mult, ALU.add)
        act(t["s"], t["s"][:], t["r"], t["r"][:], AF.Sin)
        tsc("dve", t["q"], t["q"][:], t["r"], t["r"][:], PI / 2, ALU.add)
        tsc("dve", t["m"], t["m"][:], t["q"], t["q"][:], PI, ALU.is_gt)
        stt(t["q"], t["q"][:], t["m"], t["m"][:], -2 * PI, t["q"], t["q"][:], ALU.mult, ALU.add)
        act(t["c"], t["c"][:], t["q"], t["q"][:], AF.Sin)
        tt("dve", t["are"], t["are"][:], t["mag"], t["mag"][:], t["c"], t["c"][:], ALU.mult)
        tt("dve", t["aim"], t["aim"][:], t["mag"], t["mag"][:], t["s"], t["s"][:], ALU.mult)
        tsc("dve", t["nr"], t["nr"][:], t["are"], t["are"][:], -1.0, ALU.add)
        tt("dve", t["den"], t["den"][:], LR, LR[:], LR, LR[:], ALU.mult)
        tt("dve", t["tmp"], t["tmp"][:], LI, LI[:], LI, LI[:], ALU.mult)
        tt("dve", t["den"], t["den"][:], t["den"], t["den"][:], t["tmp"], t["tmp"][:], ALU.add)
        fw.op("dve", lambda g: g.reciprocal(out=t["den"][:], in_=t["den"][:]), outs=[t["den"]], ins=[t["den"]])
        tt("dve", t["fr"], t["fr"][:], t["nr"], t["nr"][:], LR, LR[:], ALU.mult)
        tt("dve", t["tmp"], t["tmp"][:], t["aim"], t["aim"][:], LI, LI[:], ALU.mult)
        tt("dve", t["fr"], t["fr"][:], t["fr"], t["fr"][:], t["tmp"], t["tmp"][:], ALU.add)
        tt("dve", t["fr"], t["fr"][:], t["fr"], t["fr"][:], t["den"], t["den"][:], ALU.mult)
        tt("dve", t["fi"], t["fi"][:], t["aim"], t["aim"][:], LR, LR[:], ALU.mult)
        tt("dve", t["tmp"], t["tmp"][:], t["nr"], t["nr"][:], LI, LI[:], ALU.mult)
        tt("dve", t["fi"], t["fi"][:], t["fi"], t["fi"][:], t["tmp"], t["tmp"][:], ALU.subtract)
        tt("dve", t["fi"], t["fi"][:], t["fi"], t["fi"][:], t["den"], t["den"][:], ALU.mult)
        return t["are"], t["aim"], t["fr"], t["fi"]


    def cmul(S, n, ar, ai, br, bi, pre):
        orr = sb(S, pre + "re", [128, n], F32); oi = sb(S, pre + "im", [128, n], F32); t_ = sb(S, pre + "t", [128, n], F32)
        tt("dve", orr, orr[:], ar, ar[:], br, br[:], ALU.mult)
        tt("dve", t_, t_[:], ai, ai[:], bi, bi[:], ALU.mult)
        tt("dve", orr, orr[:], orr, orr[:], t_, t_[:], ALU.subtract)
        tt("dve", oi, oi[:], ar, ar[:], bi, bi[:], ALU.mult)
        tt("dve", t_, t_[:], ai, ai[:], br, br[:], ALU.mult)
        tt("dve", oi, oi[:], oi, oi[:], t_, t_[:], ALU.add)
        return orr, oi

    def expi(S, shape, dr, drap, di, diap, pre):
        t = {}
        for nm in ["mag", "q", "r", "m", "s", "c"]:
            t[nm] = sb(S, pre + nm, shape, F32)
        qi = sb(S, pre + "qi", shape, I32)
        A_ = lambda T_: T_[:]
        act(t["mag"], A_(t["mag"]), dr, drap, AF.Exp)
        tsc("dve", t["q"], A_(t["q"]), di, diap, 1.0 / (2 * PI), ALU.mult)
        cp("dve", qi, qi[:], t["q"], A_(t["q"]))
        cp("dve", t["q"], A_(t["q"]), qi, qi[:])
        stt(t["r"], A_(t["r"]), t["q"], A_(t["q"]), -2 * PI, di, diap, ALU.mult, ALU.add)
        tsc("dve", t["m"], A_(t["m"]), t["r"], A_(t["r"]), PI, ALU.is_gt)
        stt(t["r"], A_(t["r"]), t["m"], A_(t["m"]), -2 * PI, t["r"], A_(t["r"]), ALU.mult, ALU.add)
        tsc("dve", t["m"], A_(t["m"]), t["r"], A_(t["r"]), -PI, ALU.is_lt)
        stt(t["r"], A_(t["r"]), t["m"], A_(t["m"]), 2 * PI, t["r"], A_(t["r"]), ALU.mult, ALU.add)
        act(t["s"], A_(t["s"]), t["r"], A_(t["r"]), AF.Sin)
        tsc("dve", t["q"], A_(t["q"]), t["r"], A_(t["r"]), PI / 2, ALU.add)
        tsc("dve", t["m"], A_(t["m"]), t["q"], A_(t["q"]), PI, ALU.is_gt)
        stt(t["q"], A_(t["q"]), t["m"], A_(t["m"]), -2 * PI, t["q"], A_(t["q"]), ALU.mult, ALU.add)
        act(t["c"], A_(t["c"]), t["q"], A_(t["q"]), AF.Sin)
        tt("dve", t["c"], A_(t["c"]), t["mag"], A_(t["mag"]), t["c"], A_(t["c"]), ALU.mult)
        tt("dve", t["s"], A_(t["s"]), t["mag"], A_(t["mag"]), t["s"], A_(t["s"]), ALU.mult)
        return t["c"], t["s"]

    with ExitStack() as SA:
        S = SA
        cTf = sb(S, "cTf", [128, KT, 2], F32)
        fw.dma(cTf[:], cT_d[:, :, :], outs=[cTf])
        cT = sb(S, "cT", [128, KT, 2], BF16)
        act(cT, cT[:], cTf, cTf[:], AF.Silu)
        pm = ps(S, "pm", [128, 48, 2])
        was = [sb(S, "wa%d" % i, [128, KT, 512], BF16) for i in range(4)]
        for j in range(12):
            wa = was[j % 4]
            fw.dma(wa[:], wada_d[:, :, j * 512:(j + 1) * 512], outs=[wa], e="pool", detached=True)
            for ft in range(4):
                f = j * 4 + ft
                for kt in range(KT):
                    mm(pm, pm[:, f, :], wa, wa[:, kt, ft * 128:(ft + 1) * 128], cT, cT[:, kt, :], kt == 0, kt == KT - 1)
        with ExitStack() as S:
            lrs = sb(S, "lrs", [128, 32], F32); lis = sb(S, "lis", [128, 32], F32); lss = sb(S, "lss", [128, 32], F32)
            fw.dma(lrs[:], lrs_d[:, :], outs=[lrs]); fw.dma(lis[:], lis_d[:, :], outs=[lis]); fw.dma(lss[:], lss_d[:, :], outs=[lss])
            are, aim, _, _ = discretize(S, 32, lrs, lis, lss, "ds_")
            fw.op("dve", lambda g: g.memset(PWR[:, :, 0:1], 1.0), outs=[PWR])
            fw.op("dve", lambda g: g.memset(PWI[:, :, 0:1], 0.0), outs=[PWI])
            cp("dve", PWR, PWR[:, :, 1], are, are[:]); cp("dve", PWI, PWI[:, :, 1], aim, aim[:])
            t1 = sb(S, "pt1", [128, 32], F32); t2 = sb(S, "pt2", [128, 32], F32)
            for n in range(2, 5):
                tt("dve", t1, t1[:], PWR, PWR[:, :, n - 1], are, are[:], ALU.mult)
                tt("dve", t2, t2[:], PWI, PWI[:, :, n - 1], aim, aim[:], ALU.mult)
                tt("dve", PWR, PWR[:, :, n], t1, t1[:], t2, t2[:], ALU.subtract)
                tt("dve", t1, t1[:], PWR, PWR[:, :, n - 1], aim, aim[:], ALU.mult)
                tt("dve", t2, t2[:], PWI, PWI[:, :, n - 1], are, are[:], ALU.mult)
                tt("dve", PWI, PWI[:, :, n], t1, t1[:], t2, t2[:], ALU.add)
            tsc("dve", NPWI, NPWI[:], PWI, PWI[:], -1.0, ALU.mult)
            cp("dve", PIS4, PIS4[:, :, 0], NPWI, NPWI[:, :, 4]); cp("dve", PIS4, PIS4[:, :, 1], PWI, PWI[:, :, 4])
            cp("dve", PISf, PISf[:, :, 0, :], NPWI, NPWI[:, :, 1:5]); cp("dve", PISf, PISf[:, :, 1, :], PWI, PWI[:, :, 1:5])
            for i in range(4):
                cp("dve", PWRr, PWRr[:, :, i], PWR, PWR[:, :, 4 - i])
                cp("dve", PWIr, PWIr[:, :, i], PWI, PWI[:, :, 4 - i])
                cp("dve", NPWIr, NPWIr[:, :, i], NPWI, NPWI[:, :, 4 - i])
                cp("dve", PISr, PISr[:, :, 0, i], NPWI, NPWI[:, :, 4 - i]); cp("dve", PISr, PISr[:, :, 1, i], PWI, PWI[:, :, 4 - i])
            fw.barrier(skip=("pe", "pool"))
        with ExitStack() as S:
            Bp4 = sb(S, "Bp4", [128, TL, 32, 2, 128], BF16)
            Cpd = sb(S, "Cpd", [128, 32, 2, 128], BF16)
            gt_d = dscr("gt_s", [2 * TL, 32 * 128], F32)
            with ExitStack() as S2:
                lrq = sb(S2, "lrq", [128, 128], F32); liq = sb(S2, "liq", [128, 128], F32); lsq = sb(S2, "lsq", [128, 128], F32)
                fw.dma(lrq[:], lrq_d[:, :], outs=[lrq]); fw.dma(liq[:], liq_d[:, :], outs=[liq]); fw.dma(lsq[:], lsq_d[:, :], outs=[lsq])
                are_q, aim_q, gr, gi_ = discretize(S2, 128, lrq, liq, lsq, "dq_")
                for j in range(TL):
                    for ri, T_ in ((0, gr), (1, gi_)):
                        fw.dma(gt_d[2 * j + ri:2 * j + ri + 1, :].rearrange("o (q c) -> (o q) c", c=128), T_[0:32, :], outs=[], ins=[T_])
                    if j < TL - 1:
                        gr, gi_ = cmul(S2, 128, gr, gi_, are_q, aim_q, "gq%d_" % j)
                fw.barrier(skip=("pe", "pool"))
            v3 = lambda T_: T_[:].rearrange("p (q c) -> p q c", c=128)
            for hf in range(2):
                with ExitStack() as S2:
                    c0 = hf * 2048
                    br_ = sb(S2, "brh", [128, 2048], F32); bi_ = sb(S2, "bih", [128, 2048], F32)
                    fw.dma(br_[:], brp_d[:, c0:c0 + 2048], outs=[br_]); fw.dma(bi_[:], bip_d[:, c0:c0 + 2048], outs=[bi_])
                    for j in range(TL):
                        with ExitStack() as S3:
                            gbr = sb(S3, "gbr", [128, 2048], F32); gbi = sb(S3, "gbi", [128, 2048], F32)
                            fw.dma(gbr[:], gt_d[2 * j:2 * j + 1, c0:c0 + 2048].broadcast_to([128, 2048]), outs=[gbr])
                            fw.dma(gbi[:], gt_d[2 * j + 1:2 * j + 2, c0:c0 + 2048].broadcast_to([128, 2048]), outs=[gbi])
                            wr, wi = cmul(S3, 2048, gbr, gbi, br_, bi_, "bw%d_" % j)
                            act(Bp4, Bp4[:, j, 16 * hf:16 * hf + 16, 0, :], wr, v3(wr), AF.Copy)
                            act(Bp4, Bp4[:, j, 16 * hf:16 * hf + 16, 1, :], wi, v3(wi), AF.Copy)
                            fw.barrier(skip=("pe", "pool"))
                    fw.barrier(skip=("pe", "pool"))
            with ExitStack() as S2:
                cr = sb(S2, "crt", [128, 4096], F32); ci = sb(S2, "cit", [128, 4096], F32)
                fw.dma(cr[:], crp_d[:, :], outs=[cr]); fw.dma(ci[:], cip_d[:, :], outs=[ci])
                v3 = lambda T_: T_[:].rearrange("p (q c) -> p q c", c=128)
                cp("dve", Cpd, Cpd[:, :, 0, :], cr, v3(cr))
                tsc("dve", Cpd, Cpd[:, :, 1, :], ci, v3(ci), -1.0, ALU.mult)
                fw.barrier(skip=("pe", "pool"))
            fw.dma(bp_d[:, :], Bp4[:].rearrange("p a b c d -> p (a b c d)"), outs=[], ins=[Bp4])
            fw.dma(cpd_d[:, :], Cpd[:].rearrange("p b c d -> p (b c d)"), outs=[], ins=[Cpd])
            if debug:
                for nm, T_, shp, dt_ in (("A1", A1, [128, KT, 2], F32), ("PWR", PWR, [128, 32, 5], F32), ("PWI", PWI, [128, 32, 5], F32),
                                         ("Bp4", Bp4, [128, TL, 32, 2, 128], BF16), ("Cpd", Cpd, [128, 32, 2, 128], BF16)):
                    dd = dout("dbg_" + nm, shp, dt_)
                    fw.dma(dd, T_[:], outs=[], ins=[T_])
            fw.barrier(skip=("pe", "pool"))
        S = SA
        bada = sb(S, "bada", [128, 48], F32)
        fw.dma(bada[:], bada_d[:, :], outs=[bada])
        MOD = sb(S, "MOD", [128, 48, 2], F32)
        tt("dve", MOD, MOD[:], pm, pm[:], bada, bada[:].unsqueeze(2).broadcast_to([128, 48, 2]), ALU.add)
        g1t = sb(S, "g1t", [128, KT], F32); g2t = sb(S, "g2t", [128, KT], F32)
        fw.dma(g1t[:], g1_d[:, :], outs=[g1t]); fw.dma(g2t[:], g2_d[:, :], outs=[g2t])
        for (A, Bt, G, gt, base) in ((A1, B1, G1, g1t, 0), (A2, B2, G2, g2t, 24)):
            cp("dve", Bt, Bt[:], MOD, MOD[:, base:base + 8, :])
            cp("dve", G, G[:], MOD, MOD[:, base + 16:base + 24, :])
            tsc("dve", A, A[:], MOD, MOD[:, base + 8:base + 16, :], 1.0, ALU.add)
            tt("dve", A, A[:], A, A[:], gt, gt[:].unsqueeze(2).broadcast_to([128, KT, 2]), ALU.mult)
        fw.barrier()
    if stop == "setup":
        es.close()
        return nc

    def load_w(S, name, dram_ap, kt, cols, stage, w=None):
        if w is None:
            w = sb(S, name, [128, kt, cols], BF16)
        for k in range(kt):
            fw.dma(w[:, k, :], dram_ap[:, k, 0:cols], outs=[w], e="pool")
        return w

    def norm_tmps(S, pfx, n, nbuf=2):
        return [{"sq": sb(S, pfx + "sq%d" % i, [128, KT, n], BF16), "rstd": sb(S, pfx + "rstd%d" % i, [128, n], F32),
                 "tmp": sb(S, pfx + "ntmp%d" % i, [128, KT, n], F32)} for i in range(nbuf)]

    def norm_block(T_, xb, n, A, Bt, j, hT, pss):
        sq = T_["sq"]; rstd = T_["rstd"]; tmp = T_["tmp"]
        act(sq, sq[:], xb, xb[:, :, 0:n], AF.Square)
        for kt in range(KT):
            mm(pss, pss[:, 0:n], ones, ones[:], sq, sq[:, kt, :], kt == 0, kt == KT - 1)
        act(rstd, rstd[:], pss, pss[:, 0:n], AF.Sqrt, bias=epst[:, 0:1], scale=1.0 / D, extra=[epst])
        fw.op("dve", lambda g: g.reciprocal(out=rstd[:], in_=rstd[:]), outs=[rstd], ins=[rstd])
        if hT is not None:
            for kt in range(KT):
                stt(tmp, tmp[:, kt, :], xb, xb[:, kt, 0:n], A[:, kt, j:j + 1], rstd, rstd[:], ALU.mult, ALU.mult, extra=[A])
            for kt in range(KT):
                act(hT, hT[:, kt, 0:n], tmp, tmp[:, kt, :], AF.Identity, bias=Bt[:, kt, j:j + 1], scale=1.0, extra=[Bt])
        return rstd

    def phase1_bufs(S1):
        return {"pu": [ps(S1, "pu%d" % i, [128, 512]) for i in range(4)],
                "pss": [ps(S1, "pss1_%d" % i, [128, 512]) for i in range(2)],
                "xbs": [sb(S1, "xb1_%d" % i, [128, KT, 512], F32) for i in range(2)],
                "hTs": [sb(S1, "hT1_%d" % i, [128, KT, 512], BF16) for i in range(2)],
                "nt": norm_tmps(S1, "n1_", 512), "ip": 0, "ib": 0}

    def phase1(Bf, w1, x_ap, L, j, US5p, UF, uf_tile0):
        BN = min(512, L)
        pu = Bf["pu"]; pss = Bf["pss"]; xbs = Bf["xbs"]; hTs = Bf["hTs"]; nt = Bf["nt"]
        nb_ = L // BN
        base = Bf["ib"]

        def view(T_, n):
            return {"sq": _Sub(T_["sq"], n, 3), "rstd": _Sub(T_["rstd"], n, 2), "tmp": _Sub(T_["tmp"], n, 3)}

        def front(b):
            xb = xbs[(base + b) % 2]
            fw.dma(xb[:, :, 0:BN], x_ap[:, :, b * BN:(b + 1) * BN], outs=[xb])
            norm_block(view(nt[(base + b) % 2], BN), xb, BN, A1, B1, j, hTs[(base + b) % 2], pss[(base + b) % 2])

        front(0)
        for b in range(nb_):
            t0 = b * BN
            xb = xbs[(base + b) % 2]; hT = hTs[(base + b) % 2]
            if b + 1 < nb_:
                front(b + 1)
            for m in range(4):
                p = pu[Bf["ip"] % 4]; Bf["ip"] += 1
                for kt in range(KT):
                    mm(p, p[:, 0:BN], w1, w1[:, kt, m * 128:(m + 1) * 128], hT, hT[:, kt, 0:BN], kt == 0, kt == KT - 1)
                act(US5p, US5p[:, m, 3 + t0:3 + t0 + BN], p, p[:, 0:BN], AF.Copy)
            for ts_ in range(BN // 128):
                p = pu[Bf["ip"] % 4]; Bf["ip"] += 1
                for kt in range(KT):
                    mm(p, p[:, :], hT, hT[:, kt, ts_ * 128:(ts_ + 1) * 128], w1, w1[:, kt, 512:1024], kt == 0, kt == KT - 1)
                cp("dve", UF, UF[:, uf_tile0 + t0 // 128 + ts_, :], p, p[:, :])
        Bf["ib"] = base + nb_

    def s5_run(S1, segs):
        bp_v = bp_d.rearrange("p (a b c) -> p a b c", a=TL, b=32)
        Bp4d = []
        for d in range(2):
            t_ = sb(S1, "Bp4w%d" % d, [128, TL, 16, 256], BF16)
            fw.dma(t_[:], bp_v[:, :, d * 16:(d + 1) * 16, :], outs=[t_])
            Bp4d.append(t_)
        Cpd = sb(S1, "Cpdw", [128, 32, 2, 128], BF16)
        fw.dma(Cpd[:].rearrange("p b c d -> p (b c d)"), cpd_d[:, :], outs=[Cpd])
        HB = [[sb(S1, "HB%d_%d" % (d, i), [128, 16, 2, SB], BF16) for i in range(2)] for d in range(2)]
        YS = [[sb(S1, "YS%d_%d" % (d, i), [128, 4, SB], BF16) for i in range(2)] for d in range(2)]
        py = [ps(S1, "py%d" % d, [128, 4, SB]) for d in range(2)]
        V = [[sb(S1, "V%d_%d" % (d, i), [128, 16, 2, SB], F32) for i in range(2)] for d in range(2)]
        XE = [sb(S1, "XE%d" % d, [128, 16, 2], F32) for d in range(2)]
        HX = [sb(S1, "HX%d" % d, [128, 16, 2, SB + 4], F32) for d in range(2)]
        M1 = [sb(S1, "M1_%d" % d, [128, 16, 2, 4], F32) for d in range(2)]
        M2 = [sb(S1, "M2_%d" % d, [128, 16, 2, 4], F32) for d in range(2)]
        pbu = [[ps(S1, "pbu%d_%d" % (d, i), [128, 4, 2, SB]) for i in range(2)] for d in range(2)]
        engs = ["dve", "pool"]
        blocks = []
        for sg in segs:
            for bl in range(sg[1] // SB):
                blocks.append((sg, bl))

        def cmuladd(e, d, out, oap, pr, pis, src, sap, add, aap):
            m1 = M1[d]; m2 = M2[d]
            prr = pr.unsqueeze(2).broadcast_to([128, 16, 2, 4])
            tt(e, m1, m1[:], src, sap, PWR, prr, ALU.mult)
            tt(e, m2, m2[:], src, sap[:, :, ::-1, :], PWR, pis, ALU.mult)
            tt(e, m1, m1[:], m1, m1[:], m2, m2[:], ALU.add)
            tt(e, out, oap, m1, m1[:], add, aap, ALU.add)

        def stageA(gi):
            (US5p, L, X0, off, si), bi = blocks[gi]
            nsb = L // SB
            for d in range(2):
                t0 = (bi if d == 0 else nsb - 1 - bi) * SB
                q0 = d * 16
                v = V[d][gi % 2]
                for pr_ in range(16):
                    pb = pbu[d][(pr_ // 4) % 2]
                    for ri in range(2):
                        for j in range(TL):
                            sh = -j if d == 0 else j
                            mm(pb, pb[:, pr_ % 4, ri, :], Bp4d[d], Bp4d[d][:, j, pr_, ri * 128:(ri + 1) * 128], US5p,
                               US5p[:, pr_ // 4, 3 + t0 + sh:3 + t0 + sh + SB], j == 0, j == TL - 1)
                    if pr_ % 4 == 3:
                        act(v, v[:, pr_ - 3:pr_ + 1, :, :], pb, pb[:], AF.Copy)

        def stageB(gi):
            (US5p, L, X0, off, si), bi = blocks[gi]
            nsb = L // SB
            for d in range(2):
                e = engs[d]
                q0 = d * 16
                hx = HX[d]; v = V[d][gi % 2]
                p4 = PWR[:, q0:q0 + 16, 4:5].broadcast_to([128, 16, 4])
                pis4 = PIS4[:, q0:q0 + 16, :].unsqueeze(3).broadcast_to([128, 16, 2, 4])
                ng = SB // 4
                if d == 0:
                    for m in range(ng):
                        if bi == 0 and m == 0:
                            x0b = X0[d][:].unsqueeze(3).broadcast_to([128, 16, 2, 4])
                            cmuladd(e, d, hx, hx[:, :, :, 4:8], PWR[:, q0:q0 + 16, 1:5], PISf[:, q0:q0 + 16, :, :],
                                    X0[d], x0b, v, v[:, :, :, 0:4])
                        else:
                            cmuladd(e, d, hx, hx[:, :, :, 4 + 4 * m:8 + 4 * m], p4, pis4, hx, hx[:, :, :, 4 * m:4 * m + 4], v, v[:, :, :, 4 * m:4 * m + 4])
                    xend = hx[:, :, :, SB + 3]
                    data = hx[:, :, :, 4:4 + SB]
                else:
                    for m in range(ng):
                        lo = SB - 4 - 4 * m
                        if bi == 0 and m == 0:
                            x0b = X0[d][:].unsqueeze(3).broadcast_to([128, 16, 2, 4])
                            cmuladd(e, d, hx, hx[:, :, :, lo:lo + 4], PWRr[:, q0:q0 + 16, :], PISr[:, q0:q0 + 16, :, :],
                                    X0[d], x0b, v, v[:, :, :, lo:lo + 4])
                        else:
                            cmuladd(e, d, hx, hx[:, :, :, lo:lo + 4], p4, pis4, hx, hx[:, :, :, lo + 4:lo + 8], v, v[:, :, :, lo:lo + 4])
                    xend = hx[:, :, :, 0]
                    data = hx[:, :, :, 0:SB]
                hb = HB[d][gi % 2]
                act(hb, hb[:], hx, data, AF.Copy)
                if si is not None and bi == nsb - 1:
                    cp(e, XE[d], XE[d][:], hx, xend)
                    fw.dma(st_d[si, :, d * 32:(d + 1) * 32].rearrange("p (a b) -> p a b", b=2), XE[d][:], outs=[], ins=[XE[d]])
                if bi < nsb - 1:
                    if d == 0:
                        cp(e, hx, hx[:, :, :, 0:4], hx, hx[:, :, :, SB:SB + 4])
                    else:
                        cp(e, hx, hx[:, :, :, SB:SB + 4], hx, hx[:, :, :, 0:4])

        def stageC(gi):
            (US5p, L, X0, off, si), bi = blocks[gi]
            nsb = L // SB
            for d in range(2):
                t0 = (bi if d == 0 else nsb - 1 - bi) * SB
                q0 = d * 16
                hb = HB[d][gi % 2]; ys = YS[d][gi % 2]
                pyd = py[d]
                for c in range(4):
                    i = 0
                    for pr_ in range(4 * c, 4 * c + 4):
                        for ri in range(2):
                            mm(pyd, pyd[:, c, :], Cpd, Cpd[:, q0 + pr_, ri, :], hb, hb[:, pr_, ri, :], i == 0, i == 7)
                            i += 1
                if d == 0:
                    for c in range(4):
                        stt(ys, ys[:, c, :], US5p, US5p[:, c, 3 + t0:3 + t0 + SB], s5d[:, c:c + 1], pyd, pyd[:, c, :], ALU.mult, ALU.add, extra=[s5d])
                    fw.dma(ya_d[:, :, off + t0:off + t0 + SB], ys[:], outs=[], ins=[ys])
                else:
                    act(ys, ys[:], pyd, pyd[:], AF.Copy)
                    fw.dma(yb_d[:, :, off + t0:off + t0 + SB], ys[:], outs=[], ins=[ys])

        n = len(blocks)
        stageA(0); stageB(0)
        for gi in range(1, n):
            stageA(gi); stageB(gi); stageC(gi - 1)
        stageC(n - 1)
        fw.barrier()

    def s5_far(S1, US5p):
        Bp4 = sb(S1, "Bp4f", [128, TL, 32, 2, 128], BF16)
        fw.dma(Bp4[:].rearrange("p a b c d -> p (a b c d)"), bp_d[:, :], outs=[Bp4])
        PTbR = sb(S1, "PTbR", [128, 32, NK + 1], F32); PTbI = sb(S1, "PTbI", [128, 32, NK + 1], F32)
        PTfR = sb(S1, "PTfR", [128, 32, NK], F32); PTfI = sb(S1, "PTfI", [128, 32, NK], F32)
        with ExitStack() as S2:
            lrs = sb(S2, "flrs", [128, 32], F32); lis = sb(S2, "flis", [128, 32], F32); lss = sb(S2, "flss", [128, 32], F32)
            fw.dma(lrs[:], lrs_d[:, :], outs=[lrs]); fw.dma(lis[:], lis_d[:, :], outs=[lis]); fw.dma(lss[:], lss_d[:, :], outs=[lss])
            rb = sb(S2, "rampb", [128, NK + 1], F32); rf = sb(S2, "rampf", [128, NK], F32)
            fw.dma(rb[:], rampb_d[:, :], outs=[rb]); fw.dma(rf[:], rampf_d[:, :], outs=[rf])
            act(lss, lss[:], lss, lss[:], AF.Exp)
            tt("dve", lrs, lrs[:], lrs, lrs[:], lss, lss[:], ALU.mult)
            tt("dve", lis, lis[:], lis, lis[:], lss, lss[:], ALU.mult)
            for (ramp, K, PR, PI_) in ((rb, NK + 1, PTbR, PTbI), (rf, NK, PTfR, PTfI)):
                with ExitStack() as S3:
                    ar_ = sb(S3, "argr", [128, 32, K], F32); ai_ = sb(S3, "argi", [128, 32, K], F32)
                    rbc = ramp[:].unsqueeze(1).broadcast_to([128, 32, K])
                    tt("dve", ar_, ar_[:], lrs, lrs[:].unsqueeze(2).broadcast_to([128, 32, K]), ramp, rbc, ALU.mult)
                    tt("dve", ai_, ai_[:], lis, lis[:].unsqueeze(2).broadcast_to([128, 32, K]), ramp, rbc, ALU.mult)
                    re_, im_ = expi(S3, [128, 32, K], ar_, ar_[:], ai_, ai_[:], "pt_")
                    cp("dve", PR, PR[:], re_, re_[:]); cp("dve", PI_, PI_[:], im_, im_[:])
                    fw.barrier()
            fw.barrier()
        nfb = LF // FB
        V = [sb(S1, "Vf%d" % d, [128, 16, 2, NK], F32) for d in range(2)]
        M1s = [sb(S1, "M1f%d" % d, [128, 16, 2, NK], F32) for d in range(2)]; M2s = [sb(S1, "M2f%d" % d, [128, 16, 2, NK], F32) for d in range(2)]
        Ws = [sb(S1, "Wf%d" % d, [128, 16, 2], F32) for d in range(2)]
        X = [sb(S1, "Xf%d" % d, [128, 16, 2], F32) for d in range(2)]
        T1 = sb(S1, "T1f", [128, 16, 2], F32); T2 = sb(S1, "T2f", [128, 16, 2], F32)
        pbu = [[ps(S1, "pbf%d_%d" % (d, i), [128, 4, 2, NK]) for i in range(2)] for d in range(2)]
        for d, cap in ((0, capF), (1, capB)):
            cp("dve", X[d], X[d][:], H0L[d], H0L[d][:])
            fw.op("dve", lambda g: g.tensor_scalar(out=ACC[d][:], in0=H0L[d][:], scalar1=cap[:, 0:1], scalar2=None, op0=ALU.mult),
                  outs=[ACC[d]], ins=[H0L[d], cap])
        for bi in range(nfb):
            for d in range(2):
                blk = bi if d == 0 else nfb - 1 - bi
                t0 = blk * FB
                q0 = d * 16
                v = V[d]
                for pr_ in range(16):
                    pb = pbu[d][(pr_ // 4) % 2]
                    for ri in range(2):
                        for j in range(TL):
                            st_ = 3 + t0 + (TL - 1 - j if d == 0 else j)
                            mm(pb, pb[:, pr_ % 4, ri, :], Bp4, Bp4[:, j, q0 + pr_, ri, :], US5p,
                               US5p[:, pr_ // 4, st_:st_ + FB:TL], j == 0, j == TL - 1)
                    if pr_ % 4 == 3:
                        act(v, v[:, pr_ - 3:pr_ + 1, :, :], pb, pb[:], AF.Copy)
                if d == 0:
                    ptr = PTfR[:, q0:q0 + 16, :]; pti = PTfI[:, q0:q0 + 16, :]
                else:
                    ptr = PTbR[:, q0:q0 + 16, 0:NK]; pti = PTbI[:, q0:q0 + 16, 0:NK]
                e = "dve" if d == 0 else "pool"
                M1 = M1s[d]; M2 = M2s[d]; W = Ws[d]
                tt(e, M1, M1[:, :, 0, :], v, v[:, :, 0, :], PTbR, ptr, ALU.mult)
                tt(e, M2, M2[:, :, 0, :], v, v[:, :, 1, :], PTbR, pti, ALU.mult)
                tt(e, M1, M1[:, :, 0, :], M1, M1[:, :, 0, :], M2, M2[:, :, 0, :], ALU.subtract)
                tt(e, M1, M1[:, :, 1, :], v, v[:, :, 1, :], PTbR, ptr, ALU.mult)
                tt(e, M2, M2[:, :, 1, :], v, v[:, :, 0, :], PTbR, pti, ALU.mult)
                tt(e, M1, M1[:, :, 1, :], M1, M1[:, :, 1, :], M2, M2[:, :, 1, :], ALU.add)
                fw.op("dve", lambda g: g.tensor_reduce(out=W[:], in_=M1[:], axis=mybir.AxisListType.X, op=ALU.add), outs=[W], ins=[M1])
                x = X[d]
                pr128 = PTbR[:, q0:q0 + 16, NK:NK + 1].broadcast_to([128, 16, 2])
                pi128 = PTbI[:, q0:q0 + 16, NK]
                tt("dve", T1, T1[:], x, x[:], PTbR, pr128, ALU.mult)
                tt("dve", T2, T2[:, :, 0], x, x[:, :, 1], PTbR, pi128, ALU.mult)
                tt("dve", T1, T1[:, :, 0], T1, T1[:, :, 0], T2, T2[:, :, 0], ALU.subtract)
                tt("dve", T2, T2[:, :, 1], x, x[:, :, 0], PTbR, pi128, ALU.mult)
                tt("dve", T1, T1[:, :, 1], T1, T1[:, :, 1], T2, T2[:, :, 1], ALU.add)
                tt("dve", x, x[:], T1, T1[:], W, W[:], ALU.add)
                cap = capF if d == 0 else capB
                fw.op("dve", lambda g: g.scalar_tensor_tensor(out=ACC[d][:], in0=x[:], scalar=cap[:, 1 + blk:2 + blk], in1=ACC[d][:],
                                                              op0=ALU.mult, op1=ALU.add), outs=[ACC[d]], ins=[x, cap])
        fw.barrier()

    def dft_bufs(S1):
        nl = LL // 128
        return {"CMs": [sb(S1, "CM%d" % i, [128, nl, NB], BF16) for i in range(2)],
                "SMs": [sb(S1, "SM%d" % i, [128, nl, NB], BF16) for i in range(2)],
                "CC": [sb(S1, "CC%d" % i, [128, 128], BF16) for i in range(2)], "NSC": [sb(S1, "NSC%d" % i, [128, 128], BF16) for i in range(2)],
                "AC": sb(S1, "AC", [128, 4, NB], BF16), "AS": sb(S1, "AS", [128, 4, NB], BF16),
                "YFt": [sb(S1, "YFt%d" % i, [128, 4, NB], BF16) for i in range(2)],
                "pcs": [ps(S1, "pcs%d" % m, [128, 2, NB]) for m in range(4)],
                "pyfs": [ps(S1, "pyf%d" % i, [128, 2, NB]) for i in range(2)], "ik": 0}

    def dft_run(Bd, UF, nlt, nkc, cm_ap, sm_ap, cci, off):
        AC = Bd["AC"]; AS = Bd["AS"]; pcs = Bd["pcs"]; pyfs = Bd["pyfs"]
        CC = Bd["CC"][cci]; NSC = Bd["NSC"][cci]
        for kc in range(nkc):
            ik = Bd["ik"]; Bd["ik"] += 1
            CM = Bd["CMs"][ik % 2]; SM = Bd["SMs"][ik % 2]; YFt = Bd["YFt"][ik % 2]
            fw.dma(CM[:, 0:nlt, :], cm_ap[kc, :, :, :], outs=[CM])
            fw.dma(SM[:, 0:nlt, :], sm_ap[kc, :, :, :], outs=[SM])
            for m in range(4):
                for lt in range(nlt):
                    mm(pcs[m], pcs[m][:, 0, :], UF, UF[:, lt, m * 128:(m + 1) * 128], CM, CM[:, lt, :], lt == 0, lt == nlt - 1)
                for lt in range(nlt):
                    mm(pcs[m], pcs[m][:, 1, :], UF, UF[:, lt, m * 128:(m + 1) * 128], SM, SM[:, lt, :], lt == 0, lt == nlt - 1)
                act(AC, AC[:, m, :], pcs[m], pcs[m][:, 0, :], AF.Copy)
                cp("dve", AS, AS[:, m, :], pcs[m], pcs[m][:, 1, :])
            for m in range(4):
                pyf = pyfs[m // 2]
                mm(pyf, pyf[:, m % 2, :], CC, CC[:], AC, AC[:, m, :], True, False)
                mm(pyf, pyf[:, m % 2, :], NSC, NSC[:], AS, AS[:, m, :], False, True)
            for i in range(2):
                act(YFt, YFt[:, 2 * i:2 * i + 2, :], pyfs[i], pyfs[i][:], AF.Copy)
            fw.dma(yf_d[:, :, off + kc * NB:off + (kc + 1) * NB], YFt[:], outs=[], ins=[YFt])

    def zero_pads(US5p, L):
        fw.op("pool", lambda g: g.memset(US5p[:, :, 0:3], 0.0), outs=[US5p])
        fw.op("pool", lambda g: g.memset(US5p[:, :, 3 + L:6 + L], 0.0), outs=[US5p])

    def jof(t0):
        return 1 if t0 < LO else 0

    SUO = ExitStack()
    UF = sb(SUO, "UFall", [128, LL // 128, 512], BF16)
    UFs = [sb(SUO, "UFs%d" % i, [128, LS // 128, 512], BF16) for i in range(2)]
    with ExitStack() as SU:
        US5o = sb(SU, "US5o", [128, 4, LO + 6], BF16)
        US5s = [sb(SU, "US5s%d" % i, [128, 4, LS + 6], BF16) for i in range(2)]
        zero_pads(US5o, LO)
        for i in range(2):
            zero_pads(US5s[i], LS)
        with ExitStack() as SF:
            US5f = sb(SF, "US5f", [128, 4, LF + 6], BF16)
            zero_pads(US5f, LF)
            with ExitStack() as SW:
                w1 = sb(SW, "w_in1", [128, KT, 1024], BF16)
                load_w(SW, "w_in1", win_d, KT, 1024, None, w1)
                with ExitStack() as S1:
                    Bf = phase1_bufs(S1)
                    phase1(Bf, w1, xf_d, LF, 1, US5f, UF, 0)
                    phase1(Bf, w1, xo_d[:, :, 0:LO], LO, 1, US5o, UF, LF // 128)
                    for i in range(2):
                        off = LO + i * LS
                        phase1(Bf, w1, xo_d[:, :, off:off + LS], LS, 0, US5s[i], UFs[i], 0)
                    fw.barrier()
            with ExitStack() as S1:
                s5_far(S1, US5f)
            fw.barrier()
        with ExitStack() as S1:
            s5_run(S1, [(US5o, LO, ACC, 0, None)] + [(US5s[i], LS, ZST, LO + i * LS, i) for i in range(2)])
        fw.barrier()
    S3W = ExitStack()
    wgt = sb(S3W, "w_in2", [128, KT, 2048], BF16)
    wglu = sb(S3W, "wglu", [128, 4, 512], BF16); wp5 = sb(S3W, "wp5", [128, 4, D], BF16)
    wpf = sb(S3W, "wpf", [128, 4, D], BF16); wout = sb(S3W, "wout", [128, KT, D], BF16)
    for k in range(KT):
        fw.dma(wgt[:, k, :], win_d[:, k, 1024:3072], outs=[wgt], e="pool")
    load_w(S3W, "wglu", wglu_d, 4, 512, None, wglu)
    load_w(S3W, "wp5", wp5_d, 4, D, None, wp5)
    load_w(S3W, "wpf", wpf_d, 4, D, None, wpf)
    load_w(S3W, "wout", wout_d, KT, D, None, wout)
    with ExitStack() as S1:
        Bd = dft_bufs(S1)
        fw.dma(Bd["CC"][0][:], ccl_d[:, :], outs=[Bd["CC"][0]]); fw.dma(Bd["NSC"][0][:], nscl_d[:, :], outs=[Bd["NSC"][0]])
        fw.dma(Bd["CC"][1][:], ccs_d[:, :], outs=[Bd["CC"][1]]); fw.dma(Bd["NSC"][1][:], nscs_d[:, :], outs=[Bd["NSC"][1]])
        dft_run(Bd, UF, LL // 128, LO // NB, cml_d, sml_d, 0, 0)
        for i in range(2):
            dft_run(Bd, UFs[i], LS // 128, 1, cms_d, sms_d, 1, LO + i * LS)
        fw.barrier()

    with ExitStack() as S:
        pss = [ps(S, "pssA%d" % i, [128, NB]) for i in range(2)]
        pg = [ps(S, "pgA%d" % i, [128, NB]) for i in range(3)]
        pq = [ps(S, "pqA%d" % i, [128, NB]) for i in range(3)]
        xbs = [sb(S, "xbA%d" % i, [128, KT, NB], F32) for i in range(2)]
        hTs = [sb(S, "hTA%d" % i, [128, KT, NB], BF16) for i in range(2)]
        nt = norm_tmps(S, "nA_", NB)
        SG = sb(S, "SG", [128, 16, NB], BF16)
        yas = [sb(S, "yaA%d" % i, [128, 4, NB], BF16) for i in range(2)]
        ybs = [sb(S, "ybA%d" % i, [128, 4, NB], BF16) for i in range(2)]
        yfs = [sb(S, "yfA%d" % i, [128, 4, NB], BF16) for i in range(2)]
        zts = [sb(S, "ztA%d" % i, [128, NB], F32) for i in range(2)]
        ZB = sb(S, "ZB", [128, 4, NB], BF16)
        gus = [sb(S, "guA%d" % i, [128, NB], F32) for i in range(2)]
        sgls = [sb(S, "sgl%d" % i, [128, NB], F32) for i in range(2)]
        Y5 = sb(S, "Y5", [128, 4, NB], BF16)
        tqs = [sb(S, "tqA%d" % i, [128, NB], F32) for i in range(2)]
        MB = sb(S, "MB", [128, KT, NB], BF16)
        x1s = [sb(S, "x1A0", [128, KT, NB], F32)] * 2
        ig = 0; iq = 0

        def frontA(b):
            t0_ = b * NB
            fw.dma(xbs[b % 2][:], xo_d[:, :, t0_:t0_ + NB], outs=[xbs[b % 2]])
            fw.dma(yas[b % 2][:], ya_d[:, :, t0_:t0_ + NB], outs=[yas[b % 2]])
            fw.dma(ybs[b % 2][:], yb_d[:, :, t0_:t0_ + NB], outs=[ybs[b % 2]])
            fw.dma(yfs[b % 2][:], yf_d[:, :, t0_:t0_ + NB], outs=[yfs[b % 2]])
            norm_block(nt[b % 2], xbs[b % 2], NB, A1, B1, jof(t0_), hTs[b % 2], pss[b % 2])

        for b in range(NT // NB):
            t0 = b * NB
            j = jof(t0)
            xb = xbs[b % 2]; hT = hTs[b % 2]; ya = yas[b % 2]; yb = ybs[b % 2]; yf = yfs[b % 2]; x1 = x1s[b % 2]
            if b == 0:
                frontA(0)
            if b + 1 < NT // NB:
                frontA(b + 1)
            for m in range(16):
                p = pg[ig % 3]; ig += 1
                for kt in range(KT):
                    mm(p, p[:], wgt, wgt[:, kt, m * 128:(m + 1) * 128], hT, hT[:, kt, :], kt == 0, kt == KT - 1)
                act(SG, SG[:, m, :], p, p[:], AF.Sigmoid)
            for c in range(4):
                zt = zts[c % 2]; gu_ = gus[c % 2]
                tt("dve", zt, zt[:], ya, ya[:, c, :], yb, yb[:, c, :], ALU.add)
                act(gu_, gu_[:], zt, zt[:], AF.Square)
                tsc("dve", gu_, gu_[:], gu_, gu_[:], 0.044715, ALU.mult, 1.0, ALU.add)
                tt("dve", gu_, gu_[:], gu_, gu_[:], zt, zt[:], ALU.mult)
                act(gu_, gu_[:], gu_, gu_[:], AF.Sigmoid, scale=2.0 * math.sqrt(2.0 / PI))
                tt("dve", ZB, ZB[:, c, :], zt, zt[:], gu_, gu_[:], ALU.mult)
            for m in range(4):
                p = pq[iq % 3]; iq += 1
                sgl = sgls[m % 2]
                for k in range(4):
                    mm(p, p[:], wglu, wglu[:, k, m * 128:(m + 1) * 128], ZB, ZB[:, k, :], k == 0, k == 3)
                act(sgl, sgl[:], p, p[:], AF.Sigmoid, bias=bglu[:, m:m + 1], scale=1.0, extra=[bglu])
                tt("dve", Y5, Y5[:, m, :], ZB, ZB[:, m, :], sgl, sgl[:], ALU.mult)
            for m in range(KT):
                p1 = pg[ig % 3]; ig += 1
                p2 = pq[iq % 3]; iq += 1
                tq = tqs[m % 2]; sgl = sgls[m % 2]
                for k in range(4):
                    mm(p1, p1[:], wp5, wp5[:, k, m * 128:(m + 1) * 128], Y5, Y5[:, k, :], k == 0, k == 3)
                for k in range(4):
                    mm(p2, p2[:], wpf, wpf[:, k, m * 128:(m + 1) * 128], yf, yf[:, k, :], k == 0, k == 3)
                tt("dve", tq, tq[:], p1, p1[:], SG, SG[:, m, :], ALU.mult)
                tt("dve", sgl, sgl[:], p2, p2[:], SG, SG[:, 8 + m, :], ALU.mult)
                tt("pool", MB, MB[:, m, :], tq, tq[:], sgl, sgl[:], ALU.add)
            for m in range(KT):
                p = pg[ig % 3]; ig += 1
                for k in range(KT):
                    mm(p, p[:], wout, wout[:, k, m * 128:(m + 1) * 128], MB, MB[:, k, :], k == 0, k == KT - 1)
                stt(x1, x1[:, m, :], p, p[:], G1[:, m, j:j + 1], xb, xb[:, m, :], ALU.mult, ALU.add, extra=[G1])
            fw.dma(x1_d[:, :, t0:t0 + NB], x1[:], outs=[], ins=[x1])
        fw.barrier()
    S3W.close()
    SUO.close()
    if debug:
        dd = dout("dbg_x1", [128, KT, NT], F32)
        fw.dma(dd, x1_d[:, :, :], outs=[], ins=[])
        fw.barrier()
    if stop == "3a":
        es.close()
        return nc
    BN = 512
    gu_d = dscr("gu_s", [128, FT, NT], BF16)
    with ExitStack() as S:
        GRP = [(0, 6), (6, 12), (12, 17), (17, 22)]
        wgs = [sb(S, "wg_%d" % i, [128, KT, (b_ - a_) * 128], BF16) for i, (a_, b_) in enumerate(GRP)]
        wus = [sb(S, "wu_%d" % i, [128, KT, (b_ - a_) * 128], BF16) for i, (a_, b_) in enumerate(GRP)]
        for i, (a_, b_) in enumerate(GRP):
            fw.dma(wgs[i][:], wg_d[:, :, a_ * 128:b_ * 128], outs=[wgs[i]], e="pool")
            fw.dma(wus[i][:], wu_d[:, :, a_ * 128:b_ * 128], outs=[wus[i]], e="pool")
        gof = {}
        for i, (a_, b_) in enumerate(GRP):
            for m_ in range(a_, b_):
                gof[m_] = (i, m_ - a_)
        pss = [ps(S, "pssB%d" % i, [128, BN]) for i in range(2)]
        pg = [ps(S, "pgB%d" % i, [128, BN]) for i in range(3)]
        pq = [ps(S, "pqB%d" % i, [128, BN]) for i in range(2)]
        xbs = [sb(S, "xbB%d" % i, [128, KT, BN], F32) for i in range(2)]
        hTs = [sb(S, "hTB%d" % i, [128, KT, BN], BF16) for i in range(2)]
        nt = norm_tmps(S, "nB_", BN, nbuf=1) * 2
        sgts = [sb(S, "sgtB%d" % i, [128, BN], BF16) for i in range(2)]
        GU = sb(S, "GU", [128, FT, BN], BF16)
        ig = 0; iq = 0
        nbb = NT // BN

        def frontB(b):
            t0_ = b * BN
            fw.dma(xbs[b % 2][:], x1_d[:, :, t0_:t0_ + BN], outs=[xbs[b % 2]])
            norm_block(nt[b % 2], xbs[b % 2], BN, A2, B2, jof(t0_), hTs[b % 2], pss[b % 2])

        frontB(0)
        for b in range(nbb):
            t0 = b * BN
            hT = hTs[b % 2]
            if b + 1 < nbb:
                frontB(b + 1)
            for m in range(FT):
                p1 = pg[ig % 3]; ig += 1
                p2 = pq[iq % 2]; iq += 1
                sgt = sgts[m % 2]
                wg_, wu_ = wgs[gof[m][0]], wus[gof[m][0]]
                c0 = gof[m][1] * 128
                for k in range(KT):
                    mm(p1, p1[:], wg_, wg_[:, k, c0:c0 + 128], hT, hT[:, k, :], k == 0, k == KT - 1)
                for k in range(KT):
                    mm(p2, p2[:], wu_, wu_[:, k, c0:c0 + 128], hT, hT[:, k, :], k == 0, k == KT - 1)
                act(sgt, sgt[:], p1, p1[:], AF.Silu)
                tt("dve", GU, GU[:, m, :], p2, p2[:], sgt, sgt[:], ALU.mult)
            fw.dma(gu_d[:, :, t0:t0 + BN], GU[:], outs=[], ins=[GU])
        fw.barrier()
    with ExitStack() as S:
        wds = [sb(S, "wd_%d" % m_, [128, FT, 256], BF16) for m_ in range(4)]
        for m_ in range(4):
            fw.dma(wds[m_][:], wd_d[:, :, m_ * 256:(m_ + 1) * 256], outs=[wds[m_]], e="pool")
        pssf = [ps(S, "pssF%d" % i, [128, BN]) for i in range(2)]
        pg = [ps(S, "pgD%d" % i, [128, BN]) for i in range(4)]
        xbs = [sb(S, "xbD%d" % i, [128, KT, BN], F32) for i in range(2)]
        GUs = [sb(S, "GUD%d" % i, [128, FT, BN], BF16) for i in range(2)]
        ntf = [{"sq": sb(S, "nF_sq%d" % i, [128, KT, BN], BF16), "rstd": sb(S, "nF_rstd%d" % i, [128, BN], F32), "tmp": None} for i in range(2)]
        ig = 0
        nbb = NT // BN

        def frontD(b):
            t0_ = b * BN
            fw.dma(xbs[b % 2][:], x1_d[:, :, t0_:t0_ + BN], outs=[xbs[b % 2]])
            fw.dma(GUs[b % 2][:], gu_d[:, :, t0_:t0_ + BN], outs=[GUs[b % 2]])

        frontD(0)
        for b in range(nbb):
            t0 = b * BN
            j = jof(t0)
            xb = xbs[b % 2]; GU = GUs[b % 2]
            if b + 1 < nbb:
                frontD(b + 1)
            for m in range(KT):
                p = pg[ig % 4]; ig += 1
                for k in range(FT):
                    mm(p, p[:], wds[m // 2], wds[m // 2][:, k, (m % 2) * 128:(m % 2 + 1) * 128], GU, GU[:, k, :], k == 0, k == FT - 1)
                stt(xb, xb[:, m, :], p, p[:], G2[:, m, j:j + 1], xb, xb[:, m, :], ALU.mult, ALU.add, extra=[G2])
            rstd = norm_block(ntf[b % 2], xb, BN, None, None, j, None, pssf[b % 2])
            for m in range(KT):
                stt(xb, xb[:, m, :], xb, xb[:, m, :], gft[:, m:m + 1], rstd, rstd[:], ALU.mult, ALU.mult, extra=[gft])
            fw.dma(yo_d[:, :, t0:t0 + BN], xb[:], outs=[], ins=[xb])
        fw.barrier()
    fw.barrier()
    es.close()
    return nc


def _fm(x2d):
    T = x2d.shape[0]
    return np.ascontiguousarray(x2d.T.reshape(-1, 128, T).transpose(1, 0, 2))


def _unfm(a):
    return np.ascontiguousarray(a.transpose(1, 0, 2).reshape(-1, a.shape[2]).T)


def _wl(w):
    K, N = w.shape
    return np.ascontiguousarray(w.reshape(K // 128, 128, N).transpose(1, 0, 2))


def _vl(v):
    return np.ascontiguousarray(v.reshape(-1, 128).T)


_NC_CACHE = {}


def make_in_maps(x_prompt, x_sample, state_s5, c, c_ctx, norm1_g, norm2_g, w_ada, b_ada, w_in,
                 s5_lambda_re, s5_lambda_im, s5_log_step, s5_b_re, s5_b_im, s5_c_re, s5_c_im,
                 s5_d, w_glu, b_glu, w_proj_s5, w_proj_fft, w_out, w_ffn_gate, w_ffn_up,
                 w_ffn_down, final_norm_g):
    f32 = np.float32
    bf = ml_dtypes.bfloat16
    A = lambda a: np.asarray(a, dtype=f32)
    x_prompt, x_sample, state_s5, c, c_ctx = A(x_prompt), A(x_sample), A(state_s5), A(c), A(c_ctx)
    lam_re, lam_im, lstep = A(s5_lambda_re)[0], A(s5_lambda_im)[0], A(s5_log_step)[0]
    b_re, b_im, c_re, c_im = A(s5_b_re)[0], A(s5_b_im)[0], A(s5_c_re)[0], A(s5_c_im)[0]
    common = {
        "w_ada": _wl(A(w_ada)[0]), "b_ada": _vl(A(b_ada)[0]),
        "g1": _vl(A(norm1_g)[0]), "g2": _vl(A(norm2_g)[0]), "gf": _vl(A(final_norm_g)),
        "w_in": _wl(A(w_in)[0]), "w_glu": _wl(A(w_glu)[0]), "b_glu": _vl(A(b_glu)[0]),
        "s5d": _vl(A(s5_d)[0]), "w_p5": _wl(A(w_proj_s5)[0]), "w_pf": _wl(A(w_proj_fft)[0]),
        "w_out": _wl(A(w_out)[0]), "w_g": _wl(A(w_ffn_gate)[0]), "w_u": _wl(A(w_ffn_up)[0]),
        "w_d": _wl(A(w_ffn_down)[0]),
    }
    lrs = np.zeros((128, 32), f32); lis = np.zeros((128, 32), f32); lss = np.zeros((128, 32), f32)
    lrp = np.zeros((128, 2, 16, 2, 64), f32); lip = np.zeros_like(lrp); lsp = np.zeros_like(lrp)
    brp = np.zeros_like(lrp); bip = np.zeros_like(lrp)
    crp = np.zeros((128, 2, 16, 128), f32); cip = np.zeros_like(crp)
    for d in range(2):
        for pr in range(16):
            for g2 in range(2):
                g = 2 * pr + g2
                lrs[g2 * 64:(g2 + 1) * 64, d * 16 + pr] = lam_re[d, g]
                lis[g2 * 64:(g2 + 1) * 64, d * 16 + pr] = lam_im[d, g]
                lss[g2 * 64:(g2 + 1) * 64, d * 16 + pr] = lstep[d, g]
                lrp[:, d, pr, g2, :] = lam_re[d, g][None, :]
                lip[:, d, pr, g2, :] = lam_im[d, g][None, :]
                lsp[:, d, pr, g2, :] = lstep[d, g]
                gi = g % 8
                brp[gi * 16:(gi + 1) * 16, d, pr, g2, :] = b_re[d, g].T
                bip[gi * 16:(gi + 1) * 16, d, pr, g2, :] = b_im[d, g].T
                crp[g2 * 64:(g2 + 1) * 64, d, pr, gi * 16:(gi + 1) * 16] = c_re[d, g].T
                cip[g2 * 64:(g2 + 1) * 64, d, pr, gi * 16:(gi + 1) * 16] = c_im[d, g].T
    lrq = np.zeros((128, 128), f32); liq = np.zeros((128, 128), f32); lsq = np.zeros((128, 128), f32)
    for d in range(2):
        for pr in range(16):
            q = d * 16 + pr
            for g2 in range(2):
                g = 2 * pr + g2
                lrq[q, g2 * 64:(g2 + 1) * 64] = lam_re[d, g]
                liq[q, g2 * 64:(g2 + 1) * 64] = lam_im[d, g]
                lsq[q, g2 * 64:(g2 + 1) * 64] = lstep[d, g]
    lrq[32:] = lrq[0]; liq[32:] = liq[0]; lsq[32:] = lsq[0]
    common.update({"lrs": lrs, "lis": lis, "lss": lss, "lrq": lrq, "liq": liq, "lsq": lsq,
                   "brp": brp.reshape(128, 4096), "bip": bip.reshape(128, 4096),
                   "crp": crp.reshape(128, 4096), "cip": cip.reshape(128, 4096),
                   "h0s": np.zeros((128, 64), f32)})
    ch = np.arange(128, dtype=np.int64)
    angc = (2 * np.pi / 128) * ((ch[:, None] * ch[None, :]) % 128).astype(np.float64)
    for nm, L in (("l", LL), ("s", LS)):
        sc = 1.0 / math.sqrt(L * 128.0)
        common["cc" + nm] = (np.cos(angc) * sc).astype(f32).astype(bf)
        common["nsc" + nm] = (-np.sin(angc) * sc).astype(f32).astype(bf)

    def lay(m, L_l, L_k):
        return np.ascontiguousarray(m.reshape(L_l // 128, 128, L_k // NB, NB).transpose(2, 1, 0, 3)).astype(bf)

    l = np.arange(LS, dtype=np.int64)
    ang = (2 * np.pi / LS) * ((l[:, None] * l[None, :]) % LS).astype(np.float64)
    common["cms"] = lay(np.cos(ang).astype(f32), LS, LS); common["sms"] = lay(np.sin(ang).astype(f32), LS, LS)
    common["ones"] = np.ones((128, 128), f32).astype(bf)
    dft_own = {}
    for j in range(4):
        s = j * LO
        lord = np.concatenate([np.arange(0, s), np.arange(s + LO, LL), np.arange(s, s + LO)]).astype(np.int64)
        k = np.arange(s, s + LO, dtype=np.int64)
        ang = (2 * np.pi / LL) * ((lord[:, None] * k[None, :]) % LL).astype(np.float64)
        dft_own[j] = (lay(np.cos(ang).astype(f32), LL, LO), lay(np.sin(ang).astype(f32), LL, LO))
    nfb = NFB
    common["rampb"] = np.tile((TL * np.arange(NK + 1, dtype=np.float32))[None, :], (128, 1))
    common["rampf"] = np.tile((TL * (NK - 1 - np.arange(NK, dtype=np.float32)))[None, :], (128, 1))
    in_maps = []
    for i in range(8):
        b = i // 4; j = i % 4; s = j * LO
        m = dict(common)
        xs_ = x_sample[b]
        m["xf"] = _fm(np.concatenate([xs_[:s], xs_[s + LO:]], axis=0))
        m["xo"] = _fm(np.concatenate([xs_[s:s + LO], x_prompt[2 * i], x_prompt[2 * i + 1]], axis=0))
        cT = np.stack([c_ctx, c[b]], axis=1)
        m["cT"] = np.ascontiguousarray(cT.reshape(KT, 128, 2).transpose(1, 0, 2))
        h0 = np.zeros((128, 2, 16, 2), f32)
        for d in range(2):
            for ri in range(2):
                for pr in range(16):
                    for g2 in range(2):
                        h0[g2 * 64:(g2 + 1) * 64, d, pr, ri] = state_s5[b, 0, d, ri, 2 * pr + g2]
        m["h0l"] = h0.reshape(128, 64)
        capf = np.zeros((128, nfb + 1), f32); capb = np.zeros((128, nfb + 1), f32)
        if j == 0:
            capf[:, 0] = 1.0
        else:
            capf[:, 1 + (s // FB - 1)] = 1.0
        if j == 3:
            capb[:, 0] = 1.0
        else:
            capb[:, 1 + s // FB] = 1.0
        m["capf"] = capf; m["capb"] = capb
        m["cml"], m["sml"] = dft_own[j]
        in_maps.append(m)
    return in_maps


def kernel(**inputs):
    f32 = np.float32
    in_maps = make_in_maps(**inputs)
    if "nc" not in _NC_CACHE:
        _NC_CACHE["nc"] = build()
    nc = _NC_CACHE["nc"]
    res = run_bass_kernel_spmd(nc, in_maps, core_ids=list(range(8)))
    R = res.results
    y_sample = np.zeros((2, LL, D), f32)
    y_prompt = np.zeros((16, LS, D), f32)
    new_state = np.zeros((16, 1, 2, 2, 32, 64), f32)
    for i in range(8):
        b = i // 4; j = i % 4; s = j * LO
        yo = _unfm(np.asarray(R[i]["yo"], dtype=f32))
        y_sample[b, s:s + LO] = yo[:LO]
        for k in range(2):
            n = 2 * i + k
            y_prompt[n] = yo[LO + k * LS:LO + (k + 1) * LS]
            st = np.asarray(R[i]["st"][k], dtype=f32).reshape(128, 2, 16, 2)
            for d in range(2):
                for ri in range(2):
                    for pr in range(16):
                        for g2 in range(2):
                            new_state[n, 0, d, ri, 2 * pr + g2] = st[g2 * 64:(g2 + 1) * 64, d, pr, ri]
    return (y_prompt, y_sample, new_state)
```

```python
import math
from contextlib import ExitStack
import numpy as np
import ml_dtypes
import concourse.bass as bass
import concourse.mybir as mybir
from concourse.bass_utils import run_bass_kernel_spmd

F32 = mybir.dt.float32
BF16 = mybir.dt.bfloat16
I32 = mybir.dt.int32
AF = mybir.ActivationFunctionType
ALU = mybir.AluOpType

D = 1024
KT = 8
LL = 4096
LS = 256
DFF = 2816
FT = DFF // 128
EPS = 1e-6
PI = math.pi
NB = 256
SB = 64
LF = 3072
LO = 1024
NT = LO + 2 * LS
TL = 4
FB = 128
NFB = LF // FB
NK = FB // TL


class Tl:
    def __init__(self, t, name="", psum=False):
        self.t = t
        self.name = name
        self.psum = psum
        self.w = None
        self.r = []

    def __getitem__(self, idx):
        return self.t[idx]


class _Sub:
    def __init__(self, parent, n, nd):
        self.p = parent
        self.n = n
        self.nd = nd
        self.psum = parent.psum

    @property
    def w(self):
        return self.p.w

    @w.setter
    def w(self, v):
        self.p.w = v

    @property
    def r(self):
        return self.p.r

    @r.setter
    def r(self, v):
        self.p.r = v

    def __getitem__(self, idx):
        if self.nd == 2:
            base = self.p.t[:, 0:self.n]
        else:
            base = self.p.t[:, :, 0:self.n]
        return base[idx]


class FW:
    def __init__(self, nc, es, ndma=32):
        self.nc = nc
        self.engs = {"pe": nc.tensor, "act": nc.scalar, "dve": nc.vector, "pool": nc.gpsimd, "sp": nc.sync}
        self.sems = {}
        self.cnt = {}
        self.waited = {}
        for k in self.engs:
            self.sems[k] = es.enter_context(nc.semaphore("s_" + k))
            self.cnt[k] = 0
        self.dsem = []
        for i in range(ndma):
            k = "d%d" % i
            self.sems[k] = es.enter_context(nc.semaphore(k))
            self.cnt[k] = 0
            self.dsem.append(k)
        self.di = 0
        self.asem = []
        for i in range(4):
            k = "a%d" % i
            self.sems[k] = es.enter_context(nc.semaphore(k))
            self.cnt[k] = 0
            self.asem.append(k)
        self.ai = 0
        self.gsem = []
        for i in range(8):
            k = "g%d" % i
            self.sems[k] = es.enter_context(nc.semaphore(k))
            self.cnt[k] = 0
            self.gsem.append(k)
        self.gi = 0
        self.pending = {k: 0 for k in self.engs}
        self.nosync = {"pe", "pool", "act"}

    def _need(self, e, deps):
        mx = {}
        for d in deps:
            if d is None:
                continue
            k, c = d
            if e == k and e in self.nosync:
                continue
            if c > mx.get(k, 0):
                mx[k] = c
        for k, c in mx.items():
            if self.waited.get((e, k), 0) >= c:
                continue
            self.engs[e].wait_ge(self.sems[k], c)
            self.waited[(e, k)] = c

    def op(self, e, fn, outs=(), ins=(), inc=True):
        deps = []
        outs = list(outs) + [t for t in ins if t.psum and t not in outs]
        for t in ins:
            deps.append(t.w)
        for t in outs:
            deps.append(t.w)
            deps.extend(t.r)
        self._need(e, deps)
        inst = fn(self.engs[e])
        tag = (e, self.cnt[e] + 1)
        if inc:
            self.cnt[e] += 1
            inst.then_inc(self.sems[e], 1)
        for t in outs:
            t.w = tag
            t.r = []
        for t in ins:
            t.r.append(tag)
        return inst

    def dma(self, out_ap, in_ap, outs=(), ins=(), e="sp", detached=False):
        if detached:
            k = self.asem[self.ai % len(self.asem)]
            self.ai += 1
        elif e == "pool":
            k = self.gsem[self.gi % len(self.gsem)]
            self.gi += 1
        else:
            k = self.dsem[self.di % len(self.dsem)]
            self.di += 1
        deps = [(k, self.cnt[k])] if self.cnt[k] else []
        for t in ins:
            deps.append(t.w)
        for t in outs:
            deps.append(t.w)
            deps.extend(t.r)
        self._need(e, deps)
        inst = self.engs[e].dma_start(out=out_ap, in_=in_ap)
        self.cnt[k] += 16
        inst.then_inc(self.sems[k], 16)
        tag = (k, self.cnt[k])
        for t in outs:
            t.w = tag
            t.r = []
        for t in ins:
            t.r.append(tag)

    def barrier(self, skip=()):
        allc = [(k, c) for k, c in self.cnt.items() if c > 0 and k not in skip and not (skip and k[0] == "a")]
        for e in self.engs:
            if e in skip:
                continue
            self._need(e, [d for d in allc if d[0] != e])


def build(stop=None, debug=False):
    nc = bass.Bass("TRN2", target_bir_lowering=False)
    es = ExitStack()
    fw = FW(nc, es)

    def din(name, shape, dt=F32):
        return nc.dram_tensor(name, shape, dt, kind="ExternalInput").ap()

    def dout(name, shape, dt=F32):
        return nc.dram_tensor(name, shape, dt, kind="ExternalOutput").ap()

    def dscr(name, shape, dt):
        return nc.dram_tensor(name, shape, dt, kind="Internal").ap()

    xf_d = din("xf", [128, KT, LF])
    xo_d = din("xo", [128, KT, NT])
    cT_d = din("cT", [128, KT, 2])
    wada_d = din("w_ada", [128, KT, 6 * D])
    bada_d = din("b_ada", [128, 48])
    g1_d = din("g1", [128, KT]); g2_d = din("g2", [128, KT]); gf_d = din("gf", [128, KT])
    win_d = din("w_in", [128, KT, 3 * D])
    wglu_d = din("w_glu", [128, 4, 512]); bglu_d = din("b_glu", [128, 4]); s5d_d = din("s5d", [128, 4])
    wp5_d = din("w_p5", [128, 4, D]); wpf_d = din("w_pf", [128, 4, D]); wout_d = din("w_out", [128, KT, D])
    wg_d = din("w_g", [128, KT, DFF]); wu_d = din("w_u", [128, KT, DFF]); wd_d = din("w_d", [128, FT, D])
    lrs_d = din("lrs", [128, 32]); lis_d = din("lis", [128, 32]); lss_d = din("lss", [128, 32])
    lrq_d = din("lrq", [128, 128]); liq_d = din("liq", [128, 128]); lsq_d = din("lsq", [128, 128])
    brp_d = din("brp", [128, 4096]); bip_d = din("bip", [128, 4096])
    crp_d = din("crp", [128, 4096]); cip_d = din("cip", [128, 4096])
    h0l_d = din("h0l", [128, 64]); h0s_d = din("h0s", [128, 64])
    rampb_d = din("rampb", [128, NK + 1]); rampf_d = din("rampf", [128, NK])
    capf_d = din("capf", [128, NFB + 1]); capb_d = din("capb", [128, NFB + 1])
    cml_d = din("cml", [LO // NB, 128, LL // 128, NB], BF16); sml_d = din("sml", [LO // NB, 128, LL // 128, NB], BF16)
    cms_d = din("cms", [1, 128, LS // 128, NB], BF16); sms_d = din("sms", [1, 128, LS // 128, NB], BF16)
    ccl_d = din("ccl", [128, 128], BF16); nscl_d = din("nscl", [128, 128], BF16)
    ccs_d = din("ccs", [128, 128], BF16); nscs_d = din("nscs", [128, 128], BF16)
    ones_d = din("ones", [128, 128], BF16)
    yo_d = dout("yo", [128, KT, NT])
    st_d = dout("st", [2, 128, 64])
    ya_d = dscr("ya_s", [128, 4, NT], BF16); yb_d = dscr("yb_s", [128, 4, NT], BF16); yf_d = dscr("yf_s", [128, 4, NT], BF16)
    x1_d = dscr("x1_s", [128, KT, NT], F32)
    bp_d = dscr("bp_s", [128, TL * 32 * 2 * 128], BF16); cpd_d = dscr("cp_s", [128, 32 * 2 * 128], BF16)

    uid = [0]

    def sb(stack, name, shape, dt):
        uid[0] += 1
        name = "%s_%d" % (name, uid[0])
        return Tl(stack.enter_context(nc.sbuf_tensor(name, shape, dt)), name)

    def ps(stack, name, shape, dt=F32):
        uid[0] += 1
        name = "%s_%d" % (name, uid[0])
        return Tl(stack.enter_context(nc.psum_tensor(name, shape, dt)), name, psum=True)

    def tt(e, out, oap, a, aap, b, bap, op):
        fw.op(e, lambda g: g.tensor_tensor(out=oap, in0=aap, in1=bap, op=op), outs=[out], ins=[a, b])

    def tsc(e, out, oap, a, aap, s1, op0, s2=None, op1=None):
        if op1 is None:
            fw.op(e, lambda g: g.tensor_scalar(out=oap, in0=aap, scalar1=s1, scalar2=None, op0=op0),
                  outs=[out], ins=[a])
        else:
            fw.op(e, lambda g: g.tensor_scalar(out=oap, in0=aap, scalar1=s1, scalar2=s2, op0=op0, op1=op1),
                  outs=[out], ins=[a])

    def stt(out, oap, a, aap, scal, b, bap, op0, op1, extra=()):
        fw.op("dve", lambda g: g.scalar_tensor_tensor(out=oap, in0=aap, scalar=scal, in1=bap, op0=op0, op1=op1),
              outs=[out], ins=[a, b] + list(extra))

    def act(out, oap, a, aap, func, bias=None, scale=None, extra=()):
        kw = {}
        if bias is not None:
            kw["bias"] = bias
        if scale is not None:
            kw["scale"] = scale
        fw.op("act", lambda g: g.activation(out=oap, in_=aap, func=func, **kw), outs=[out], ins=[a] + list(extra))

    def cp(e, out, oap, a, aap):
        fw.op(e, lambda g: g.tensor_copy(out=oap, in_=aap), outs=[out], ins=[a])

    def mm(out, oap, l, lap, r, rap, start, stop):
        fw.op("pe", lambda g: g.matmul(oap, lhsT=lap, rhs=rap, start=start, stop=stop),
              outs=[out], ins=[l, r], inc=stop)


    P = es
    ones = sb(P, "ones", [128, 128], BF16)
    fw.dma(ones[:], ones_d[:, :], outs=[ones])
    A1 = sb(P, "A1", [128, KT, 2], F32); B1 = sb(P, "B1", [128, KT, 2], F32); G1 = sb(P, "G1", [128, KT, 2], F32)
    A2 = sb(P, "A2", [128, KT, 2], F32); B2 = sb(P, "B2", [128, KT, 2], F32); G2 = sb(P, "G2", [128, KT, 2], F32)
    gft = sb(P, "gft", [128, KT], F32)
    epst = sb(P, "epst", [128, 1], F32)
    fw.op("pool", lambda g: g.memset(epst[:], EPS), outs=[epst])
    fw.dma(gft[:], gf_d[:, :], outs=[gft])
    bglu = sb(P, "bglu", [128, 4], F32); s5d = sb(P, "s5dt", [128, 4], F32)
    fw.dma(bglu[:], bglu_d[:, :], outs=[bglu]); fw.dma(s5d[:], s5d_d[:, :], outs=[s5d])
    capF = sb(P, "capF", [128, NFB + 1], F32); capB = sb(P, "capB", [128, NFB + 1], F32)
    fw.dma(capF[:], capf_d[:, :], outs=[capF]); fw.dma(capB[:], capb_d[:, :], outs=[capB])
    PWR = sb(P, "PWR", [128, 32, 5], F32); PWI = sb(P, "PWI", [128, 32, 5], F32); NPWI = sb(P, "NPWI", [128, 32, 5], F32)
    PWRr = sb(P, "PWRr", [128, 32, 4], F32); PWIr = sb(P, "PWIr", [128, 32, 4], F32); NPWIr = sb(P, "NPWIr", [128, 32, 4], F32)
    PIS4 = sb(P, "PIS4", [128, 32, 2], F32)
    PISf = sb(P, "PISf", [128, 32, 2, 4], F32)
    PISr = sb(P, "PISr", [128, 32, 2, 4], F32)
    ACC = [sb(P, "ACC%d" % d, [128, 16, 2], F32) for d in range(2)]
    ZST = [sb(P, "ZST%d" % d, [128, 16, 2], F32) for d in range(2)]
    H0L = [sb(P, "H0L%d" % d, [128, 16, 2], F32) for d in range(2)]
    for d in range(2):
        fw.dma(H0L[d][:], h0l_d[:, d * 32:(d + 1) * 32].rearrange("p (a b) -> p a b", b=2), outs=[H0L[d]])
        fw.dma(ZST[d][:], h0s_d[:, d * 32:(d + 1) * 32].rearrange("p (a b) -> p a b", b=2), outs=[ZST[d]])

    def discretize(S, n, LR, LI, LSt, pre):
        t = {}
        for nm in ["step", "dr", "di", "mag", "q", "r", "m", "s", "c", "are", "aim", "nr", "den", "fr", "fi", "tmp"]:
            t[nm] = sb(S, pre + nm, [128, n], F32)
        qi = sb(S, pre + "qi", [128, n], I32)
        act(t["step"], t["step"][:], LSt, LSt[:], AF.Exp)
        tt("dve", t["dr"], t["dr"][:], LR, LR[:], t["step"], t["step"][:], ALU.mult)
        tt("dve", t["di"], t["di"][:], LI, LI[:], t["step"], t["step"][:], ALU.mult)
        act(t["mag"], t["mag"][:], t["dr"], t["dr"][:], AF.Exp)
        tsc("dve", t["q"], t["q"][:], t["di"], t["di"][:], 1.0 / (2 * PI), ALU.mult)
        cp("dve", qi, qi[:], t["q"], t["q"][:])
        cp("dve", t["q"], t["q"][:], qi, qi[:])
        stt(t["r"], t["r"][:], t["q"], t["q"][:], -2 * PI, t["di"], t["di"][:], ALU.mult, ALU.add)
        tsc("dve", t["m"], t["m"][:], t["r"], t["r"][:], PI, ALU.is_gt)
        stt(t["r"], t["r"][:], t["m"], t["m"][:], -2 * PI, t["r"], t["r"][:], ALU.mult, ALU.add)
        tsc("dve", t["m"], t["m"][:], t["r"], t["r"][:], -PI, ALU.is_lt)
        stt(t["r"], t["r"][:], t["m"], t["m"][:], 2 * PI, t["r"], t["r"][:], ALU.mult, ALU.add)
        act(t["s"], t["s"][:], t["r"], t["r"][:], AF.Sin)
        tsc("dve", t["q"], t["q"][:], t["r"], t["r"][:], PI / 2, ALU.add)
        tsc("dve", t["m"], t["m"][:], t["q"], t["q"][:], PI, ALU.is_gt)
        stt(t["q"], t["q"][:], t["m"], t["m"][:], -2 * PI, t["q"], t["q"][:], ALU.mult, ALU.add)
        act(t["c"], t["c"][:], t["q"], t["q"][:], AF.Sin)
        tt("dve", t["are"], t["are"][:], t["mag"], t["mag"][:], t["c"], t["c"][:], ALU.mult)
        tt("dve", t["aim"], t["aim"][:], t["mag"], t["mag"][:], t["s"], t["s"][:], ALU.mult)
        tsc("dve", t["nr"], t["nr"][:], t["are"], t["are"][:], -1.0, ALU.add)
        tt("dve", t["den"], t["den"][:], LR, LR[:], LR, LR[:], ALU.mult)
        tt("dve", t["tmp"], t["tmp"][:], LI, LI[:], LI, LI[:], ALU.mult)
        tt("dve", t["den"], t["den"][:], t["den"], t["den"][:], t["tmp"], t["tmp"][:], ALU.add)
        fw.op("dve", lambda g: g.reciprocal(out=t["den"][:], in_=t["den"][:]), outs=[t["den"]], ins=[t["den"]])
        tt("dve", t["fr"], t["fr"][:], t["nr"], t["nr"][:], LR, LR[:], ALU.mult)
        tt("dve", t["tmp"], t["tmp"][:], t["aim"], t["aim"][:], LI, LI[:], ALU.mult)
        tt("dve", t["fr"], t["fr"][:], t["fr"], t["fr"][:], t["tmp"], t["tmp"][:], ALU.add)
        tt("dve", t["fr"], t["fr"][:], t["fr"], t["fr"][:], t["den"], t["den"][:], ALU.mult)
        tt("dve", t["fi"], t["fi"][:], t["aim"], t["aim"][:], LR, LR[:], ALU.mult)
        tt("dve", t["tmp"], t["tmp"][:], t["nr"], t["nr"][:], LI, LI[:], ALU.mult)
        tt("dve", t["fi"], t["fi"][:], t["fi"], t["fi"][:], t["tmp"], t["tmp"][:], ALU.subtract)
        tt("dve", t["fi"], t["fi"][:], t["fi"], t["fi"][:], t["den"], t["den"][:], ALU.mult)
        return t["are"], t["aim"], t["fr"], t["fi"]


    def cmul(S, n, ar, ai, br, bi, pre):
        orr = sb(S, pre + "re", [128, n], F32); oi = sb(S, pre + "im", [128, n], F32); t_ = sb(S, pre + "t", [128, n], F32)
        tt("dve", orr, orr[:], ar, ar[:], br, br[:], ALU.mult)
        tt("dve", t_, t_[:], ai, ai[:], bi, bi[:], ALU.mult)
        tt("dve", orr, orr[:], orr, orr[:], t_, t_[:], ALU.subtract)
        tt("dve", oi, oi[:], ar, ar[:], bi, bi[:], ALU.mult)
        tt("dve", t_, t_[:], ai, ai[:], br, br[:], ALU.mult)
        tt("dve", oi, oi[:], oi, oi[:], t_, t_[:], ALU.add)
        return orr, oi

    def expi(S, shape, dr, drap, di, diap, pre):
        t = {}
        for nm in ["mag", "q", "r", "m", "s", "c"]:
            t[nm] = sb(S, pre + nm, shape, F32)
        qi = sb(S, pre + "qi", shape, I32)
        A_ = lambda T_: T_[:]
        act(t["mag"], A_(t["mag"]), dr, drap, AF.Exp)
        tsc("dve", t["q"], A_(t["q"]), di, diap, 1.0 / (2 * PI), ALU.mult)
        cp("dve", qi, qi[:], t["q"], A_(t["q"]))
        cp("dve", t["q"], A_(t["q"]), qi, qi[:])
        stt(t["r"], A_(t["r"]), t["q"], A_(t["q"]), -2 * PI, di, diap, ALU.mult, ALU.add)
        tsc("dve", t["m"], A_(t["m"]), t["r"], A_(t["r"]), PI, ALU.is_gt)
        stt(t["r"], A_(t["r"]), t["m"], A_(t["m"]), -2 * PI, t["r"], A_(t["r"]), ALU.mult, ALU.add)
        tsc("dve", t["m"], A_(t["m"]), t["r"], A_(t["r"]), -PI, ALU.is_lt)
        stt(t["r"], A_(t["r"]), t["m"], A_(t["m"]), 2 * PI, t["r"], A_(t["r"]), ALU.mult, ALU.add)
        act(t["s"], A_(t["s"]), t["r"], A_(t["r"]), AF.Sin)
        tsc("dve", t["q"], A_(t["q"]), t["r"], A_(t["r"]), PI / 2, ALU.add)
        tsc("dve", t["m"], A_(t["m"]), t["q"], A_(t["q"]), PI, ALU.is_gt)
        stt(t["q"], A_(t["q"]), t["m"], A_(t["m"]), -2 * PI, t["q"], A_(t["q"]), ALU.mult, ALU.add)
        act(t["c"], A_(t["c"]), t["q"], A_(t["q"]), AF.Sin)
        tt("dve", t["c"], A_(t["c"]), t["mag"], A_(t["mag"]), t["c"], A_(t["c"]), ALU.mult)
        tt("dve", t["s"], A_(t["s"]), t["mag"], A_(t["mag"]), t["s"], A_(t["s"]), ALU.mult)
        return t["c"], t["s"]

    with ExitStack() as SA:
        S = SA
        cTf = sb(S, "cTf", [128, KT, 2], F32)
        fw.dma(cTf[:], cT_d[:, :, :], outs=[cTf])
        cT = sb(S, "cT", [128, KT, 2], BF16)
        act(cT, cT[:], cTf, cTf[:], AF.Silu)
        pm = ps(S, "pm", [128, 48, 2])
        was = [sb(S, "wa%d" % i, [128, KT, 512], BF16) for i in range(4)]
        for j in range(12):
            wa = was[j % 4]
            fw.dma(wa[:], wada_d[:, :, j * 512:(j + 1) * 512], outs=[wa], e="pool", detached=True)
            for ft in range(4):
                f = j * 4 + ft
                for kt in range(KT):
                    mm(pm, pm[:, f, :], wa, wa[:, kt, ft * 128:(ft + 1) * 128], cT, cT[:, kt, :], kt == 0, kt == KT - 1)
        with ExitStack() as S:
            lrs = sb(S, "lrs", [128, 32], F32); lis = sb(S, "lis", [128, 32], F32); lss = sb(S, "lss", [128, 32], F32)
            fw.dma(lrs[:], lrs_d[:, :], outs=[lrs]); fw.dma(lis[:], lis_d[:, :], outs=[lis]); fw.dma(lss[:], lss_d[:, :], outs=[lss])
            are, aim, _, _ = discretize(S, 32, lrs, lis, lss, "ds_")
            fw.op("dve", lambda g: g.memset(PWR[:, :, 0:1], 1.0), outs=[PWR])
            fw.op("dve", lambda g: g.memset(PWI[:, :, 0:1], 0.0), outs=[PWI])
            cp("dve", PWR, PWR[:, :, 1], are, are[:]); cp("dve", PWI, PWI[:, :, 1], aim, aim[:])
            t1 = sb(S, "pt1", [128, 32], F32); t2 = sb(S, "pt2", [128, 32], F32)
            for n in range(2, 5):
                tt("dve", t1, t1[:], PWR, PWR[:, :, n - 1], are, are[:], ALU.mult)
                tt("dve", t2, t2[:], PWI, PWI[:, :, n - 1], aim, aim[:], ALU.mult)
                tt("dve", PWR, PWR[:, :, n], t1, t1[:], t2, t2[:], ALU.subtract)
                tt("dve", t1, t1[:], PWR, PWR[:, :, n - 1], aim, aim[:], ALU.mult)
                tt("dve", t2, t2[:], PWI, PWI[:, :, n - 1], are, are[:], ALU.mult)
                tt("dve", PWI, PWI[:, :, n], t1, t1[:], t2, t2[:], ALU.add)
            tsc("dve", NPWI, NPWI[:], PWI, PWI[:], -1.0, ALU.mult)
            cp("dve", PIS4, PIS4[:, :, 0], NPWI, NPWI[:, :, 4]); cp("dve", PIS4, PIS4[:, :, 1], PWI, PWI[:, :, 4])
            cp("dve", PISf, PISf[:, :, 0, :], NPWI, NPWI[:, :, 1:5]); cp("dve", PISf, PISf[:, :, 1, :], PWI, PWI[:, :, 1:5])
            for i in range(4):
                cp("dve", PWRr, PWRr[:, :, i], PWR, PWR[:, :, 4 - i])
                cp("dve", PWIr, PWIr[:, :, i], PWI, PWI[:, :, 4 - i])
                cp("dve", NPWIr, NPWIr[:, :, i], NPWI, NPWI[:, :, 4 - i])
                cp("dve", PISr, PISr[:, :, 0, i], NPWI, NPWI[:, :, 4 - i]); cp("dve", PISr, PISr[:, :, 1, i], PWI, PWI[:, :, 4 - i])
            fw.barrier(skip=("pe", "pool"))
        with ExitStack() as S:
            Bp4 = sb(S, "Bp4", [128, TL, 32, 2, 128], BF16)
            Cpd = sb(S, "Cpd", [128, 32, 2, 128], BF16)
            gt_d = dscr("gt_s", [2 * TL, 32 * 128], F32)
            with ExitStack() as S2:
                lrq = sb(S2, "lrq", [128, 128], F32); liq = sb(S2, "liq", [128, 128], F32); lsq = sb(S2, "lsq", [128, 128], F32)
                fw.dma(lrq[:], lrq_d[:, :], outs=[lrq]); fw.dma(liq[:], liq_d[:, :], outs=[liq]); fw.dma(lsq[:], lsq_d[:, :], outs=[lsq])
                are_q, aim_q, gr, gi_ = discretize(S2, 128, lrq, liq, lsq, "dq_")
                for j in range(TL):
                    for ri, T_ in ((0, gr), (1, gi_)):
                        fw.dma(gt_d[2 * j + ri:2 * j + ri + 1, :].rearrange("o (q c) -> (o q) c", c=128), T_[0:32, :], outs=[], ins=[T_])
                    if j < TL - 1:
                        gr, gi_ = cmul(S2, 128, gr, gi_, are_q, aim_q, "gq%d_" % j)
                fw.barrier(skip=("pe", "pool"))
            v3 = lambda T_: T_[:].rearrange("p (q c) -> p q c", c=128)
            QW = 1024
            with ExitStack() as SBp:
                sets = [{k_: sb(SBp, "%s%d" % (k_, i_), [128, QW], F32) for k_ in ("gbr", "gbi", "wr", "wi", "t")} for i_ in range(2)]
                brs = [sb(SBp, "brq%d" % i_, [128, QW], F32) for i_ in range(2)]
                bis = [sb(SBp, "biq%d" % i_, [128, QW], F32) for i_ in range(2)]
                it_ = 0
                for qt in range(4096 // QW):
                    c0 = qt * QW
                    br_ = brs[qt % 2]; bi_ = bis[qt % 2]
                    fw.dma(br_[:], brp_d[:, c0:c0 + QW], outs=[br_]); fw.dma(bi_[:], bip_d[:, c0:c0 + QW], outs=[bi_])
                    nq = QW // 128
                    for j in range(TL):
                        T_ = sets[it_ % 2]; it_ += 1
                        gbr, gbi, wr, wi, t_ = T_["gbr"], T_["gbi"], T_["wr"], T_["wi"], T_["t"]
                        fw.dma(gbr[:], gt_d[2 * j:2 * j + 1, c0:c0 + QW].broadcast_to([128, QW]), outs=[gbr])
                        fw.dma(gbi[:], gt_d[2 * j + 1:2 * j + 2, c0:c0 + QW].broadcast_to([128, QW]), outs=[gbi])
                        tt("dve", wr, wr[:], gbr, gbr[:], br_, br_[:], ALU.mult)
                        tt("dve", t_, t_[:], gbi, gbi[:], bi_, bi_[:], ALU.mult)
                        tt("dve", wr, wr[:], wr, wr[:], t_, t_[:], ALU.subtract)
                        tt("dve", wi, wi[:], gbr, gbr[:], bi_, bi_[:], ALU.mult)
                        tt("dve", t_, t_[:], gbi, gbi[:], br_, br_[:], ALU.mult)
                        tt("dve", wi, wi[:], wi, wi[:], t_, t_[:], ALU.add)
                        act(Bp4, Bp4[:, j, nq * qt:nq * qt + nq, 0, :], wr, v3(wr), AF.Copy)
                        act(Bp4, Bp4[:, j, nq * qt:nq * qt + nq, 1, :], wi, v3(wi), AF.Copy)
                fw.barrier(skip=("pe", "pool"))
            with ExitStack() as S2:
                cr = sb(S2, "crt", [128, 4096], F32); ci = sb(S2, "cit", [128, 4096], F32)
                fw.dma(cr[:], crp_d[:, :], outs=[cr]); fw.dma(ci[:], cip_d[:, :], outs=[ci])
                v3 = lambda T_: T_[:].rearrange("p (q c) -> p q c", c=128)
                cp("dve", Cpd, Cpd[:, :, 0, :], cr, v3(cr))
                tsc("dve", Cpd, Cpd[:, :, 1, :], ci, v3(ci), -1.0, ALU.mult)
                fw.barrier(skip=("pe", "pool"))
            fw.dma(bp_d[:, :], Bp4[:].rearrange("p a b c d -> p (a b c d)"), outs=[], ins=[Bp4])
            fw.dma(cpd_d[:, :], Cpd[:].rearrange("p b c d -> p (b c d)"), outs=[], ins=[Cpd])
            if debug:
                for nm, T_, shp, dt_ in (("A1", A1, [128, KT, 2], F32), ("PWR", PWR, [128, 32, 5], F32), ("PWI", PWI, [128, 32, 5], F32),
                                         ("Bp4", Bp4, [128, TL, 32, 2, 128], BF16), ("Cpd", Cpd, [128, 32, 2, 128], BF16)):
                    dd = dout("dbg_" + nm, shp, dt_)
                    fw.dma(dd, T_[:], outs=[], ins=[T_])
            fw.barrier(skip=("pe", "pool"))
        S = SA
        bada = sb(S, "bada", [128, 48], F32)
        fw.dma(bada[:], bada_d[:, :], outs=[bada])
        MOD = sb(S, "MOD", [128, 48, 2], F32)
        tt("dve", MOD, MOD[:], pm, pm[:], bada, bada[:].unsqueeze(2).broadcast_to([128, 48, 2]), ALU.add)
        g1t = sb(S, "g1t", [128, KT], F32); g2t = sb(S, "g2t", [128, KT], F32)
        fw.dma(g1t[:], g1_d[:, :], outs=[g1t]); fw.dma(g2t[:], g2_d[:, :], outs=[g2t])
        for (A, Bt, G, gt, base) in ((A1, B1, G1, g1t, 0), (A2, B2, G2, g2t, 24)):
            cp("dve", Bt, Bt[:], MOD, MOD[:, base:base + 8, :])
            cp("dve", G, G[:], MOD, MOD[:, base + 16:base + 24, :])
            tsc("dve", A, A[:], MOD, MOD[:, base + 8:base + 16, :], 1.0, ALU.add)
            tt("dve", A, A[:], A, A[:], gt, gt[:].unsqueeze(2).broadcast_to([128, KT, 2]), ALU.mult)
        fw.barrier()
    if stop == "setup":
        es.close()
        return nc

    def load_w(S, name, dram_ap, kt, cols, stage, w=None):
        if w is None:
            w = sb(S, name, [128, kt, cols], BF16)
        for k in range(kt):
            fw.dma(w[:, k, :], dram_ap[:, k, 0:cols], outs=[w], e="pool")
        return w

    def norm_tmps(S, pfx, n, nbuf=2):
        return [{"sq": sb(S, pfx + "sq%d" % i, [128, KT, n], BF16), "rstd": sb(S, pfx + "rstd%d" % i, [128, n], F32),
                 "tmp": sb(S, pfx + "ntmp%d" % i, [128, KT, n], F32)} for i in range(nbuf)]

    def norm_block(T_, xb, n, A, Bt, j, hT, pss):
        sq = T_["sq"]; rstd = T_["rstd"]; tmp = T_["tmp"]
        act(sq, sq[:], xb, xb[:, :, 0:n], AF.Square)
        for kt in range(KT):
            mm(pss, pss[:, 0:n], ones, ones[:], sq, sq[:, kt, :], kt == 0, kt == KT - 1)
        act(rstd, rstd[:], pss, pss[:, 0:n], AF.Sqrt, bias=epst[:, 0:1], scale=1.0 / D, extra=[epst])
        fw.op("dve", lambda g: g.reciprocal(out=rstd[:], in_=rstd[:]), outs=[rstd], ins=[rstd])
        if hT is not None:
            for kt in range(KT):
                stt(tmp, tmp[:, kt, :], xb, xb[:, kt, 0:n], A[:, kt, j:j + 1], rstd, rstd[:], ALU.mult, ALU.mult, extra=[A])
            for kt in range(KT):
                act(hT, hT[:, kt, 0:n], tmp, tmp[:, kt, :], AF.Identity, bias=Bt[:, kt, j:j + 1], scale=1.0, extra=[Bt])
        return rstd

    def phase1_bufs(S1):
        return {"pu": [ps(S1, "pu%d" % i, [128, 512]) for i in range(4)],
                "pss": [ps(S1, "pss1_%d" % i, [128, 512]) for i in range(2)],
                "xbs": [sb(S1, "xb1_%d" % i, [128, KT, 512], F32) for i in range(2)],
                "hTs": [sb(S1, "hT1_%d" % i, [128, KT, 512], BF16) for i in range(2)],
                "nt": norm_tmps(S1, "n1_", 512), "ip": 0, "ib": 0}

    def phase1(Bf, w1, x_ap, L, j, US5p, UF, uf_tile0):
        BN = min(512, L)
        pu = Bf["pu"]; pss = Bf["pss"]; xbs = Bf["xbs"]; hTs = Bf["hTs"]; nt = Bf["nt"]
        nb_ = L // BN
        base = Bf["ib"]

        def view(T_, n):
            return {"sq": _Sub(T_["sq"], n, 3), "rstd": _Sub(T_["rstd"], n, 2), "tmp": _Sub(T_["tmp"], n, 3)}

        def front(b):
            xb = xbs[(base + b) % 2]
            fw.dma(xb[:, :, 0:BN], x_ap[:, :, b * BN:(b + 1) * BN], outs=[xb])
            norm_block(view(nt[(base + b) % 2], BN), xb, BN, A1, B1, j, hTs[(base + b) % 2], pss[(base + b) % 2])

        front(0)
        for b in range(nb_):
            t0 = b * BN
            xb = xbs[(base + b) % 2]; hT = hTs[(base + b) % 2]
            if b + 1 < nb_:
                front(b + 1)
            for m in range(4):
                p = pu[Bf["ip"] % 4]; Bf["ip"] += 1
                for kt in range(KT):
                    mm(p, p[:, 0:BN], w1, w1[:, kt, m * 128:(m + 1) * 128], hT, hT[:, kt, 0:BN], kt == 0, kt == KT - 1)
                act(US5p, US5p[:, m, 3 + t0:3 + t0 + BN], p, p[:, 0:BN], AF.Copy)
            for ts_ in range(BN // 128):
                p = pu[Bf["ip"] % 4]; Bf["ip"] += 1
                for kt in range(KT):
                    mm(p, p[:, :], hT, hT[:, kt, ts_ * 128:(ts_ + 1) * 128], w1, w1[:, kt, 512:1024], kt == 0, kt == KT - 1)
                cp("dve", UF, UF[:, uf_tile0 + t0 // 128 + ts_, :], p, p[:, :])
        Bf["ib"] = base + nb_

    def s5_run(S1, segs):
        bp_v = bp_d.rearrange("p (a b c) -> p a b c", a=TL, b=32)
        Bp4d = []
        for d in range(2):
            t_ = sb(S1, "Bp4w%d" % d, [128, TL, 16, 256], BF16)
            fw.dma(t_[:], bp_v[:, :, d * 16:(d + 1) * 16, :], outs=[t_])
            Bp4d.append(t_)
        Cpd = sb(S1, "Cpdw", [128, 32, 2, 128], BF16)
        fw.dma(Cpd[:].rearrange("p b c d -> p (b c d)"), cpd_d[:, :], outs=[Cpd])
        HB = [[sb(S1, "HB%d_%d" % (d, i), [128, 16, 2, SB], BF16) for i in range(2)] for d in range(2)]
        YS = [[sb(S1, "YS%d_%d" % (d, i), [128, 4, SB], BF16) for i in range(2)] for d in range(2)]
        py = [ps(S1, "py%d" % d, [128, 4, SB]) for d in range(2)]
        V = [[sb(S1, "V%d_%d" % (d, i), [128, 16, 2, SB], F32) for i in range(2)] for d in range(2)]
        XE = [sb(S1, "XE%d" % d, [128, 16, 2], F32) for d in range(2)]
        HX = [sb(S1, "HX%d" % d, [128, 16, 2, SB + 4], F32) for d in range(2)]
        M1 = [sb(S1, "M1_%d" % d, [128, 16, 2, 4], F32) for d in range(2)]
        M2 = [sb(S1, "M2_%d" % d, [128, 16, 2, 4], F32) for d in range(2)]
        pbu = [[ps(S1, "pbu%d_%d" % (d, i), [128, 4, 2, SB]) for i in range(2)] for d in range(2)]
        engs = ["dve", "pool"]
        blocks = []
        for sg in segs:
            for bl in range(sg[1] // SB):
                blocks.append((sg, bl))

        def cmuladd(e, d, out, oap, pr, pis, src, sap, add, aap):
            m1 = M1[d]; m2 = M2[d]
            prr = pr.unsqueeze(2).broadcast_to([128, 16, 2, 4])
            tt(e, m1, m1[:], src, sap, PWR, prr, ALU.mult)
            tt(e, m2, m2[:], src, sap[:, :, ::-1, :], PWR, pis, ALU.mult)
            tt(e, m1, m1[:], m1, m1[:], m2, m2[:], ALU.add)
            tt(e, out, oap, m1, m1[:], add, aap, ALU.add)

        def stageA(gi):
            (US5p, L, X0, off, si), bi = blocks[gi]
            nsb = L // SB
            for d in range(2):
                t0 = (bi if d == 0 else nsb - 1 - bi) * SB
                q0 = d * 16
                v = V[d][gi % 2]
                for pr_ in range(16):
                    pb = pbu[d][(pr_ // 4) % 2]
                    for ri in range(2):
                        for j in range(TL):
                            sh = -j if d == 0 else j
                            mm(pb, pb[:, pr_ % 4, ri, :], Bp4d[d], Bp4d[d][:, j, pr_, ri * 128:(ri + 1) * 128], US5p,
                               US5p[:, pr_ // 4, 3 + t0 + sh:3 + t0 + sh + SB], j == 0, j == TL - 1)
                    if pr_ % 4 == 3:
                        act(v, v[:, pr_ - 3:pr_ + 1, :, :], pb, pb[:], AF.Copy)

        def stageB(gi):
            (US5p, L, X0, off, si), bi = blocks[gi]
            nsb = L // SB
            for d in range(2):
                e = engs[d]
                q0 = d * 16
                hx = HX[d]; v = V[d][gi % 2]
                p4 = PWR[:, q0:q0 + 16, 4:5].broadcast_to([128, 16, 4])
                pis4 = PIS4[:, q0:q0 + 16, :].unsqueeze(3).broadcast_to([128, 16, 2, 4])
                ng = SB // 4
                if d == 0:
                    for m in range(ng):
                        if bi == 0 and m == 0:
                            x0b = X0[d][:].unsqueeze(3).broadcast_to([128, 16, 2, 4])
                            cmuladd(e, d, hx, hx[:, :, :, 4:8], PWR[:, q0:q0 + 16, 1:5], PISf[:, q0:q0 + 16, :, :],
                                    X0[d], x0b, v, v[:, :, :, 0:4])
                        else:
                            cmuladd(e, d, hx, hx[:, :, :, 4 + 4 * m:8 + 4 * m], p4, pis4, hx, hx[:, :, :, 4 * m:4 * m + 4], v, v[:, :, :, 4 * m:4 * m + 4])
                    xend = hx[:, :, :, SB + 3]
                    data = hx[:, :, :, 4:4 + SB]
                else:
                    for m in range(ng):
                        lo = SB - 4 - 4 * m
                        if bi == 0 and m == 0:
                            x0b = X0[d][:].unsqueeze(3).broadcast_to([128, 16, 2, 4])
                            cmuladd(e, d, hx, hx[:, :, :, lo:lo + 4], PWRr[:, q0:q0 + 16, :], PISr[:, q0:q0 + 16, :, :],
                                    X0[d], x0b, v, v[:, :, :, lo:lo + 4])
                        else:
                            cmuladd(e, d, hx, hx[:, :, :, lo:lo + 4], p4, pis4, hx, hx[:, :, :, lo + 4:lo + 8], v, v[:, :, :, lo:lo + 4])
                    xend = hx[:, :, :, 0]
                    data = hx[:, :, :, 0:SB]
                hb = HB[d][gi % 2]
                act(hb, hb[:], hx, data, AF.Copy)
                if si is not None and bi == nsb - 1:
                    cp(e, XE[d], XE[d][:], hx, xend)
                    fw.dma(st_d[si, :, d * 32:(d + 1) * 32].rearrange("p (a b) -> p a b", b=2), XE[d][:], outs=[], ins=[XE[d]])
                if bi < nsb - 1:
                    if d == 0:
                        cp(e, hx, hx[:, :, :, 0:4], hx, hx[:, :, :, SB:SB + 4])
                    else:
                        cp(e, hx, hx[:, :, :, SB:SB + 4], hx, hx[:, :, :, 0:4])

        def stageC(gi):
            (US5p, L, X0, off, si), bi = blocks[gi]
            nsb = L // SB
            for d in range(2):
                t0 = (bi if d == 0 else nsb - 1 - bi) * SB
                q0 = d * 16
                hb = HB[d][gi % 2]; ys = YS[d][gi % 2]
                pyd = py[d]
                for c in range(4):
                    i = 0
                    for pr_ in range(4 * c, 4 * c + 4):
                        for ri in range(2):
                            mm(pyd, pyd[:, c, :], Cpd, Cpd[:, q0 + pr_, ri, :], hb, hb[:, pr_, ri, :], i == 0, i == 7)
                            i += 1
                if d == 0:
                    for c in range(4):
                        stt(ys, ys[:, c, :], US5p, US5p[:, c, 3 + t0:3 + t0 + SB], s5d[:, c:c + 1], pyd, pyd[:, c, :], ALU.mult, ALU.add, extra=[s5d])
                    fw.dma(ya_d[:, :, off + t0:off + t0 + SB], ys[:], outs=[], ins=[ys])
                else:
                    act(ys, ys[:], pyd, pyd[:], AF.Copy)
                    fw.dma(yb_d[:, :, off + t0:off + t0 + SB], ys[:], outs=[], ins=[ys])

        n = len(blocks)
        stageA(0); stageB(0)
        for gi in range(1, n):
            stageA(gi); stageB(gi); stageC(gi - 1)
        stageC(n - 1)
        fw.barrier()

    def s5_far(S1, US5p):
        Bp4 = sb(S1, "Bp4f", [128, TL, 32, 2, 128], BF16)
        fw.dma(Bp4[:].rearrange("p a b c d -> p (a b c d)"), bp_d[:, :], outs=[Bp4])
        PTbR = sb(S1, "PTbR", [128, 32, NK + 1], F32); PTbI = sb(S1, "PTbI", [128, 32, NK + 1], F32)
        PTfR = sb(S1, "PTfR", [128, 32, NK], F32); PTfI = sb(S1, "PTfI", [128, 32, NK], F32)
        with ExitStack() as S2:
            lrs = sb(S2, "flrs", [128, 32], F32); lis = sb(S2, "flis", [128, 32], F32); lss = sb(S2, "flss", [128, 32], F32)
            fw.dma(lrs[:], lrs_d[:, :], outs=[lrs]); fw.dma(lis[:], lis_d[:, :], outs=[lis]); fw.dma(lss[:], lss_d[:, :], outs=[lss])
            rb = sb(S2, "rampb", [128, NK + 1], F32); rf = sb(S2, "rampf", [128, NK], F32)
            fw.dma(rb[:], rampb_d[:, :], outs=[rb]); fw.dma(rf[:], rampf_d[:, :], outs=[rf])
            act(lss, lss[:], lss, lss[:], AF.Exp)
            tt("dve", lrs, lrs[:], lrs, lrs[:], lss, lss[:], ALU.mult)
            tt("dve", lis, lis[:], lis, lis[:], lss, lss[:], ALU.mult)
            for (ramp, K, PR, PI_) in ((rb, NK + 1, PTbR, PTbI), (rf, NK, PTfR, PTfI)):
                with ExitStack() as S3:
                    ar_ = sb(S3, "argr", [128, 32, K], F32); ai_ = sb(S3, "argi", [128, 32, K], F32)
                    rbc = ramp[:].unsqueeze(1).broadcast_to([128, 32, K])
                    tt("dve", ar_, ar_[:], lrs, lrs[:].unsqueeze(2).broadcast_to([128, 32, K]), ramp, rbc, ALU.mult)
                    tt("dve", ai_, ai_[:], lis, lis[:].unsqueeze(2).broadcast_to([128, 32, K]), ramp, rbc, ALU.mult)
                    re_, im_ = expi(S3, [128, 32, K], ar_, ar_[:], ai_, ai_[:], "pt_")
                    cp("dve", PR, PR[:], re_, re_[:]); cp("dve", PI_, PI_[:], im_, im_[:])
                    fw.barrier()
            fw.barrier()
        nfb = LF // FB
        V = [sb(S1, "Vf%d" % d, [128, 16, 2, NK], F32) for d in range(2)]
        M1s = [sb(S1, "M1f%d" % d, [128, 16, 2, NK], F32) for d in range(2)]; M2s = [sb(S1, "M2f%d" % d, [128, 16, 2, NK], F32) for d in range(2)]
        Ws = [sb(S1, "Wf%d" % d, [128, 16, 2], F32) for d in range(2)]
        X = [sb(S1, "Xf%d" % d, [128, 16, 2], F32) for d in range(2)]
        T1 = sb(S1, "T1f", [128, 16, 2], F32); T2 = sb(S1, "T2f", [128, 16, 2], F32)
        pbu = [[ps(S1, "pbf%d_%d" % (d, i), [128, 4, 2, NK]) for i in range(2)] for d in range(2)]
        for d, cap in ((0, capF), (1, capB)):
            cp("dve", X[d], X[d][:], H0L[d], H0L[d][:])
            fw.op("dve", lambda g: g.tensor_scalar(out=ACC[d][:], in0=H0L[d][:], scalar1=cap[:, 0:1], scalar2=None, op0=ALU.mult),
                  outs=[ACC[d]], ins=[H0L[d], cap])
        for bi in range(nfb):
            for d in range(2):
                blk = bi if d == 0 else nfb - 1 - bi
                t0 = blk * FB
                q0 = d * 16
                v = V[d]
                for pr_ in range(16):
                    pb = pbu[d][(pr_ // 4) % 2]
                    for ri in range(2):
                        for j in range(TL):
                            st_ = 3 + t0 + (TL - 1 - j if d == 0 else j)
                            mm(pb, pb[:, pr_ % 4, ri, :], Bp4, Bp4[:, j, q0 + pr_, ri, :], US5p,
                               US5p[:, pr_ // 4, st_:st_ + FB:TL], j == 0, j == TL - 1)
                    if pr_ % 4 == 3:
                        act(v, v[:, pr_ - 3:pr_ + 1, :, :], pb, pb[:], AF.Copy)
                if d == 0:
                    ptr = PTfR[:, q0:q0 + 16, :]; pti = PTfI[:, q0:q0 + 16, :]
                else:
                    ptr = PTbR[:, q0:q0 + 16, 0:NK]; pti = PTbI[:, q0:q0 + 16, 0:NK]
                e = "dve" if d == 0 else "pool"
                M1 = M1s[d]; M2 = M2s[d]; W = Ws[d]
                tt(e, M1, M1[:, :, 0, :], v, v[:, :, 0, :], PTbR, ptr, ALU.mult)
                tt(e, M2, M2[:, :, 0, :], v, v[:, :, 1, :], PTbR, pti, ALU.mult)
                tt(e, M1, M1[:, :, 0, :], M1, M1[:, :, 0, :], M2, M2[:, :, 0, :], ALU.subtract)
                tt(e, M1, M1[:, :, 1, :], v, v[:, :, 1, :], PTbR, ptr, ALU.mult)
                tt(e, M2, M2[:, :, 1, :], v, v[:, :, 0, :], PTbR, pti, ALU.mult)
                tt(e, M1, M1[:, :, 1, :], M1, M1[:, :, 1, :], M2, M2[:, :, 1, :], ALU.add)
                fw.op("dve", lambda g: g.tensor_reduce(out=W[:], in_=M1[:], axis=mybir.AxisListType.X, op=ALU.add), outs=[W], ins=[M1])
                x = X[d]
                pr128 = PTbR[:, q0:q0 + 16, NK:NK + 1].broadcast_to([128, 16, 2])
                pi128 = PTbI[:, q0:q0 + 16, NK]
                tt("dve", T1, T1[:], x, x[:], PTbR, pr128, ALU.mult)
                tt("dve", T2, T2[:, :, 0], x, x[:, :, 1], PTbR, pi128, ALU.mult)
                tt("dve", T1, T1[:, :, 0], T1, T1[:, :, 0], T2, T2[:, :, 0], ALU.subtract)
                tt("dve", T2, T2[:, :, 1], x, x[:, :, 0], PTbR, pi128, ALU.mult)
                tt("dve", T1, T1[:, :, 1], T1, T1[:, :, 1], T2, T2[:, :, 1], ALU.add)
                tt("dve", x, x[:], T1, T1[:], W, W[:], ALU.add)
                cap = capF if d == 0 else capB
                fw.op("dve", lambda g: g.scalar_tensor_tensor(out=ACC[d][:], in0=x[:], scalar=cap[:, 1 + blk:2 + blk], in1=ACC[d][:],
                                                              op0=ALU.mult, op1=ALU.add), outs=[ACC[d]], ins=[x, cap])
        fw.barrier()

    def dft_bufs(S1):
        nl = LL // 128
        return {"CMs": [sb(S1, "CM%d" % i, [128, nl, NB], BF16) for i in range(2)],
                "SMs": [sb(S1, "SM%d" % i, [128, nl, NB], BF16) for i in range(2)],
                "CC": [sb(S1, "CC%d" % i, [128, 128], BF16) for i in range(2)], "NSC": [sb(S1, "NSC%d" % i, [128, 128], BF16) for i in range(2)],
                "AC": sb(S1, "AC", [128, 4, NB], BF16), "AS": sb(S1, "AS", [128, 4, NB], BF16),
                "YFt": [sb(S1, "YFt%d" % i, [128, 4, NB], BF16) for i in range(2)],
                "pcs": [ps(S1, "pcs%d" % m, [128, 2, NB]) for m in range(4)],
                "pyfs": [ps(S1, "pyf%d" % i, [128, 2, NB]) for i in range(2)], "ik": 0}

    def dft_run(Bd, UF, nlt, nkc, cm_ap, sm_ap, cci, off):
        AC = Bd["AC"]; AS = Bd["AS"]; pcs = Bd["pcs"]; pyfs = Bd["pyfs"]
        CC = Bd["CC"][cci]; NSC = Bd["NSC"][cci]
        for kc in range(nkc):
            ik = Bd["ik"]; Bd["ik"] += 1
            CM = Bd["CMs"][ik % 2]; SM = Bd["SMs"][ik % 2]; YFt = Bd["YFt"][ik % 2]
            fw.dma(CM[:, 0:nlt, :], cm_ap[kc, :, :, :], outs=[CM])
            fw.dma(SM[:, 0:nlt, :], sm_ap[kc, :, :, :], outs=[SM])
            for m in range(4):
                for lt in range(nlt):
                    mm(pcs[m], pcs[m][:, 0, :], UF, UF[:, lt, m * 128:(m + 1) * 128], CM, CM[:, lt, :], lt == 0, lt == nlt - 1)
                for lt in range(nlt):
                    mm(pcs[m], pcs[m][:, 1, :], UF, UF[:, lt, m * 128:(m + 1) * 128], SM, SM[:, lt, :], lt == 0, lt == nlt - 1)
                act(AC, AC[:, m, :], pcs[m], pcs[m][:, 0, :], AF.Copy)
                cp("dve", AS, AS[:, m, :], pcs[m], pcs[m][:, 1, :])
            for m in range(4):
                pyf = pyfs[m // 2]
                mm(pyf, pyf[:, m % 2, :], CC, CC[:], AC, AC[:, m, :], True, False)
                mm(pyf, pyf[:, m % 2, :], NSC, NSC[:], AS, AS[:, m, :], False, True)
            for i in range(2):
                act(YFt, YFt[:, 2 * i:2 * i + 2, :], pyfs[i], pyfs[i][:], AF.Copy)
            fw.dma(yf_d[:, :, off + kc * NB:off + (kc + 1) * NB], YFt[:], outs=[], ins=[YFt])

    def zero_pads(US5p, L):
        fw.op("pool", lambda g: g.memset(US5p[:, :, 0:3], 0.0), outs=[US5p])
        fw.op("pool", lambda g: g.memset(US5p[:, :, 3 + L:6 + L], 0.0), outs=[US5p])

    def jof(t0):
        return 1 if t0 < LO else 0

    SUO = ExitStack()
    UF = sb(SUO, "UFall", [128, LL // 128, 512], BF16)
    UFs = [sb(SUO, "UFs%d" % i, [128, LS // 128, 512], BF16) for i in range(2)]
    with ExitStack() as SU:
        US5o = sb(SU, "US5o", [128, 4, LO + 6], BF16)
        US5s = [sb(SU, "US5s%d" % i, [128, 4, LS + 6], BF16) for i in range(2)]
        zero_pads(US5o, LO)
        for i in range(2):
            zero_pads(US5s[i], LS)
        with ExitStack() as SF:
            US5f = sb(SF, "US5f", [128, 4, LF + 6], BF16)
            zero_pads(US5f, LF)
            with ExitStack() as SW:
                w1 = sb(SW, "w_in1", [128, KT, 1024], BF16)
                load_w(SW, "w_in1", win_d, KT, 1024, None, w1)
                with ExitStack() as S1:
                    Bf = phase1_bufs(S1)
                    phase1(Bf, w1, xf_d, LF, 1, US5f, UF, 0)
                    phase1(Bf, w1, xo_d[:, :, 0:LO], LO, 1, US5o, UF, LF // 128)
                    for i in range(2):
                        off = LO + i * LS
                        phase1(Bf, w1, xo_d[:, :, off:off + LS], LS, 0, US5s[i], UFs[i], 0)
                    fw.barrier()
            with ExitStack() as S1:
                s5_far(S1, US5f)
            fw.barrier()
        with ExitStack() as S1:
            s5_run(S1, [(US5o, LO, ACC, 0, None)] + [(US5s[i], LS, ZST, LO + i * LS, i) for i in range(2)])
        fw.barrier()
    S3W = ExitStack()
    wgt = sb(S3W, "w_in2", [128, KT, 2048], BF16)
    wglu = sb(S3W, "wglu", [128, 4, 512], BF16); wp5 = sb(S3W, "wp5", [128, 4, D], BF16)
    wpf = sb(S3W, "wpf", [128, 4, D], BF16); wout = sb(S3W, "wout", [128, KT, D], BF16)
    for k in range(KT):
        fw.dma(wgt[:, k, :], win_d[:, k, 1024:3072], outs=[wgt], e="pool")
    load_w(S3W, "wglu", wglu_d, 4, 512, None, wglu)
    load_w(S3W, "wp5", wp5_d, 4, D, None, wp5)
    load_w(S3W, "wpf", wpf_d, 4, D, None, wpf)
    load_w(S3W, "wout", wout_d, KT, D, None, wout)
    with ExitStack() as S1:
        Bd = dft_bufs(S1)
        fw.dma(Bd["CC"][0][:], ccl_d[:, :], outs=[Bd["CC"][0]]); fw.dma(Bd["NSC"][0][:], nscl_d[:, :], outs=[Bd["NSC"][0]])
        fw.dma(Bd["CC"][1][:], ccs_d[:, :], outs=[Bd["CC"][1]]); fw.dma(Bd["NSC"][1][:], nscs_d[:, :], outs=[Bd["NSC"][1]])
        dft_run(Bd, UF, LL // 128, LO // NB, cml_d, sml_d, 0, 0)
        for i in range(2):
            dft_run(Bd, UFs[i], LS // 128, 1, cms_d, sms_d, 1, LO + i * LS)
        fw.barrier()

    with ExitStack() as S:
        pss = [ps(S, "pssA%d" % i, [128, NB]) for i in range(2)]
        pg = [ps(S, "pgA%d" % i, [128, NB]) for i in range(3)]
        pq = [ps(S, "pqA%d" % i, [128, NB]) for i in range(3)]
        xbs = [sb(S, "xbA%d" % i, [128, KT, NB], F32) for i in range(2)]
        hTs = [sb(S, "hTA%d" % i, [128, KT, NB], BF16) for i in range(2)]
        nt = norm_tmps(S, "nA_", NB)
        SG = sb(S, "SG", [128, 16, NB], BF16)
        yas = [sb(S, "yaA%d" % i, [128, 4, NB], BF16) for i in range(2)]
        ybs = [sb(S, "ybA%d" % i, [128, 4, NB], BF16) for i in range(2)]
        yfs = [sb(S, "yfA%d" % i, [128, 4, NB], BF16) for i in range(2)]
        zts = [sb(S, "ztA%d" % i, [128, NB], F32) for i in range(2)]
        ZB = sb(S, "ZB", [128, 4, NB], BF16)
        gus = [sb(S, "guA%d" % i, [128, NB], F32) for i in range(2)]
        sgls = [sb(S, "sgl%d" % i, [128, NB], F32) for i in range(2)]
        Y5 = sb(S, "Y5", [128, 4, NB], BF16)
        tqs = [sb(S, "tqA%d" % i, [128, NB], F32) for i in range(2)]
        MB = sb(S, "MB", [128, KT, NB], BF16)
        x1s = [sb(S, "x1A0", [128, KT, NB], F32)] * 2
        ig = 0; iq = 0

        def frontA(b):
            t0_ = b * NB
            fw.dma(xbs[b % 2][:], xo_d[:, :, t0_:t0_ + NB], outs=[xbs[b % 2]])
            fw.dma(yas[b % 2][:], ya_d[:, :, t0_:t0_ + NB], outs=[yas[b % 2]])
            fw.dma(ybs[b % 2][:], yb_d[:, :, t0_:t0_ + NB], outs=[ybs[b % 2]])
            fw.dma(yfs[b % 2][:], yf_d[:, :, t0_:t0_ + NB], outs=[yfs[b % 2]])
            norm_block(nt[b % 2], xbs[b % 2], NB, A1, B1, jof(t0_), hTs[b % 2], pss[b % 2])

        for b in range(NT // NB):
            t0 = b * NB
            j = jof(t0)
            xb = xbs[b % 2]; hT = hTs[b % 2]; ya = yas[b % 2]; yb = ybs[b % 2]; yf = yfs[b % 2]; x1 = x1s[b % 2]
            if b == 0:
                frontA(0)
            if b + 1 < NT // NB:
                frontA(b + 1)
            for m in range(16):
                p = pg[ig % 3]; ig += 1
                for kt in range(KT):
                    mm(p, p[:], wgt, wgt[:, kt, m * 128:(m + 1) * 128], hT, hT[:, kt, :], kt == 0, kt == KT - 1)
                act(SG, SG[:, m, :], p, p[:], AF.Sigmoid)
            for c in range(4):
                zt = zts[c % 2]; gu_ = gus[c % 2]
                tt("dve", zt, zt[:], ya, ya[:, c, :], yb, yb[:, c, :], ALU.add)
                act(gu_, gu_[:], zt, zt[:], AF.Square)
                tsc("dve", gu_, gu_[:], gu_, gu_[:], 0.044715, ALU.mult, 1.0, ALU.add)
                tt("dve", gu_, gu_[:], gu_, gu_[:], zt, zt[:], ALU.mult)
                act(gu_, gu_[:], gu_, gu_[:], AF.Sigmoid, scale=2.0 * math.sqrt(2.0 / PI))
                tt("dve", ZB, ZB[:, c, :], zt, zt[:], gu_, gu_[:], ALU.mult)
            for m in range(4):
                p = pq[iq % 3]; iq += 1
                sgl = sgls[m % 2]
                for k in range(4):
                    mm(p, p[:], wglu, wglu[:, k, m * 128:(m + 1) * 128], ZB, ZB[:, k, :], k == 0, k == 3)
                act(sgl, sgl[:], p, p[:], AF.Sigmoid, bias=bglu[:, m:m + 1], scale=1.0, extra=[bglu])
                tt("dve", Y5, Y5[:, m, :], ZB, ZB[:, m, :], sgl, sgl[:], ALU.mult)
            for m in range(KT):
                p1 = pg[ig % 3]; ig += 1
                p2 = pq[iq % 3]; iq += 1
                tq = tqs[m % 2]; sgl = sgls[m % 2]
                for k in range(4):
                    mm(p1, p1[:], wp5, wp5[:, k, m * 128:(m + 1) * 128], Y5, Y5[:, k, :], k == 0, k == 3)
                for k in range(4):
                    mm(p2, p2[:], wpf, wpf[:, k, m * 128:(m + 1) * 128], yf, yf[:, k, :], k == 0, k == 3)
                tt("dve", tq, tq[:], p1, p1[:], SG, SG[:, m, :], ALU.mult)
                tt("dve", sgl, sgl[:], p2, p2[:], SG, SG[:, 8 + m, :], ALU.mult)
                tt("pool", MB, MB[:, m, :], tq, tq[:], sgl, sgl[:], ALU.add)
            for m in range(KT):
                p = pg[ig % 3]; ig += 1
                for k in range(KT):
                    mm(p, p[:], wout, wout[:, k, m * 128:(m + 1) * 128], MB, MB[:, k, :], k == 0, k == KT - 1)
                stt(x1, x1[:, m, :], p, p[:], G1[:, m, j:j + 1], xb, xb[:, m, :], ALU.mult, ALU.add, extra=[G1])
            fw.dma(x1_d[:, :, t0:t0 + NB], x1[:], outs=[], ins=[x1])
        fw.barrier()
    S3W.close()
    SUO.close()
    if debug:
        dd = dout("dbg_x1", [128, KT, NT], F32)
        fw.dma(dd, x1_d[:, :, :], outs=[], ins=[])
        fw.barrier()
    if stop == "3a":
        es.close()
        return nc
    BN = 512
    gu_d = dscr("gu_s", [128, FT, NT], BF16)
    with ExitStack() as S:
        GRP = [(0, 6), (6, 12), (12, 17), (17, 22)]
        wgs = [sb(S, "wg_%d" % i, [128, KT, (b_ - a_) * 128], BF16) for i, (a_, b_) in enumerate(GRP)]
        wus = [sb(S, "wu_%d" % i, [128, KT, (b_ - a_) * 128], BF16) for i, (a_, b_) in enumerate(GRP)]
        for i, (a_, b_) in enumerate(GRP):
            fw.dma(wgs[i][:], wg_d[:, :, a_ * 128:b_ * 128], outs=[wgs[i]], e="pool")
            fw.dma(wus[i][:], wu_d[:, :, a_ * 128:b_ * 128], outs=[wus[i]], e="pool")
        gof = {}
        for i, (a_, b_) in enumerate(GRP):
            for m_ in range(a_, b_):
                gof[m_] = (i, m_ - a_)
        pss = [ps(S, "pssB%d" % i, [128, BN]) for i in range(2)]
        pg = [ps(S, "pgB%d" % i, [128, BN]) for i in range(3)]
        pq = [ps(S, "pqB%d" % i, [128, BN]) for i in range(2)]
        xbs = [sb(S, "xbB%d" % i, [128, KT, BN], F32) for i in range(2)]
        hTs = [sb(S, "hTB%d" % i, [128, KT, BN], BF16) for i in range(2)]
        nt = norm_tmps(S, "nB_", BN, nbuf=1) * 2
        sgts = [sb(S, "sgtB%d" % i, [128, BN], BF16) for i in range(2)]
        GU = sb(S, "GU", [128, FT, BN], BF16)
        ig = 0; iq = 0
        nbb = NT // BN

        def frontB(b):
            t0_ = b * BN
            fw.dma(xbs[b % 2][:], x1_d[:, :, t0_:t0_ + BN], outs=[xbs[b % 2]])
            norm_block(nt[b % 2], xbs[b % 2], BN, A2, B2, jof(t0_), hTs[b % 2], pss[b % 2])

        frontB(0)
        for b in range(nbb):
            t0 = b * BN
            hT = hTs[b % 2]
            if b + 1 < nbb:
                frontB(b + 1)
            for m in range(FT):
                p1 = pg[ig % 3]; ig += 1
                p2 = pq[iq % 2]; iq += 1
                sgt = sgts[m % 2]
                wg_, wu_ = wgs[gof[m][0]], wus[gof[m][0]]
                c0 = gof[m][1] * 128
                for k in range(KT):
                    mm(p1, p1[:], wg_, wg_[:, k, c0:c0 + 128], hT, hT[:, k, :], k == 0, k == KT - 1)
                for k in range(KT):
                    mm(p2, p2[:], wu_, wu_[:, k, c0:c0 + 128], hT, hT[:, k, :], k == 0, k == KT - 1)
                act(sgt, sgt[:], p1, p1[:], AF.Silu)
                tt("dve", GU, GU[:, m, :], p2, p2[:], sgt, sgt[:], ALU.mult)
            fw.dma(gu_d[:, :, t0:t0 + BN], GU[:], outs=[], ins=[GU])
        fw.barrier()
    with ExitStack() as S:
        wds = [sb(S, "wd_%d" % m_, [128, FT, 256], BF16) for m_ in range(4)]
        for m_ in range(4):
            fw.dma(wds[m_][:], wd_d[:, :, m_ * 256:(m_ + 1) * 256], outs=[wds[m_]], e="pool")
        pssf = [ps(S, "pssF%d" % i, [128, BN]) for i in range(2)]
        pg = [ps(S, "pgD%d" % i, [128, BN]) for i in range(4)]
        xbs = [sb(S, "xbD%d" % i, [128, KT, BN], F32) for i in range(2)]
        GUs = [sb(S, "GUD%d" % i, [128, FT, BN], BF16) for i in range(2)]
        ntf = [{"sq": sb(S, "nF_sq%d" % i, [128, KT, BN], BF16), "rstd": sb(S, "nF_rstd%d" % i, [128, BN], F32), "tmp": None} for i in range(2)]
        ig = 0
        nbb = NT // BN

        def frontD(b):
            t0_ = b * BN
            fw.dma(xbs[b % 2][:], x1_d[:, :, t0_:t0_ + BN], outs=[xbs[b % 2]])
            fw.dma(GUs[b % 2][:], gu_d[:, :, t0_:t0_ + BN], outs=[GUs[b % 2]])

        frontD(0)
        for b in range(nbb):
            t0 = b * BN
            j = jof(t0)
            xb = xbs[b % 2]; GU = GUs[b % 2]
            if b + 1 < nbb:
                frontD(b + 1)
            for m in range(KT):
                p = pg[ig % 4]; ig += 1
                for k in range(FT):
                    mm(p, p[:], wds[m // 2], wds[m // 2][:, k, (m % 2) * 128:(m % 2 + 1) * 128], GU, GU[:, k, :], k == 0, k == FT - 1)
                stt(xb, xb[:, m, :], p, p[:], G2[:, m, j:j + 1], xb, xb[:, m, :], ALU.mult, ALU.add, extra=[G2])
            rstd = norm_block(ntf[b % 2], xb, BN, None, None, j, None, pssf[b % 2])
            for m in range(KT):
                stt(xb, xb[:, m, :], xb, xb[:, m, :], gft[:, m:m + 1], rstd, rstd[:], ALU.mult, ALU.mult, extra=[gft])
            fw.dma(yo_d[:, :, t0:t0 + BN], xb[:], outs=[], ins=[xb])
        fw.barrier()
    fw.barrier()
    es.close()
    return nc


def _fm(x2d):
    T = x2d.shape[0]
    return np.ascontiguousarray(x2d.T.reshape(-1, 128, T).transpose(1, 0, 2))


def _unfm(a):
    return np.ascontiguousarray(a.transpose(1, 0, 2).reshape(-1, a.shape[2]).T)


def _wl(w):
    K, N = w.shape
    return np.ascontiguousarray(w.reshape(K // 128, 128, N).transpose(1, 0, 2))


def _vl(v):
    return np.ascontiguousarray(v.reshape(-1, 128).T)


_NC_CACHE = {}


def make_in_maps(x_prompt, x_sample, state_s5, c, c_ctx, norm1_g, norm2_g, w_ada, b_ada, w_in,
                 s5_lambda_re, s5_lambda_im, s5_log_step, s5_b_re, s5_b_im, s5_c_re, s5_c_im,
                 s5_d, w_glu, b_glu, w_proj_s5, w_proj_fft, w_out, w_ffn_gate, w_ffn_up,
                 w_ffn_down, final_norm_g):
    f32 = np.float32
    bf = ml_dtypes.bfloat16
    A = lambda a: np.asarray(a, dtype=f32)
    x_prompt, x_sample, state_s5, c, c_ctx = A(x_prompt), A(x_sample), A(state_s5), A(c), A(c_ctx)
    lam_re, lam_im, lstep = A(s5_lambda_re)[0], A(s5_lambda_im)[0], A(s5_log_step)[0]
    b_re, b_im, c_re, c_im = A(s5_b_re)[0], A(s5_b_im)[0], A(s5_c_re)[0], A(s5_c_im)[0]
    common = {
        "w_ada": _wl(A(w_ada)[0]), "b_ada": _vl(A(b_ada)[0]),
        "g1": _vl(A(norm1_g)[0]), "g2": _vl(A(norm2_g)[0]), "gf": _vl(A(final_norm_g)),
        "w_in": _wl(A(w_in)[0]), "w_glu": _wl(A(w_glu)[0]), "b_glu": _vl(A(b_glu)[0]),
        "s5d": _vl(A(s5_d)[0]), "w_p5": _wl(A(w_proj_s5)[0]), "w_pf": _wl(A(w_proj_fft)[0]),
        "w_out": _wl(A(w_out)[0]), "w_g": _wl(A(w_ffn_gate)[0]), "w_u": _wl(A(w_ffn_up)[0]),
        "w_d": _wl(A(w_ffn_down)[0]),
    }
    lrs = np.zeros((128, 32), f32); lis = np.zeros((128, 32), f32); lss = np.zeros((128, 32), f32)
    lrp = np.zeros((128, 2, 16, 2, 64), f32); lip = np.zeros_like(lrp); lsp = np.zeros_like(lrp)
    brp = np.zeros_like(lrp); bip = np.zeros_like(lrp)
    crp = np.zeros((128, 2, 16, 128), f32); cip = np.zeros_like(crp)
    for d in range(2):
        for pr in range(16):
            for g2 in range(2):
                g = 2 * pr + g2
                lrs[g2 * 64:(g2 + 1) * 64, d * 16 + pr] = lam_re[d, g]
                lis[g2 * 64:(g2 + 1) * 64, d * 16 + pr] = lam_im[d, g]
                lss[g2 * 64:(g2 + 1) * 64, d * 16 + pr] = lstep[d, g]
                lrp[:, d, pr, g2, :] = lam_re[d, g][None, :]
                lip[:, d, pr, g2, :] = lam_im[d, g][None, :]
                lsp[:, d, pr, g2, :] = lstep[d, g]
                gi = g % 8
                brp[gi * 16:(gi + 1) * 16, d, pr, g2, :] = b_re[d, g].T
                bip[gi * 16:(gi + 1) * 16, d, pr, g2, :] = b_im[d, g].T
                crp[g2 * 64:(g2 + 1) * 64, d, pr, gi * 16:(gi + 1) * 16] = c_re[d, g].T
                cip[g2 * 64:(g2 + 1) * 64, d, pr, gi * 16:(gi + 1) * 16] = c_im[d, g].T
    lrq = np.zeros((128, 128), f32); liq = np.zeros((128, 128), f32); lsq = np.zeros((128, 128), f32)
    for d in range(2):
        for pr in range(16):
            q = d * 16 + pr
            for g2 in range(2):
                g = 2 * pr + g2
                lrq[q, g2 * 64:(g2 + 1) * 64] = lam_re[d, g]
                liq[q, g2 * 64:(g2 + 1) * 64] = lam_im[d, g]
                lsq[q, g2 * 64:(g2 + 1) * 64] = lstep[d, g]
    lrq[32:] = lrq[0]; liq[32:] = liq[0]; lsq[32:] = lsq[0]
    common.update({"lrs": lrs, "lis": lis, "lss": lss, "lrq": lrq, "liq": liq, "lsq": lsq,
                   "brp": brp.reshape(128, 4096), "bip": bip.reshape(128, 4096),
                   "crp": crp.reshape(128, 4096), "cip": cip.reshape(128, 4096),
                   "h0s": np.zeros((128, 64), f32)})
    ch = np.arange(128, dtype=np.int64)
    angc = (2 * np.pi / 128) * ((ch[:, None] * ch[None, :]) % 128).astype(np.float64)
    for nm, L in (("l", LL), ("s", LS)):
        sc = 1.0 / math.sqrt(L * 128.0)
        common["cc" + nm] = (np.cos(angc) * sc).astype(f32).astype(bf)
        common["nsc" + nm] = (-np.sin(angc) * sc).astype(f32).astype(bf)

    def lay(m, L_l, L_k):
        return np.ascontiguousarray(m.reshape(L_l // 128, 128, L_k // NB, NB).transpose(2, 1, 0, 3)).astype(bf)

    l = np.arange(LS, dtype=np.int64)
    ang = (2 * np.pi / LS) * ((l[:, None] * l[None, :]) % LS).astype(np.float64)
    common["cms"] = lay(np.cos(ang).astype(f32), LS, LS); common["sms"] = lay(np.sin(ang).astype(f32), LS, LS)
    common["ones"] = np.ones((128, 128), f32).astype(bf)
    dft_own = {}
    for j in range(4):
        s = j * LO
        lord = np.concatenate([np.arange(0, s), np.arange(s + LO, LL), np.arange(s, s + LO)]).astype(np.int64)
        k = np.arange(s, s + LO, dtype=np.int64)
        ang = (2 * np.pi / LL) * ((lord[:, None] * k[None, :]) % LL).astype(np.float64)
        dft_own[j] = (lay(np.cos(ang).astype(f32), LL, LO), lay(np.sin(ang).astype(f32), LL, LO))
    nfb = NFB
    common["rampb"] = np.tile((TL * np.arange(NK + 1, dtype=np.float32))[None, :], (128, 1))
    common["rampf"] = np.tile((TL * (NK - 1 - np.arange(NK, dtype=np.float32)))[None, :], (128, 1))
    in_maps = []
    for i in range(8):
        b = i // 4; j = i % 4; s = j * LO
        m = dict(common)
        xs_ = x_sample[b]
        m["xf"] = _fm(np.concatenate([xs_[:s], xs_[s + LO:]], axis=0))
        m["xo"] = _fm(np.concatenate([xs_[s:s + LO], x_prompt[2 * i], x_prompt[2 * i + 1]], axis=0))
        cT = np.stack([c_ctx, c[b]], axis=1)
        m["cT"] = np.ascontiguousarray(cT.reshape(KT, 128, 2).transpose(1, 0, 2))
        h0 = np.zeros((128, 2, 16, 2), f32)
        for d in range(2):
            for ri in range(2):
                for pr in range(16):
                    for g2 in range(2):
                        h0[g2 * 64:(g2 + 1) * 64, d, pr, ri] = state_s5[b, 0, d, ri, 2 * pr + g2]
        m["h0l"] = h0.reshape(128, 64)
        capf = np.zeros((128, nfb + 1), f32); capb = np.zeros((128, nfb + 1), f32)
        if j == 0:
            capf[:, 0] = 1.0
        else:
            capf[:, 1 + (s // FB - 1)] = 1.0
        if j == 3:
            capb[:, 0] = 1.0
        else:
            capb[:, 1 + s // FB] = 1.0
        m["capf"] = capf; m["capb"] = capb
        m["cml"], m["sml"] = dft_own[j]
        in_maps.append(m)
    return in_maps


def kernel(**inputs):
    f32 = np.float32
    in_maps = make_in_maps(**inputs)
    if "nc" not in _NC_CACHE:
        _NC_CACHE["nc"] = build()
    nc = _NC_CACHE["nc"]
    res = run_bass_kernel_spmd(nc, in_maps, core_ids=list(range(8)))
    R = res.results
    y_sample = np.zeros((2, LL, D), f32)
    y_prompt = np.zeros((16, LS, D), f32)
    new_state = np.zeros((16, 1, 2, 2, 32, 64), f32)
    for i in range(8):
        b = i // 4; j = i % 4; s = j * LO
        yo = _unfm(np.asarray(R[i]["yo"], dtype=f32))
        y_sample[b, s:s + LO] = yo[:LO]
        for k in range(2):
            n = 2 * i + k
            y_prompt[n] = yo[LO + k * LS:LO + (k + 1) * LS]
            st = np.asarray(R[i]["st"][k], dtype=f32).reshape(128, 2, 16, 2)
            for d in range(2):
                for ri in range(2):
                    for pr in range(16):
                        for g2 in range(2):
                            new_state[n, 0, d, ri, 2 * pr + g2] = st[g2 * 64:(g2 + 1) * 64, d, pr, ri]
    return (y_prompt, y_sample, new_state)
```

```python
import math
from contextlib import ExitStack
import numpy as np
import ml_dtypes
import concourse.bass as bass
import concourse.mybir as mybir
from concourse.bass_utils import run_bass_kernel_spmd

F32 = mybir.dt.float32
BF16 = mybir.dt.bfloat16
I32 = mybir.dt.int32
AF = mybir.ActivationFunctionType
ALU = mybir.AluOpType

D = 1024
KT = 8
LL = 4096
LS = 256
DFF = 2816
FT = DFF // 128
EPS = 1e-6
PI = math.pi
NB = 256
SB = 64
LF = 3072
LO = 1024
NT = LO + 2 * LS
TL = 4
FB = 128
NFB = LF // FB
NK = FB // TL


class Tl:
    def __init__(self, t, name="", psum=False):
        self.t = t
        self.name = name
        self.psum = psum
        self.w = None
        self.r = []

    def __getitem__(self, idx):
        return self.t[idx]


class _Sub:
    def __init__(self, parent, n, nd):
        self.p = parent
        self.n = n
        self.nd = nd
        self.psum = parent.psum

    @property
    def w(self):
        return self.p.w

    @w.setter
    def w(self, v):
        self.p.w = v

    @property
    def r(self):
        return self.p.r

    @r.setter
    def r(self, v):
        self.p.r = v

    def __getitem__(self, idx):
        if self.nd == 2:
            base = self.p.t[:, 0:self.n]
        else:
            base = self.p.t[:, :, 0:self.n]
        return base[idx]


class FW:
    def __init__(self, nc, es, ndma=32):
        self.nc = nc
        self.engs = {"pe": nc.tensor, "act": nc.scalar, "dve": nc.vector, "pool": nc.gpsimd, "sp": nc.sync}
        self.sems = {}
        self.cnt = {}
        self.waited = {}
        for k in self.engs:
            self.sems[k] = es.enter_context(nc.semaphore("s_" + k))
            self.cnt[k] = 0
        self.dsem = []
        for i in range(ndma):
            k = "d%d" % i
            self.sems[k] = es.enter_context(nc.semaphore(k))
            self.cnt[k] = 0
            self.dsem.append(k)
        self.di = 0
        self.asem = []
        for i in range(4):
            k = "a%d" % i
            self.sems[k] = es.enter_context(nc.semaphore(k))
            self.cnt[k] = 0
            self.asem.append(k)
        self.ai = 0
        self.gsem = []
        for i in range(8):
            k = "g%d" % i
            self.sems[k] = es.enter_context(nc.semaphore(k))
            self.cnt[k] = 0
            self.gsem.append(k)
        self.gi = 0
        self.pending = {k: 0 for k in self.engs}
        self.nosync = {"pe", "pool", "act"}

    def _need(self, e, deps):
        mx = {}
        for d in deps:
            if d is None:
                continue
            k, c = d
            if e == k and e in self.nosync:
                continue
            if c > mx.get(k, 0):
                mx[k] = c
        for k, c in mx.items():
            if self.waited.get((e, k), 0) >= c:
                continue
            self.engs[e].wait_ge(self.sems[k], c)
            self.waited[(e, k)] = c

    def op(self, e, fn, outs=(), ins=(), inc=True):
        deps = []
        outs = list(outs) + [t for t in ins if t.psum and t not in outs]
        for t in ins:
            deps.append(t.w)
        for t in outs:
            deps.append(t.w)
            deps.extend(t.r)
        self._need(e, deps)
        inst = fn(self.engs[e])
        tag = (e, self.cnt[e] + 1)
        if inc:
            self.cnt[e] += 1
            inst.then_inc(self.sems[e], 1)
        for t in outs:
            t.w = tag
            t.r = []
        for t in ins:
            t.r.append(tag)
        return inst

    def dma(self, out_ap, in_ap, outs=(), ins=(), e="sp", detached=False):
        if detached:
            k = self.asem[self.ai % len(self.asem)]
            self.ai += 1
        elif e == "pool":
            k = self.gsem[self.gi % len(self.gsem)]
            self.gi += 1
        else:
            k = self.dsem[self.di % len(self.dsem)]
            self.di += 1
        deps = [(k, self.cnt[k])] if self.cnt[k] else []
        for t in ins:
            deps.append(t.w)
        for t in outs:
            deps.append(t.w)
            deps.extend(t.r)
        self._need(e, deps)
        inst = self.engs[e].dma_start(out=out_ap, in_=in_ap)
        self.cnt[k] += 16
        inst.then_inc(self.sems[k], 16)
        tag = (k, self.cnt[k])
        for t in outs:
            t.w = tag
            t.r = []
        for t in ins:
            t.r.append(tag)

    def barrier(self, skip=()):
        allc = [(k, c) for k, c in self.cnt.items() if c > 0 and k not in skip and not (skip and k[0] == "a")]
        for e in self.engs:
            if e in skip:
                continue
            self._need(e, [d for d in allc if d[0] != e])


def build(stop=None, debug=False):
    nc = bass.Bass("TRN2", target_bir_lowering=False)
    es = ExitStack()
    fw = FW(nc, es)

    def din(name, shape, dt=F32):
        return nc.dram_tensor(name, shape, dt, kind="ExternalInput").ap()

    def dout(name, shape, dt=F32):
        return nc.dram_tensor(name, shape, dt, kind="ExternalOutput").ap()

    def dscr(name, shape, dt):
        return nc.dram_tensor(name, shape, dt, kind="Internal").ap()

    xf_d = din("xf", [128, KT, LF])
    xo_d = din("xo", [128, KT, NT])
    cT_d = din("cT", [128, KT, 2])
    wada_d = din("w_ada", [128, KT, 6 * D])
    bada_d = din("b_ada", [128, 48])
    g1_d = din("g1", [128, KT]); g2_d = din("g2", [128, KT]); gf_d = din("gf", [128, KT])
    win_d = din("w_in", [128, KT, 3 * D])
    wglu_d = din("w_glu", [128, 4, 512]); bglu_d = din("b_glu", [128, 4]); s5d_d = din("s5d", [128, 4])
    wp5_d = din("w_p5", [128, 4, D]); wpf_d = din("w_pf", [128, 4, D]); wout_d = din("w_out", [128, KT, D])
    wg_d = din("w_g", [128, KT, DFF]); wu_d = din("w_u", [128, KT, DFF]); wd_d = din("w_d", [128, FT, D])
    lrs_d = din("lrs", [128, 32]); lis_d = din("lis", [128, 32]); lss_d = din("lss", [128, 32])
    lrq_d = din("lrq", [128, 128]); liq_d = din("liq", [128, 128]); lsq_d = din("lsq", [128, 128])
    brp_d = din("brp", [128, 4096]); bip_d = din("bip", [128, 4096])
    crp_d = din("crp", [128, 4096]); cip_d = din("cip", [128, 4096])
    h0l_d = din("h0l", [128, 64]); h0s_d = din("h0s", [128, 64])
    rampb_d = din("rampb", [128, NK + 1]); rampf_d = din("rampf", [128, NK])
    capf_d = din("capf", [128, NFB + 1]); capb_d = din("capb", [128, NFB + 1])
    cml_d = din("cml", [LO // NB, 128, LL // 128, NB], BF16); sml_d = din("sml", [LO // NB, 128, LL // 128, NB], BF16)
    cms_d = din("cms", [1, 128, LS // 128, NB], BF16); sms_d = din("sms", [1, 128, LS // 128, NB], BF16)
    ccl_d = din("ccl", [128, 128], BF16); nscl_d = din("nscl", [128, 128], BF16)
    ccs_d = din("ccs", [128, 128], BF16); nscs_d = din("nscs", [128, 128], BF16)
    ones_d = din("ones", [128, 128], BF16)
    yo_d = dout("yo", [128, KT, NT])
    st_d = dout("st", [2, 128, 64])
    ya_d = dscr("ya_s", [128, 4, NT], BF16); yb_d = dscr("yb_s", [128, 4, NT], BF16); yf_d = dscr("yf_s", [128, 4, NT], BF16)
    x1_d = dscr("x1_s", [128, KT, NT], F32)
    bp_d = dscr("bp_s", [128, TL * 32 * 2 * 128], BF16); cpd_d = dscr("cp_s", [128, 32 * 2 * 128], BF16)

    uid = [0]

    def sb(stack, name, shape, dt):
        uid[0] += 1
        name = "%s_%d" % (name, uid[0])
        return Tl(stack.enter_context(nc.sbuf_tensor(name, shape, dt)), name)

    def ps(stack, name, shape, dt=F32):
        uid[0] += 1
        name = "%s_%d" % (name, uid[0])
        return Tl(stack.enter_context(nc.psum_tensor(name, shape, dt)), name, psum=True)

    def tt(e, out, oap, a, aap, b, bap, op):
        fw.op(e, lambda g: g.tensor_tensor(out=oap, in0=aap, in1=bap, op=op), outs=[out], ins=[a, b])

    def tsc(e, out, oap, a, aap, s1, op0, s2=None, op1=None):
        if op1 is None:
            fw.op(e, lambda g: g.tensor_scalar(out=oap, in0=aap, scalar1=s1, scalar2=None, op0=op0),
                  outs=[out], ins=[a])
        else:
            fw.op(e, lambda g: g.tensor_scalar(out=oap, in0=aap, scalar1=s1, scalar2=s2, op0=op0, op1=op1),
                  outs=[out], ins=[a])

    def stt(out, oap, a, aap, scal, b, bap, op0, op1, extra=()):
        fw.op("dve", lambda g: g.scalar_tensor_tensor(out=oap, in0=aap, scalar=scal, in1=bap, op0=op0, op1=op1),
              outs=[out], ins=[a, b] + list(extra))

    def act(out, oap, a, aap, func, bias=None, scale=None, extra=()):
        kw = {}
        if bias is not None:
            kw["bias"] = bias
        if scale is not None:
            kw["scale"] = scale
        fw.op("act", lambda g: g.activation(out=oap, in_=aap, func=func, **kw), outs=[out], ins=[a] + list(extra))

    def cp(e, out, oap, a, aap):
        fw.op(e, lambda g: g.tensor_copy(out=oap, in_=aap), outs=[out], ins=[a])

    def mm(out, oap, l, lap, r, rap, start, stop):
        fw.op("pe", lambda g: g.matmul(oap, lhsT=lap, rhs=rap, start=start, stop=stop),
              outs=[out], ins=[l, r], inc=stop)


    P = es
    ones = sb(P, "ones", [128, 128], BF16)
    fw.dma(ones[:], ones_d[:, :], outs=[ones])
    A1 = sb(P, "A1", [128, KT, 2], F32); B1 = sb(P, "B1", [128, KT, 2], F32); G1 = sb(P, "G1", [128, KT, 2], F32)
    A2 = sb(P, "A2", [128, KT, 2], F32); B2 = sb(P, "B2", [128, KT, 2], F32); G2 = sb(P, "G2", [128, KT, 2], F32)
    gft = sb(P, "gft", [128, KT], F32)
    epst = sb(P, "epst", [128, 1], F32)
    fw.op("pool", lambda g: g.memset(epst[:], EPS), outs=[epst])
    fw.dma(gft[:], gf_d[:, :], outs=[gft])
    bglu = sb(P, "bglu", [128, 4], F32); s5d = sb(P, "s5dt", [128, 4], F32)
    fw.dma(bglu[:], bglu_d[:, :], outs=[bglu]); fw.dma(s5d[:], s5d_d[:, :], outs=[s5d])
    capF = sb(P, "capF", [128, NFB + 1], F32); capB = sb(P, "capB", [128, NFB + 1], F32)
    fw.dma(capF[:], capf_d[:, :], outs=[capF]); fw.dma(capB[:], capb_d[:, :], outs=[capB])
    PWR = sb(P, "PWR", [128, 32, 5], F32); PWI = sb(P, "PWI", [128, 32, 5], F32); NPWI = sb(P, "NPWI", [128, 32, 5], F32)
    PWRr = sb(P, "PWRr", [128, 32, 4], F32); PWIr = sb(P, "PWIr", [128, 32, 4], F32); NPWIr = sb(P, "NPWIr", [128, 32, 4], F32)
    PIS4 = sb(P, "PIS4", [128, 32, 2], F32)
    PISf = sb(P, "PISf", [128, 32, 2, 4], F32)
    PISr = sb(P, "PISr", [128, 32, 2, 4], F32)
    ACC = [sb(P, "ACC%d" % d, [128, 16, 2], F32) for d in range(2)]
    ZST = [sb(P, "ZST%d" % d, [128, 16, 2], F32) for d in range(2)]
    H0L = [sb(P, "H0L%d" % d, [128, 16, 2], F32) for d in range(2)]
    for d in range(2):
        fw.dma(H0L[d][:], h0l_d[:, d * 32:(d + 1) * 32].rearrange("p (a b) -> p a b", b=2), outs=[H0L[d]])
        fw.dma(ZST[d][:], h0s_d[:, d * 32:(d + 1) * 32].rearrange("p (a b) -> p a b", b=2), outs=[ZST[d]])

    def discretize(S, n, LR, LI, LSt, pre):
        t = {}
        for nm in ["step", "dr", "di", "mag", "q", "r", "m", "s", "c", "are", "aim", "nr", "den", "fr", "fi", "tmp"]:
            t[nm] = sb(S, pre + nm, [128, n], F32)
        qi = sb(S, pre + "qi", [128, n], I32)
        act(t["step"], t["step"][:], LSt, LSt[:], AF.Exp)
        tt("dve", t["dr"], t["dr"][:], LR, LR[:], t["step"], t["step"][:], ALU.mult)
        tt("dve", t["di"], t["di"][:], LI, LI[:], t["step"], t["step"][:], ALU.mult)
        act(t["mag"], t["mag"][:], t["dr"], t["dr"][:], AF.Exp)
        tsc("dve", t["q"], t["q"][:], t["di"], t["di"][:], 1.0 / (2 * PI), ALU.mult)
        cp("dve", qi, qi[:], t["q"], t["q"][:])
        cp("dve", t["q"], t["q"][:], qi, qi[:])
        stt(t["r"], t["r"][:], t["q"], t["q"][:], -2 * PI, t["di"], t["di"][:], ALU.mult, ALU.add)
        tsc("dve", t["m"], t["m"][:], t["r"], t["r"][:], PI, ALU.is_gt)
        stt(t["r"], t["r"][:], t["m"], t["m"][:], -2 * PI, t["r"], t["r"][:], ALU.mult, ALU.add)
        tsc("dve", t["m"], t["m"][:], t["r"], t["r"][:], -PI, ALU.is_lt)
        stt(t["r"], t["r"][:], t["m"], t["m"][:], 2 * PI, t["r"], t["r"][:], ALU.mult, ALU.add)
        act(t["s"], t["s"][:], t["r"], t["r"][:], AF.Sin)
        tsc("dve", t["q"], t["q"][:], t["r"], t["r"][:], PI / 2, ALU.add)
        tsc("dve", t["m"], t["m"][:], t["q"], t["q"][:], PI, ALU.is_gt)
        stt(t["q"], t["q"][:], t["m"], t["m"][:], -2 * PI, t["q"], t["q"][:], ALU.mult, ALU.add)
        act(t["c"], t["c"][:], t["q"], t["q"][:], AF.Sin)
        tt("dve", t["are"], t["are"][:], t["mag"], t["mag"][:], t["c"], t["c"][:], ALU.mult)
        tt("dve", t["aim"], t["aim"][:], t["mag"], t["mag"][:], t["s"], t["s"][:], ALU.mult)
        tsc("dve", t["nr"], t["nr"][:], t["are"], t["are"][:], -1.0, ALU.add)
        tt("dve", t["den"], t["den"][:], LR, LR[:], LR, LR[:], ALU.mult)
        tt("dve", t["tmp"], t["tmp"][:], LI, LI[:], LI, LI[:], ALU.mult)
        tt("dve", t["den"], t["den"][:], t["den"], t["den"][:], t["tmp"], t["tmp"][:], ALU.add)
        fw.op("dve", lambda g: g.reciprocal(out=t["den"][:], in_=t["den"][:]), outs=[t["den"]], ins=[t["den"]])
        tt("dve", t["fr"], t["fr"][:], t["nr"], t["nr"][:], LR, LR[:], ALU.mult)
        tt("dve", t["tmp"], t["tmp"][:], t["aim"], t["aim"][:], LI, LI[:], ALU.mult)
        tt("dve", t["fr"], t["fr"][:], t["fr"], t["fr"][:], t["tmp"], t["tmp"][:], ALU.add)
        tt("dve", t["fr"], t["fr"][:], t["fr"], t["fr"][:], t["den"], t["den"][:], ALU.mult)
        tt("dve", t["fi"], t["fi"][:], t["aim"], t["aim"][:], LR, LR[:], ALU.mult)
        tt("dve", t["tmp"], t["tmp"][:], t["nr"], t["nr"][:], LI, LI[:], ALU.mult)
        tt("dve", t["fi"], t["fi"][:], t["fi"], t["fi"][:], t["tmp"], t["tmp"][:], ALU.subtract)
        tt("dve", t["fi"], t["fi"][:], t["fi"], t["fi"][:], t["den"], t["den"][:], ALU.mult)
        return t["are"], t["aim"], t["fr"], t["fi"]


    def cmul(S, n, ar, ai, br, bi, pre):
        orr = sb(S, pre + "re", [128, n], F32); oi = sb(S, pre + "im", [128, n], F32); t_ = sb(S, pre + "t", [128, n], F32)
        tt("dve", orr, orr[:], ar, ar[:], br, br[:], ALU.mult)
        tt("dve", t_, t_[:], ai, ai[:], bi, bi[:], ALU.mult)
        tt("dve", orr, orr[:], orr, orr[:], t_, t_[:], ALU.subtract)
        tt("dve", oi, oi[:], ar, ar[:], bi, bi[:], ALU.mult)
        tt("dve", t_, t_[:], ai, ai[:], br, br[:], ALU.mult)
        tt("dve", oi, oi[:], oi, oi[:], t_, t_[:], ALU.add)
        return orr, oi

    def expi(S, shape, dr, drap, di, diap, pre):
        t = {}
        for nm in ["mag", "q", "r", "m", "s", "c"]:
            t[nm] = sb(S, pre + nm, shape, F32)
        qi = sb(S, pre + "qi", shape, I32)
        A_ = lambda T_: T_[:]
        act(t["mag"], A_(t["mag"]), dr, drap, AF.Exp)
        tsc("dve", t["q"], A_(t["q"]), di, diap, 1.0 / (2 * PI), ALU.mult)
        cp("dve", qi, qi[:], t["q"], A_(t["q"]))
        cp("dve", t["q"], A_(t["q"]), qi, qi[:])
        stt(t["r"], A_(t["r"]), t["q"], A_(t["q"]), -2 * PI, di, diap, ALU.mult, ALU.add)
        tsc("dve", t["m"], A_(t["m"]), t["r"], A_(t["r"]), PI, ALU.is_gt)
        stt(t["r"], A_(t["r"]), t["m"], A_(t["m"]), -2 * PI, t["r"], A_(t["r"]), ALU.mult, ALU.add)
        tsc("dve", t["m"], A_(t["m"]), t["r"], A_(t["r"]), -PI, ALU.is_lt)
        stt(t["r"], A_(t["r"]), t["m"], A_(t["m"]), 2 * PI, t["r"], A_(t["r"]), ALU.mult, ALU.add)
        act(t["s"], A_(t["s"]), t["r"], A_(t["r"]), AF.Sin)
        tsc("dve", t["q"], A_(t["q"]), t["r"], A_(t["r"]), PI / 2, ALU.add)
        tsc("dve", t["m"], A_(t["m"]), t["q"], A_(t["q"]), PI, ALU.is_gt)
        stt(t["q"], A_(t["q"]), t["m"], A_(t["m"]), -2 * PI, t["q"], A_(t["q"]), ALU.mult, ALU.add)
        act(t["c"], A_(t["c"]), t["q"], A_(t["q"]), AF.Sin)
        tt("dve", t["c"], A_(t["c"]), t["mag"], A_(t["mag"]), t["c"], A_(t["c"]), ALU.mult)
        tt("dve", t["s"], A_(t["s"]), t["mag"], A_(t["mag"]), t["s"], A_(t["s"]), ALU.mult)
        return t["c"], t["s"]

    with ExitStack() as SA:
        S = SA
        cTf = sb(S, "cTf", [128, KT, 2], F32)
        fw.dma(cTf[:], cT_d[:, :, :], outs=[cTf])
        cT = sb(S, "cT", [128, KT, 2], BF16)
        act(cT, cT[:], cTf, cTf[:], AF.Silu)
        pm = ps(S, "pm", [128, 48, 2])
        was = [sb(S, "wa%d" % i, [128, KT, 512], BF16) for i in range(4)]
        for j in range(12):
            wa = was[j % 4]
            fw.dma(wa[:], wada_d[:, :, j * 512:(j + 1) * 512], outs=[wa], e="pool", detached=True)
            for ft in range(4):
                f = j * 4 + ft
                for kt in range(KT):
                    mm(pm, pm[:, f, :], wa, wa[:, kt, ft * 128:(ft + 1) * 128], cT, cT[:, kt, :], kt == 0, kt == KT - 1)
        with ExitStack() as S:
            lrs = sb(S, "lrs", [128, 32], F32); lis = sb(S, "lis", [128, 32], F32); lss = sb(S, "lss", [128, 32], F32)
            fw.dma(lrs[:], lrs_d[:, :], outs=[lrs]); fw.dma(lis[:], lis_d[:, :], outs=[lis]); fw.dma(lss[:], lss_d[:, :], outs=[lss])
            are, aim, _, _ = discretize(S, 32, lrs, lis, lss, "ds_")
            fw.op("dve", lambda g: g.memset(PWR[:, :, 0:1], 1.0), outs=[PWR])
            fw.op("dve", lambda g: g.memset(PWI[:, :, 0:1], 0.0), outs=[PWI])
            cp("dve", PWR, PWR[:, :, 1], are, are[:]); cp("dve", PWI, PWI[:, :, 1], aim, aim[:])
            t1 = sb(S, "pt1", [128, 32], F32); t2 = sb(S, "pt2", [128, 32], F32)
            for n in range(2, 5):
                tt("dve", t1, t1[:], PWR, PWR[:, :, n - 1], are, are[:], ALU.mult)
                tt("dve", t2, t2[:], PWI, PWI[:, :, n - 1], aim, aim[:], ALU.mult)
                tt("dve", PWR, PWR[:, :, n], t1, t1[:], t2, t2[:], ALU.subtract)
                tt("dve", t1, t1[:], PWR, PWR[:, :, n - 1], aim, aim[:], ALU.mult)
                tt("dve", t2, t2[:], PWI, PWI[:, :, n - 1], are, are[:], ALU.mult)
                tt("dve", PWI, PWI[:, :, n], t1, t1[:], t2, t2[:], ALU.add)
            tsc("dve", NPWI, NPWI[:], PWI, PWI[:], -1.0, ALU.mult)
            cp("dve", PIS4, PIS4[:, :, 0], NPWI, NPWI[:, :, 4]); cp("dve", PIS4, PIS4[:, :, 1], PWI, PWI[:, :, 4])
            cp("dve", PISf, PISf[:, :, 0, :], NPWI, NPWI[:, :, 1:5]); cp("dve", PISf, PISf[:, :, 1, :], PWI, PWI[:, :, 1:5])
            for i in range(4):
                cp("dve", PWRr, PWRr[:, :, i], PWR, PWR[:, :, 4 - i])
                cp("dve", PWIr, PWIr[:, :, i], PWI, PWI[:, :, 4 - i])
                cp("dve", NPWIr, NPWIr[:, :, i], NPWI, NPWI[:, :, 4 - i])
                cp("dve", PISr, PISr[:, :, 0, i], NPWI, NPWI[:, :, 4 - i]); cp("dve", PISr, PISr[:, :, 1, i], PWI, PWI[:, :, 4 - i])
            fw.barrier(skip=("pe", "pool"))
        with ExitStack() as S:
            Bp4 = sb(S, "Bp4", [128, TL, 32, 2, 128], BF16)
            Cpd = sb(S, "Cpd", [128, 32, 2, 128], BF16)
            gt_d = dscr("gt_s", [2 * TL, 32 * 128], F32)
            with ExitStack() as S2:
                lrq = sb(S2, "lrq", [128, 128], F32); liq = sb(S2, "liq", [128, 128], F32); lsq = sb(S2, "lsq", [128, 128], F32)
                fw.dma(lrq[:], lrq_d[:, :], outs=[lrq]); fw.dma(liq[:], liq_d[:, :], outs=[liq]); fw.dma(lsq[:], lsq_d[:, :], outs=[lsq])
                are_q, aim_q, gr, gi_ = discretize(S2, 128, lrq, liq, lsq, "dq_")
                for j in range(TL):
                    for ri, T_ in ((0, gr), (1, gi_)):
                        fw.dma(gt_d[2 * j + ri:2 * j + ri + 1, :].rearrange("o (q c) -> (o q) c", c=128), T_[0:32, :], outs=[], ins=[T_])
                    if j < TL - 1:
                        gr, gi_ = cmul(S2, 128, gr, gi_, are_q, aim_q, "gq%d_" % j)
                fw.barrier(skip=("pe", "pool"))
            v3 = lambda T_: T_[:].rearrange("p (q c) -> p q c", c=128)
            QW = 1024
            with ExitStack() as SBp:
                sets = [{k_: sb(SBp, "%s%d" % (k_, i_), [128, QW], F32) for k_ in ("gbr", "gbi", "wr", "wi", "t")} for i_ in range(2)]
                brs = [sb(SBp, "brq%d" % i_, [128, QW], F32) for i_ in range(2)]
                bis = [sb(SBp, "biq%d" % i_, [128, QW], F32) for i_ in range(2)]
                it_ = 0
                for qt in range(4096 // QW):
                    c0 = qt * QW
                    br_ = brs[qt % 2]; bi_ = bis[qt % 2]
                    fw.dma(br_[:], brp_d[:, c0:c0 + QW], outs=[br_]); fw.dma(bi_[:], bip_d[:, c0:c0 + QW], outs=[bi_])
                    nq = QW // 128
                    for j in range(TL):
                        T_ = sets[it_ % 2]; it_ += 1
                        gbr, gbi, wr, wi, t_ = T_["gbr"], T_["gbi"], T_["wr"], T_["wi"], T_["t"]
                        fw.dma(gbr[:], gt_d[2 * j:2 * j + 1, c0:c0 + QW].broadcast_to([128, QW]), outs=[gbr])
                        fw.dma(gbi[:], gt_d[2 * j + 1:2 * j + 2, c0:c0 + QW].broadcast_to([128, QW]), outs=[gbi])
                        tt("dve", wr, wr[:], gbr, gbr[:], br_, br_[:], ALU.mult)
                        tt("dve", t_, t_[:], gbi, gbi[:], bi_, bi_[:], ALU.mult)
                        tt("dve", wr, wr[:], wr, wr[:], t_, t_[:], ALU.subtract)
                        tt("dve", wi, wi[:], gbr, gbr[:], bi_, bi_[:], ALU.mult)
                        tt("dve", t_, t_[:], gbi, gbi[:], br_, br_[:], ALU.mult)
                        tt("dve", wi, wi[:], wi, wi[:], t_, t_[:], ALU.add)
                        act(Bp4, Bp4[:, j, nq * qt:nq * qt + nq, 0, :], wr, v3(wr), AF.Copy)
                        act(Bp4, Bp4[:, j, nq * qt:nq * qt + nq, 1, :], wi, v3(wi), AF.Copy)
                fw.barrier(skip=("pe", "pool"))
            with ExitStack() as S2:
                cr = sb(S2, "crt", [128, 4096], F32); ci = sb(S2, "cit", [128, 4096], F32)
                fw.dma(cr[:], crp_d[:, :], outs=[cr]); fw.dma(ci[:], cip_d[:, :], outs=[ci])
                v3 = lambda T_: T_[:].rearrange("p (q c) -> p q c", c=128)
                cp("dve", Cpd, Cpd[:, :, 0, :], cr, v3(cr))
                tsc("dve", Cpd, Cpd[:, :, 1, :], ci, v3(ci), -1.0, ALU.mult)
                fw.barrier(skip=("pe", "pool"))
            fw.dma(bp_d[:, :], Bp4[:].rearrange("p a b c d -> p (a b c d)"), outs=[], ins=[Bp4])
            fw.dma(cpd_d[:, :], Cpd[:].rearrange("p b c d -> p (b c d)"), outs=[], ins=[Cpd])
            if debug:
                for nm, T_, shp, dt_ in (("A1", A1, [128, KT, 2], F32), ("PWR", PWR, [128, 32, 5], F32), ("PWI", PWI, [128, 32, 5], F32),
                                         ("Bp4", Bp4, [128, TL, 32, 2, 128], BF16), ("Cpd", Cpd, [128, 32, 2, 128], BF16)):
                    dd = dout("dbg_" + nm, shp, dt_)
                    fw.dma(dd, T_[:], outs=[], ins=[T_])
            fw.barrier(skip=("pe", "pool"))
        S = SA
        bada = sb(S, "bada", [128, 48], F32)
        fw.dma(bada[:], bada_d[:, :], outs=[bada])
        MOD = sb(S, "MOD", [128, 48, 2], F32)
        tt("dve", MOD, MOD[:], pm, pm[:], bada, bada[:].unsqueeze(2).broadcast_to([128, 48, 2]), ALU.add)
        g1t = sb(S, "g1t", [128, KT], F32); g2t = sb(S, "g2t", [128, KT], F32)
        fw.dma(g1t[:], g1_d[:, :], outs=[g1t]); fw.dma(g2t[:], g2_d[:, :], outs=[g2t])
        for (A, Bt, G, gt, base) in ((A1, B1, G1, g1t, 0), (A2, B2, G2, g2t, 24)):
            cp("dve", Bt, Bt[:], MOD, MOD[:, base:base + 8, :])
            cp("dve", G, G[:], MOD, MOD[:, base + 16:base + 24, :])
            tsc("dve", A, A[:], MOD, MOD[:, base + 8:base + 16, :], 1.0, ALU.add)
            tt("dve", A, A[:], A, A[:], gt, gt[:].unsqueeze(2).broadcast_to([128, KT, 2]), ALU.mult)
        fw.barrier()
    if stop == "setup":
        es.close()
        return nc

    def load_w(S, name, dram_ap, kt, cols, stage, w=None):
        if w is None:
            w = sb(S, name, [128, kt, cols], BF16)
        for k in range(kt):
            fw.dma(w[:, k, :], dram_ap[:, k, 0:cols], outs=[w], e="pool")
        return w

    def norm_tmps(S, pfx, n, nbuf=2):
        return [{"sq": sb(S, pfx + "sq%d" % i, [128, KT, n], BF16), "rstd": sb(S, pfx + "rstd%d" % i, [128, n], F32),
                 "tmp": sb(S, pfx + "ntmp%d" % i, [128, KT, n], F32)} for i in range(nbuf)]

    def norm_block(T_, xb, n, A, Bt, j, hT, pss):
        sq = T_["sq"]; rstd = T_["rstd"]; tmp = T_["tmp"]
        act(sq, sq[:], xb, xb[:, :, 0:n], AF.Square)
        for kt in range(KT):
            mm(pss, pss[:, 0:n], ones, ones[:], sq, sq[:, kt, :], kt == 0, kt == KT - 1)
        act(rstd, rstd[:], pss, pss[:, 0:n], AF.Sqrt, bias=epst[:, 0:1], scale=1.0 / D, extra=[epst])
        fw.op("dve", lambda g: g.reciprocal(out=rstd[:], in_=rstd[:]), outs=[rstd], ins=[rstd])
        if hT is not None:
            for kt in range(KT):
                stt(tmp, tmp[:, kt, :], xb, xb[:, kt, 0:n], A[:, kt, j:j + 1], rstd, rstd[:], ALU.mult, ALU.mult, extra=[A])
            for kt in range(KT):
                act(hT, hT[:, kt, 0:n], tmp, tmp[:, kt, :], AF.Identity, bias=Bt[:, kt, j:j + 1], scale=1.0, extra=[Bt])
        return rstd

    def phase1_bufs(S1):
        return {"pu": [ps(S1, "pu%d" % i, [128, 512]) for i in range(4)],
                "pss": [ps(S1, "pss1_%d" % i, [128, 512]) for i in range(2)],
                "xbs": [sb(S1, "xb1_%d" % i, [128, KT, 512], F32) for i in range(2)],
                "hTs": [sb(S1, "hT1_%d" % i, [128, KT, 512], BF16) for i in range(2)],
                "nt": norm_tmps(S1, "n1_", 512), "ip": 0, "ib": 0}

    def phase1(Bf, w1, x_ap, L, j, US5p, UF, uf_tile0):
        BN = min(512, L)
        pu = Bf["pu"]; pss = Bf["pss"]; xbs = Bf["xbs"]; hTs = Bf["hTs"]; nt = Bf["nt"]
        nb_ = L // BN
        base = Bf["ib"]

        def view(T_, n):
            return {"sq": _Sub(T_["sq"], n, 3), "rstd": _Sub(T_["rstd"], n, 2), "tmp": _Sub(T_["tmp"], n, 3)}

        def front(b):
            xb = xbs[(base + b) % 2]
            fw.dma(xb[:, :, 0:BN], x_ap[:, :, b * BN:(b + 1) * BN], outs=[xb])
            norm_block(view(nt[(base + b) % 2], BN), xb, BN, A1, B1, j, hTs[(base + b) % 2], pss[(base + b) % 2])

        front(0)
        for b in range(nb_):
            t0 = b * BN
            xb = xbs[(base + b) % 2]; hT = hTs[(base + b) % 2]
            if b + 1 < nb_:
                front(b + 1)
            for m in range(4):
                p = pu[Bf["ip"] % 4]; Bf["ip"] += 1
                for kt in range(KT):
                    mm(p, p[:, 0:BN], w1, w1[:, kt, m * 128:(m + 1) * 128], hT, hT[:, kt, 0:BN], kt == 0, kt == KT - 1)
                act(US5p, US5p[:, m, 3 + t0:3 + t0 + BN], p, p[:, 0:BN], AF.Copy)
            for ts_ in range(BN // 128):
                p = pu[Bf["ip"] % 4]; Bf["ip"] += 1
                for kt in range(KT):
                    mm(p, p[:, :], hT, hT[:, kt, ts_ * 128:(ts_ + 1) * 128], w1, w1[:, kt, 512:1024], kt == 0, kt == KT - 1)
                cp("dve", UF, UF[:, uf_tile0 + t0 // 128 + ts_, :], p, p[:, :])
        Bf["ib"] = base + nb_

    def s5_run(S1, segs):
        bp_v = bp_d.rearrange("p (a b c) -> p a b c", a=TL, b=32)
        Bp4d = []
        for d in range(2):
            t_ = sb(S1, "Bp4w%d" % d, [128, TL, 16, 256], BF16)
            fw.dma(t_[:], bp_v[:, :, d * 16:(d + 1) * 16, :], outs=[t_])
            Bp4d.append(t_)
        Cpd = sb(S1, "Cpdw", [128, 32, 2, 128], BF16)
        fw.dma(Cpd[:].rearrange("p b c d -> p (b c d)"), cpd_d[:, :], outs=[Cpd])
        HB = [[sb(S1, "HB%d_%d" % (d, i), [128, 16, 2, SB], BF16) for i in range(2)] for d in range(2)]
        YS = [[sb(S1, "YS%d_%d" % (d, i), [128, 4, SB], BF16) for i in range(2)] for d in range(2)]
        py = [ps(S1, "py%d" % d, [128, 4, SB]) for d in range(2)]
        V = [[sb(S1, "V%d_%d" % (d, i), [128, 16, 2, SB], F32) for i in range(2)] for d in range(2)]
        XE = [sb(S1, "XE%d" % d, [128, 16, 2], F32) for d in range(2)]
        HX = [sb(S1, "HX%d" % d, [128, 16, 2, SB + 4], F32) for d in range(2)]
        M1 = [sb(S1, "M1_%d" % d, [128, 16, 2, 4], F32) for d in range(2)]
        M2 = [sb(S1, "M2_%d" % d, [128, 16, 2, 4], F32) for d in range(2)]
        pbu = [[ps(S1, "pbu%d_%d" % (d, i), [128, 4, 2, SB]) for i in range(2)] for d in range(2)]
        engs = ["dve", "pool"]
        blocks = []
        for sg in segs:
            for bl in range(sg[1] // SB):
                blocks.append((sg, bl))

        def cmuladd(e, d, out, oap, pr, pis, src, sap, add, aap):
            m1 = M1[d]; m2 = M2[d]
            prr = pr.unsqueeze(2).broadcast_to([128, 16, 2, 4])
            tt(e, m1, m1[:], src, sap, PWR, prr, ALU.mult)
            tt(e, m2, m2[:], src, sap[:, :, ::-1, :], PWR, pis, ALU.mult)
            tt(e, m1, m1[:], m1, m1[:], m2, m2[:], ALU.add)
            tt(e, out, oap, m1, m1[:], add, aap, ALU.add)

        def stageA(gi):
            (US5p, L, X0, off, si), bi = blocks[gi]
            nsb = L // SB
            for d in range(2):
                t0 = (bi if d == 0 else nsb - 1 - bi) * SB
                q0 = d * 16
                v = V[d][gi % 2]
                for pr_ in range(16):
                    pb = pbu[d][(pr_ // 4) % 2]
                    for ri in range(2):
                        for j in range(TL):
                            sh = -j if d == 0 else j
                            mm(pb, pb[:, pr_ % 4, ri, :], Bp4d[d], Bp4d[d][:, j, pr_, ri * 128:(ri + 1) * 128], US5p,
                               US5p[:, pr_ // 4, 3 + t0 + sh:3 + t0 + sh + SB], j == 0, j == TL - 1)
                    if pr_ % 4 == 3:
                        act(v, v[:, pr_ - 3:pr_ + 1, :, :], pb, pb[:], AF.Copy)

        def stageB(gi):
            (US5p, L, X0, off, si), bi = blocks[gi]
            nsb = L // SB
            for d in range(2):
                e = engs[d]
                q0 = d * 16
                hx = HX[d]; v = V[d][gi % 2]
                p4 = PWR[:, q0:q0 + 16, 4:5].broadcast_to([128, 16, 4])
                pis4 = PIS4[:, q0:q0 + 16, :].unsqueeze(3).broadcast_to([128, 16, 2, 4])
                ng = SB // 4
                if d == 0:
                    for m in range(ng):
                        if bi == 0 and m == 0:
                            x0b = X0[d][:].unsqueeze(3).broadcast_to([128, 16, 2, 4])
                            cmuladd(e, d, hx, hx[:, :, :, 4:8], PWR[:, q0:q0 + 16, 1:5], PISf[:, q0:q0 + 16, :, :],
                                    X0[d], x0b, v, v[:, :, :, 0:4])
                        else:
                            cmuladd(e, d, hx, hx[:, :, :, 4 + 4 * m:8 + 4 * m], p4, pis4, hx, hx[:, :, :, 4 * m:4 * m + 4], v, v[:, :, :, 4 * m:4 * m + 4])
                    xend = hx[:, :, :, SB + 3]
                    data = hx[:, :, :, 4:4 + SB]
                else:
                    for m in range(ng):
                        lo = SB - 4 - 4 * m
                        if bi == 0 and m == 0:
                            x0b = X0[d][:].unsqueeze(3).broadcast_to([128, 16, 2, 4])
                            cmuladd(e, d, hx, hx[:, :, :, lo:lo + 4], PWRr[:, q0:q0 + 16, :], PISr[:, q0:q0 + 16, :, :],
                                    X0[d], x0b, v, v[:, :, :, lo:lo + 4])
                        else:
                            cmuladd(e, d, hx, hx[:, :, :, lo:lo + 4], p4, pis4, hx, hx[:, :, :, lo + 4:lo + 8], v, v[:, :, :, lo:lo + 4])
                    xend = hx[:, :, :, 0]
                    data = hx[:, :, :, 0:SB]
                hb = HB[d][gi % 2]
                act(hb, hb[:], hx, data, AF.Copy)
                if si is not None and bi == nsb - 1:
                    cp(e, XE[d], XE[d][:], hx, xend)
                    fw.dma(st_d[si, :, d * 32:(d + 1) * 32].rearrange("p (a b) -> p a b", b=2), XE[d][:], outs=[], ins=[XE[d]])
                if bi < nsb - 1:
                    if d == 0:
                        cp(e, hx, hx[:, :, :, 0:4], hx, hx[:, :, :, SB:SB + 4])
                    else:
                        cp(e, hx, hx[:, :, :, SB:SB + 4], hx, hx[:, :, :, 0:4])

        def stageC(gi):
            (US5p, L, X0, off, si), bi = blocks[gi]
            nsb = L // SB
            for d in range(2):
                t0 = (bi if d == 0 else nsb - 1 - bi) * SB
                q0 = d * 16
                hb = HB[d][gi % 2]; ys = YS[d][gi % 2]
                pyd = py[d]
                for c in range(4):
                    i = 0
                    for pr_ in range(4 * c, 4 * c + 4):
                        for ri in range(2):
                            mm(pyd, pyd[:, c, :], Cpd, Cpd[:, q0 + pr_, ri, :], hb, hb[:, pr_, ri, :], i == 0, i == 7)
                            i += 1
                if d == 0:
                    for c in range(4):
                        stt(ys, ys[:, c, :], US5p, US5p[:, c, 3 + t0:3 + t0 + SB], s5d[:, c:c + 1], pyd, pyd[:, c, :], ALU.mult, ALU.add, extra=[s5d])
                    fw.dma(ya_d[:, :, off + t0:off + t0 + SB], ys[:], outs=[], ins=[ys])
                else:
                    act(ys, ys[:], pyd, pyd[:], AF.Copy)
                    fw.dma(yb_d[:, :, off + t0:off + t0 + SB], ys[:], outs=[], ins=[ys])

        n = len(blocks)
        stageA(0); stageB(0)
        for gi in range(1, n):
            stageA(gi); stageB(gi); stageC(gi - 1)
        stageC(n - 1)
        fw.barrier()

    def s5_far(S1, US5p):
        Bp4 = sb(S1, "Bp4f", [128, TL, 32, 2, 128], BF16)
        fw.dma(Bp4[:].rearrange("p a b c d -> p (a b c d)"), bp_d[:, :], outs=[Bp4])
        PTbR = sb(S1, "PTbR", [128, 32, NK + 1], F32); PTbI = sb(S1, "PTbI", [128, 32, NK + 1], F32)
        PTfR = sb(S1, "PTfR", [128, 32, NK], F32); PTfI = sb(S1, "PTfI", [128, 32, NK], F32)
        with ExitStack() as S2:
            lrs = sb(S2, "flrs", [128, 32], F32); lis = sb(S2, "flis", [128, 32], F32); lss = sb(S2, "flss", [128, 32], F32)
            fw.dma(lrs[:], lrs_d[:, :], outs=[lrs]); fw.dma(lis[:], lis_d[:, :], outs=[lis]); fw.dma(lss[:], lss_d[:, :], outs=[lss])
            rb = sb(S2, "rampb", [128, NK + 1], F32); rf = sb(S2, "rampf", [128, NK], F32)
            fw.dma(rb[:], rampb_d[:, :], outs=[rb]); fw.dma(rf[:], rampf_d[:, :], outs=[rf])
            act(lss, lss[:], lss, lss[:], AF.Exp)
            tt("dve", lrs, lrs[:], lrs, lrs[:], lss, lss[:], ALU.mult)
            tt("dve", lis, lis[:], lis, lis[:], lss, lss[:], ALU.mult)
            for (ramp, K, PR, PI_) in ((rb, NK + 1, PTbR, PTbI), (rf, NK, PTfR, PTfI)):
                with ExitStack() as S3:
                    ar_ = sb(S3, "argr", [128, 32, K], F32); ai_ = sb(S3, "argi", [128, 32, K], F32)
                    rbc = ramp[:].unsqueeze(1).broadcast_to([128, 32, K])
                    tt("dve", ar_, ar_[:], lrs, lrs[:].unsqueeze(2).broadcast_to([128, 32, K]), ramp, rbc, ALU.mult)
                    tt("dve", ai_, ai_[:], lis, lis[:].unsqueeze(2).broadcast_to([128, 32, K]), ramp, rbc, ALU.mult)
                    re_, im_ = expi(S3, [128, 32, K], ar_, ar_[:], ai_, ai_[:], "pt_")
                    cp("dve", PR, PR[:], re_, re_[:]); cp("dve", PI_, PI_[:], im_, im_[:])
                    fw.barrier()
            fw.barrier()
        nfb = LF // FB
        V = [sb(S1, "Vf%d" % d, [128, 16, 2, NK], F32) for d in range(2)]
        M1s = [sb(S1, "M1f%d" % d, [128, 16, 2, NK], F32) for d in range(2)]; M2s = [sb(S1, "M2f%d" % d, [128, 16, 2, NK], F32) for d in range(2)]
        Ws = [sb(S1, "Wf%d" % d, [128, 16, 2], F32) for d in range(2)]
        X = [sb(S1, "Xf%d" % d, [128, 16, 2], F32) for d in range(2)]
        T1s = [sb(S1, "T1f%d" % d, [128, 16, 2], F32) for d in range(2)]; T2s = [sb(S1, "T2f%d" % d, [128, 16, 2], F32) for d in range(2)]
        pbu = [[ps(S1, "pbf%d_%d" % (d, i), [128, 4, 2, NK]) for i in range(2)] for d in range(2)]
        for d, cap in ((0, capF), (1, capB)):
            cp("dve", X[d], X[d][:], H0L[d], H0L[d][:])
            fw.op("dve", lambda g: g.tensor_scalar(out=ACC[d][:], in0=H0L[d][:], scalar1=cap[:, 0:1], scalar2=None, op0=ALU.mult),
                  outs=[ACC[d]], ins=[H0L[d], cap])
        for bi in range(nfb):
            for d in range(2):
                blk = bi if d == 0 else nfb - 1 - bi
                t0 = blk * FB
                q0 = d * 16
                v = V[d]
                for pr_ in range(16):
                    pb = pbu[d][(pr_ // 4) % 2]
                    for ri in range(2):
                        for j in range(TL):
                            st_ = 3 + t0 + (TL - 1 - j if d == 0 else j)
                            mm(pb, pb[:, pr_ % 4, ri, :], Bp4, Bp4[:, j, q0 + pr_, ri, :], US5p,
                               US5p[:, pr_ // 4, st_:st_ + FB:TL], j == 0, j == TL - 1)
                    if pr_ % 4 == 3:
                        act(v, v[:, pr_ - 3:pr_ + 1, :, :], pb, pb[:], AF.Copy)
                if d == 0:
                    ptr = PTfR[:, q0:q0 + 16, :]; pti = PTfI[:, q0:q0 + 16, :]
                else:
                    ptr = PTbR[:, q0:q0 + 16, 0:NK]; pti = PTbI[:, q0:q0 + 16, 0:NK]
                e = "dve" if d == 0 else "pool"
                M1 = M1s[d]; M2 = M2s[d]; W = Ws[d]
                tt(e, M1, M1[:, :, 0, :], v, v[:, :, 0, :], PTbR, ptr, ALU.mult)
                tt(e, M2, M2[:, :, 0, :], v, v[:, :, 1, :], PTbR, pti, ALU.mult)
                tt(e, M1, M1[:, :, 0, :], M1, M1[:, :, 0, :], M2, M2[:, :, 0, :], ALU.subtract)
                tt(e, M1, M1[:, :, 1, :], v, v[:, :, 1, :], PTbR, ptr, ALU.mult)
                tt(e, M2, M2[:, :, 1, :], v, v[:, :, 0, :], PTbR, pti, ALU.mult)
                tt(e, M1, M1[:, :, 1, :], M1, M1[:, :, 1, :], M2, M2[:, :, 1, :], ALU.add)
                fw.op("dve", lambda g: g.tensor_reduce(out=W[:], in_=M1[:], axis=mybir.AxisListType.X, op=ALU.add), outs=[W], ins=[M1])
                x = X[d]
                pr128 = PTbR[:, q0:q0 + 16, NK:NK + 1].broadcast_to([128, 16, 2])
                pi128 = PTbI[:, q0:q0 + 16, NK]
                T1 = T1s[d]; T2 = T2s[d]
                er = e
                tt(er, T1, T1[:], x, x[:], PTbR, pr128, ALU.mult)
                tt(er, T2, T2[:, :, 0], x, x[:, :, 1], PTbR, pi128, ALU.mult)
                tt(er, T1, T1[:, :, 0], T1, T1[:, :, 0], T2, T2[:, :, 0], ALU.subtract)
                tt(er, T2, T2[:, :, 1], x, x[:, :, 0], PTbR, pi128, ALU.mult)
                tt(er, T1, T1[:, :, 1], T1, T1[:, :, 1], T2, T2[:, :, 1], ALU.add)
                tt(er, x, x[:], T1, T1[:], W, W[:], ALU.add)
                cap = capF if d == 0 else capB
                fw.op("dve", lambda g: g.scalar_tensor_tensor(out=ACC[d][:], in0=x[:], scalar=cap[:, 1 + blk:2 + blk], in1=ACC[d][:],
                                                              op0=ALU.mult, op1=ALU.add), outs=[ACC[d]], ins=[x, cap])
        fw.barrier()

    def dft_bufs(S1):
        nl = LL // 128
        return {"CMs": [sb(S1, "CM%d" % i, [128, nl, NB], BF16) for i in range(2)],
                "SMs": [sb(S1, "SM%d" % i, [128, nl, NB], BF16) for i in range(2)],
                "CC": [sb(S1, "CC%d" % i, [128, 128], BF16) for i in range(2)], "NSC": [sb(S1, "NSC%d" % i, [128, 128], BF16) for i in range(2)],
                "AC": sb(S1, "AC", [128, 4, NB], BF16), "AS": sb(S1, "AS", [128, 4, NB], BF16),
                "YFt": [sb(S1, "YFt%d" % i, [128, 4, NB], BF16) for i in range(2)],
                "pcs": [ps(S1, "pcs%d" % m, [128, 2, NB]) for m in range(4)],
                "pyfs": [ps(S1, "pyf%d" % i, [128, 2, NB]) for i in range(2)], "ik": 0}

    def dft_run(Bd, UF, nlt, nkc, cm_ap, sm_ap, cci, off):
        AC = Bd["AC"]; AS = Bd["AS"]; pcs = Bd["pcs"]; pyfs = Bd["pyfs"]
        CC = Bd["CC"][cci]; NSC = Bd["NSC"][cci]
        for kc in range(nkc):
            ik = Bd["ik"]; Bd["ik"] += 1
            CM = Bd["CMs"][ik % 2]; SM = Bd["SMs"][ik % 2]; YFt = Bd["YFt"][ik % 2]
            fw.dma(CM[:, 0:nlt, :], cm_ap[kc, :, :, :], outs=[CM])
            fw.dma(SM[:, 0:nlt, :], sm_ap[kc, :, :, :], outs=[SM])
            for m in range(4):
                for lt in range(nlt):
                    mm(pcs[m], pcs[m][:, 0, :], UF, UF[:, lt, m * 128:(m + 1) * 128], CM, CM[:, lt, :], lt == 0, lt == nlt - 1)
                for lt in range(nlt):
                    mm(pcs[m], pcs[m][:, 1, :], UF, UF[:, lt, m * 128:(m + 1) * 128], SM, SM[:, lt, :], lt == 0, lt == nlt - 1)
                act(AC, AC[:, m, :], pcs[m], pcs[m][:, 0, :], AF.Copy)
                cp("dve", AS, AS[:, m, :], pcs[m], pcs[m][:, 1, :])
            for m in range(4):
                pyf = pyfs[m // 2]
                mm(pyf, pyf[:, m % 2, :], CC, CC[:], AC, AC[:, m, :], True, False)
                mm(pyf, pyf[:, m % 2, :], NSC, NSC[:], AS, AS[:, m, :], False, True)
            for i in range(2):
                act(YFt, YFt[:, 2 * i:2 * i + 2, :], pyfs[i], pyfs[i][:], AF.Copy)
            fw.dma(yf_d[:, :, off + kc * NB:off + (kc + 1) * NB], YFt[:], outs=[], ins=[YFt])

    def zero_pads(US5p, L):
        fw.op("pool", lambda g: g.memset(US5p[:, :, 0:3], 0.0), outs=[US5p])
        fw.op("pool", lambda g: g.memset(US5p[:, :, 3 + L:6 + L], 0.0), outs=[US5p])

    def jof(t0):
        return 1 if t0 < LO else 0

    SUO = ExitStack()
    UF = sb(SUO, "UFall", [128, LL // 128, 512], BF16)
    UFs = [sb(SUO, "UFs%d" % i, [128, LS // 128, 512], BF16) for i in range(2)]
    with ExitStack() as SU:
        US5o = sb(SU, "US5o", [128, 4, LO + 6], BF16)
        US5s = [sb(SU, "US5s%d" % i, [128, 4, LS + 6], BF16) for i in range(2)]
        zero_pads(US5o, LO)
        for i in range(2):
            zero_pads(US5s[i], LS)
        with ExitStack() as SF:
            US5f = sb(SF, "US5f", [128, 4, LF + 6], BF16)
            zero_pads(US5f, LF)
            with ExitStack() as SW:
                w1 = sb(SW, "w_in1", [128, KT, 1024], BF16)
                load_w(SW, "w_in1", win_d, KT, 1024, None, w1)
                with ExitStack() as S1:
                    Bf = phase1_bufs(S1)
                    phase1(Bf, w1, xf_d, LF, 1, US5f, UF, 0)
                    phase1(Bf, w1, xo_d[:, :, 0:LO], LO, 1, US5o, UF, LF // 128)
                    for i in range(2):
                        off = LO + i * LS
                        phase1(Bf, w1, xo_d[:, :, off:off + LS], LS, 0, US5s[i], UFs[i], 0)
                    fw.barrier()
            with ExitStack() as S1:
                s5_far(S1, US5f)
            fw.barrier()
        with ExitStack() as S1:
            s5_run(S1, [(US5o, LO, ACC, 0, None)] + [(US5s[i], LS, ZST, LO + i * LS, i) for i in range(2)])
        fw.barrier()
    S3W = ExitStack()
    wgt = sb(S3W, "w_in2", [128, KT, 2048], BF16)
    wglu = sb(S3W, "wglu", [128, 4, 512], BF16); wp5 = sb(S3W, "wp5", [128, 4, D], BF16)
    wpf = sb(S3W, "wpf", [128, 4, D], BF16); wout = sb(S3W, "wout", [128, KT, D], BF16)
    for k in range(KT):
        fw.dma(wgt[:, k, :], win_d[:, k, 1024:3072], outs=[wgt], e="pool")
    load_w(S3W, "wglu", wglu_d, 4, 512, None, wglu)
    load_w(S3W, "wp5", wp5_d, 4, D, None, wp5)
    load_w(S3W, "wpf", wpf_d, 4, D, None, wpf)
    load_w(S3W, "wout", wout_d, KT, D, None, wout)
    with ExitStack() as S1:
        Bd = dft_bufs(S1)
        fw.dma(Bd["CC"][0][:], ccl_d[:, :], outs=[Bd["CC"][0]]); fw.dma(Bd["NSC"][0][:], nscl_d[:, :], outs=[Bd["NSC"][0]])
        fw.dma(Bd["CC"][1][:], ccs_d[:, :], outs=[Bd["CC"][1]]); fw.dma(Bd["NSC"][1][:], nscs_d[:, :], outs=[Bd["NSC"][1]])
        dft_run(Bd, UF, LL // 128, LO // NB, cml_d, sml_d, 0, 0)
        for i in range(2):
            dft_run(Bd, UFs[i], LS // 128, 1, cms_d, sms_d, 1, LO + i * LS)
        fw.barrier()

    with ExitStack() as S:
        pss = [ps(S, "pssA%d" % i, [128, NB]) for i in range(2)]
        pg = [ps(S, "pgA%d" % i, [128, NB]) for i in range(3)]
        pq = [ps(S, "pqA%d" % i, [128, NB]) for i in range(3)]
        xbs = [sb(S, "xbA%d" % i, [128, KT, NB], F32) for i in range(2)]
        hTs = [sb(S, "hTA%d" % i, [128, KT, NB], BF16) for i in range(2)]
        nt = norm_tmps(S, "nA_", NB)
        SG = sb(S, "SG", [128, 16, NB], BF16)
        yas = [sb(S, "yaA%d" % i, [128, 4, NB], BF16) for i in range(2)]
        ybs = [sb(S, "ybA%d" % i, [128, 4, NB], BF16) for i in range(2)]
        yfs = [sb(S, "yfA%d" % i, [128, 4, NB], BF16) for i in range(2)]
        zts = [sb(S, "ztA%d" % i, [128, NB], F32) for i in range(2)]
        ZB = sb(S, "ZB", [128, 4, NB], BF16)
        gus = [sb(S, "guA%d" % i, [128, NB], F32) for i in range(2)]
        sgls = [sb(S, "sgl%d" % i, [128, NB], F32) for i in range(2)]
        Y5 = sb(S, "Y5", [128, 4, NB], BF16)
        tqs = [sb(S, "tqA%d" % i, [128, NB], F32) for i in range(2)]
        MB = sb(S, "MB", [128, KT, NB], BF16)
        x1s = [sb(S, "x1A0", [128, KT, NB], F32)] * 2
        ig = 0; iq = 0

        def frontA(b):
            t0_ = b * NB
            fw.dma(xbs[b % 2][:], xo_d[:, :, t0_:t0_ + NB], outs=[xbs[b % 2]])
            fw.dma(yas[b % 2][:], ya_d[:, :, t0_:t0_ + NB], outs=[yas[b % 2]])
            fw.dma(ybs[b % 2][:], yb_d[:, :, t0_:t0_ + NB], outs=[ybs[b % 2]])
            fw.dma(yfs[b % 2][:], yf_d[:, :, t0_:t0_ + NB], outs=[yfs[b % 2]])
            norm_block(nt[b % 2], xbs[b % 2], NB, A1, B1, jof(t0_), hTs[b % 2], pss[b % 2])

        for b in range(NT // NB):
            t0 = b * NB
            j = jof(t0)
            xb = xbs[b % 2]; hT = hTs[b % 2]; ya = yas[b % 2]; yb = ybs[b % 2]; yf = yfs[b % 2]; x1 = x1s[b % 2]
            if b == 0:
                frontA(0)
            if b + 1 < NT // NB:
                frontA(b + 1)
            for m in range(16):
                p = pg[ig % 3]; ig += 1
                for kt in range(KT):
                    mm(p, p[:], wgt, wgt[:, kt, m * 128:(m + 1) * 128], hT, hT[:, kt, :], kt == 0, kt == KT - 1)
                act(SG, SG[:, m, :], p, p[:], AF.Sigmoid)
            for c in range(4):
                zt = zts[c % 2]; gu_ = gus[c % 2]
                tt("dve", zt, zt[:], ya, ya[:, c, :], yb, yb[:, c, :], ALU.add)
                act(gu_, gu_[:], zt, zt[:], AF.Square)
                tsc("dve", gu_, gu_[:], gu_, gu_[:], 0.044715, ALU.mult, 1.0, ALU.add)
                tt("dve", gu_, gu_[:], gu_, gu_[:], zt, zt[:], ALU.mult)
                act(gu_, gu_[:], gu_, gu_[:], AF.Sigmoid, scale=2.0 * math.sqrt(2.0 / PI))
                tt("dve", ZB, ZB[:, c, :], zt, zt[:], gu_, gu_[:], ALU.mult)
            for m in range(4):
                p = pq[iq % 3]; iq += 1
                sgl = sgls[m % 2]
                for k in range(4):
                    mm(p, p[:], wglu, wglu[:, k, m * 128:(m + 1) * 128], ZB, ZB[:, k, :], k == 0, k == 3)
                act(sgl, sgl[:], p, p[:], AF.Sigmoid, bias=bglu[:, m:m + 1], scale=1.0, extra=[bglu])
                tt("dve", Y5, Y5[:, m, :], ZB, ZB[:, m, :], sgl, sgl[:], ALU.mult)
            for m in range(KT):
                p1 = pg[ig % 3]; ig += 1
                p2 = pq[iq % 3]; iq += 1
                tq = tqs[m % 2]; sgl = sgls[m % 2]
                for k in range(4):
                    mm(p1, p1[:], wp5, wp5[:, k, m * 128:(m + 1) * 128], Y5, Y5[:, k, :], k == 0, k == 3)
                for k in range(4):
                    mm(p2, p2[:], wpf, wpf[:, k, m * 128:(m + 1) * 128], yf, yf[:, k, :], k == 0, k == 3)
                tt("dve", tq, tq[:], p1, p1[:], SG, SG[:, m, :], ALU.mult)
                tt("dve", sgl, sgl[:], p2, p2[:], SG, SG[:, 8 + m, :], ALU.mult)
                tt("pool", MB, MB[:, m, :], tq, tq[:], sgl, sgl[:], ALU.add)
            for m in range(KT):
                p = pg[ig % 3]; ig += 1
                for k in range(KT):
                    mm(p, p[:], wout, wout[:, k, m * 128:(m + 1) * 128], MB, MB[:, k, :], k == 0, k == KT - 1)
                stt(x1, x1[:, m, :], p, p[:], G1[:, m, j:j + 1], xb, xb[:, m, :], ALU.mult, ALU.add, extra=[G1])
            fw.dma(x1_d[:, :, t0:t0 + NB], x1[:], outs=[], ins=[x1])
        fw.barrier()
    S3W.close()
    SUO.close()
    if debug:
        dd = dout("dbg_x1", [128, KT, NT], F32)
        fw.dma(dd, x1_d[:, :, :], outs=[], ins=[])
        fw.barrier()
    if stop == "3a":
        es.close()
        return nc
    BN = 512
    gu_d = dscr("gu_s", [128, FT, NT], BF16)
    with ExitStack() as S:
        GRP = [(0, 6), (6, 12), (12, 17), (17, 22)]
        wgs = [sb(S, "wg_%d" % i, [128, KT, (b_ - a_) * 128], BF16) for i, (a_, b_) in enumerate(GRP)]
        wus = [sb(S, "wu_%d" % i, [128, KT, (b_ - a_) * 128], BF16) for i, (a_, b_) in enumerate(GRP)]
        for i, (a_, b_) in enumerate(GRP):
            fw.dma(wgs[i][:], wg_d[:, :, a_ * 128:b_ * 128], outs=[wgs[i]], e="pool")
            fw.dma(wus[i][:], wu_d[:, :, a_ * 128:b_ * 128], outs=[wus[i]], e="pool")
        gof = {}
        for i, (a_, b_) in enumerate(GRP):
            for m_ in range(a_, b_):
                gof[m_] = (i, m_ - a_)
        pss = [ps(S, "pssB%d" % i, [128, BN]) for i in range(2)]
        pg = [ps(S, "pgB%d" % i, [128, BN]) for i in range(3)]
        pq = [ps(S, "pqB%d" % i, [128, BN]) for i in range(2)]
        xbs = [sb(S, "xbB%d" % i, [128, KT, BN], F32) for i in range(2)]
        hTs = [sb(S, "hTB%d" % i, [128, KT, BN], BF16) for i in range(2)]
        nt = norm_tmps(S, "nB_", BN, nbuf=1) * 2
        sgts = [sb(S, "sgtB%d" % i, [128, BN], BF16) for i in range(2)]
        GU = sb(S, "GU", [128, FT, BN], BF16)
        ig = 0; iq = 0
        nbb = NT // BN

        def frontB(b):
            t0_ = b * BN
            fw.dma(xbs[b % 2][:], x1_d[:, :, t0_:t0_ + BN], outs=[xbs[b % 2]])
            norm_block(nt[b % 2], xbs[b % 2], BN, A2, B2, jof(t0_), hTs[b % 2], pss[b % 2])

        frontB(0)
        for b in range(nbb):
            t0 = b * BN
            hT = hTs[b % 2]
            if b + 1 < nbb:
                frontB(b + 1)
            for m in range(FT):
                p1 = pg[ig % 3]; ig += 1
                p2 = pq[iq % 2]; iq += 1
                sgt = sgts[m % 2]
                wg_, wu_ = wgs[gof[m][0]], wus[gof[m][0]]
                c0 = gof[m][1] * 128
                for k in range(KT):
                    mm(p1, p1[:], wg_, wg_[:, k, c0:c0 + 128], hT, hT[:, k, :], k == 0, k == KT - 1)
                for k in range(KT):
                    mm(p2, p2[:], wu_, wu_[:, k, c0:c0 + 128], hT, hT[:, k, :], k == 0, k == KT - 1)
                act(sgt, sgt[:], p1, p1[:], AF.Silu)
                tt("dve", GU, GU[:, m, :], p2, p2[:], sgt, sgt[:], ALU.mult)
            fw.dma(gu_d[:, :, t0:t0 + BN], GU[:], outs=[], ins=[GU])
        fw.barrier()
    with ExitStack() as S:
        wds = [sb(S, "wd_%d" % m_, [128, FT, 256], BF16) for m_ in range(4)]
        for m_ in range(4):
            fw.dma(wds[m_][:], wd_d[:, :, m_ * 256:(m_ + 1) * 256], outs=[wds[m_]], e="pool")
        pssf = [ps(S, "pssF%d" % i, [128, BN]) for i in range(2)]
        pg = [ps(S, "pgD%d" % i, [128, BN]) for i in range(4)]
        xbs = [sb(S, "xbD%d" % i, [128, KT, BN], F32) for i in range(2)]
        GUs = [sb(S, "GUD%d" % i, [128, FT, BN], BF16) for i in range(2)]
        ntf = [{"sq": sb(S, "nF_sq%d" % i, [128, KT, BN], BF16), "rstd": sb(S, "nF_rstd%d" % i, [128, BN], F32), "tmp": None} for i in range(2)]
        ig = 0
        nbb = NT // BN

        def frontD(b):
            t0_ = b * BN
            fw.dma(xbs[b % 2][:], x1_d[:, :, t0_:t0_ + BN], outs=[xbs[b % 2]])
            fw.dma(GUs[b % 2][:], gu_d[:, :, t0_:t0_ + BN], outs=[GUs[b % 2]])

        frontD(0)
        for b in range(nbb):
            t0 = b * BN
            j = jof(t0)
            xb = xbs[b % 2]; GU = GUs[b % 2]
            if b + 1 < nbb:
                frontD(b + 1)
            for m in range(KT):
                p = pg[ig % 4]; ig += 1
                for k in range(FT):
                    mm(p, p[:], wds[m // 2], wds[m // 2][:, k, (m % 2) * 128:(m % 2 + 1) * 128], GU, GU[:, k, :], k == 0, k == FT - 1)
                stt(xb, xb[:, m, :], p, p[:], G2[:, m, j:j + 1], xb, xb[:, m, :], ALU.mult, ALU.add, extra=[G2])
            rstd = norm_block(ntf[b % 2], xb, BN, None, None, j, None, pssf[b % 2])
            for m in range(KT):
                stt(xb, xb[:, m, :], xb, xb[:, m, :], gft[:, m:m + 1], rstd, rstd[:], ALU.mult, ALU.mult, extra=[gft])
            fw.dma(yo_d[:, :, t0:t0 + BN], xb[:], outs=[], ins=[xb])
        fw.barrier()
    fw.barrier()
    es.close()
    return nc


def _fm(x2d):
    T = x2d.shape[0]
    return np.ascontiguousarray(x2d.T.reshape(-1, 128, T).transpose(1, 0, 2))


def _unfm(a):
    return np.ascontiguousarray(a.transpose(1, 0, 2).reshape(-1, a.shape[2]).T)


def _wl(w):
    K, N = w.shape
    return np.ascontiguousarray(w.reshape(K // 128, 128, N).transpose(1, 0, 2))


def _vl(v):
    return np.ascontiguousarray(v.reshape(-1, 128).T)


_NC_CACHE = {}


def make_in_maps(x_prompt, x_sample, state_s5, c, c_ctx, norm1_g, norm2_g, w_ada, b_ada, w_in,
                 s5_lambda_re, s5_lambda_im, s5_log_step, s5_b_re, s5_b_im, s5_c_re, s5_c_im,
                 s5_d, w_glu, b_glu, w_proj_s5, w_proj_fft, w_out, w_ffn_gate, w_ffn_up,
                 w_ffn_down, final_norm_g):
    f32 = np.float32
    bf = ml_dtypes.bfloat16
    A = lambda a: np.asarray(a, dtype=f32)
    x_prompt, x_sample, state_s5, c, c_ctx = A(x_prompt), A(x_sample), A(state_s5), A(c), A(c_ctx)
    lam_re, lam_im, lstep = A(s5_lambda_re)[0], A(s5_lambda_im)[0], A(s5_log_step)[0]
    b_re, b_im, c_re, c_im = A(s5_b_re)[0], A(s5_b_im)[0], A(s5_c_re)[0], A(s5_c_im)[0]
    common = {
        "w_ada": _wl(A(w_ada)[0]), "b_ada": _vl(A(b_ada)[0]),
        "g1": _vl(A(norm1_g)[0]), "g2": _vl(A(norm2_g)[0]), "gf": _vl(A(final_norm_g)),
        "w_in": _wl(A(w_in)[0]), "w_glu": _wl(A(w_glu)[0]), "b_glu": _vl(A(b_glu)[0]),
        "s5d": _vl(A(s5_d)[0]), "w_p5": _wl(A(w_proj_s5)[0]), "w_pf": _wl(A(w_proj_fft)[0]),
        "w_out": _wl(A(w_out)[0]), "w_g": _wl(A(w_ffn_gate)[0]), "w_u": _wl(A(w_ffn_up)[0]),
        "w_d": _wl(A(w_ffn_down)[0]),
    }
    lrs = np.zeros((128, 32), f32); lis = np.zeros((128, 32), f32); lss = np.zeros((128, 32), f32)
    lrp = np.zeros((128, 2, 16, 2, 64), f32); lip = np.zeros_like(lrp); lsp = np.zeros_like(lrp)
    brp = np.zeros_like(lrp); bip = np.zeros_like(lrp)
    crp = np.zeros((128, 2, 16, 128), f32); cip = np.zeros_like(crp)
    for d in range(2):
        for pr in range(16):
            for g2 in range(2):
                g = 2 * pr + g2
                lrs[g2 * 64:(g2 + 1) * 64, d * 16 + pr] = lam_re[d, g]
                lis[g2 * 64:(g2 + 1) * 64, d * 16 + pr] = lam_im[d, g]
                lss[g2 * 64:(g2 + 1) * 64, d * 16 + pr] = lstep[d, g]
                lrp[:, d, pr, g2, :] = lam_re[d, g][None, :]
                lip[:, d, pr, g2, :] = lam_im[d, g][None, :]
                lsp[:, d, pr, g2, :] = lstep[d, g]
                gi = g % 8
                brp[gi * 16:(gi + 1) * 16, d, pr, g2, :] = b_re[d, g].T
                bip[gi * 16:(gi + 1) * 16, d, pr, g2, :] = b_im[d, g].T
                crp[g2 * 64:(g2 + 1) * 64, d, pr, gi * 16:(gi + 1) * 16] = c_re[d, g].T
                cip[g2 * 64:(g2 + 1) * 64, d, pr, gi * 16:(gi + 1) * 16] = c_im[d, g].T
    lrq = np.zeros((128, 128), f32); liq = np.zeros((128, 128), f32); lsq = np.zeros((128, 128), f32)
    for d in range(2):
        for pr in range(16):
            q = d * 16 + pr
            for g2 in range(2):
                g = 2 * pr + g2
                lrq[q, g2 * 64:(g2 + 1) * 64] = lam_re[d, g]
                liq[q, g2 * 64:(g2 + 1) * 64] = lam_im[d, g]
                lsq[q, g2 * 64:(g2 + 1) * 64] = lstep[d, g]
    lrq[32:] = lrq[0]; liq[32:] = liq[0]; lsq[32:] = lsq[0]
    common.update({"lrs": lrs, "lis": lis, "lss": lss, "lrq": lrq, "liq": liq, "lsq": lsq,
                   "brp": brp.reshape(128, 4096), "bip": bip.reshape(128, 4096),
                   "crp": crp.reshape(128, 4096), "cip": cip.reshape(128, 4096),
                   "h0s": np.zeros((128, 64), f32)})
    ch = np.arange(128, dtype=np.int64)
    angc = (2 * np.pi / 128) * ((ch[:, None] * ch[None, :]) % 128).astype(np.float64)
    for nm, L in (("l", LL), ("s", LS)):
        sc = 1.0 / math.sqrt(L * 128.0)
        common["cc" + nm] = (np.cos(angc) * sc).astype(f32).astype(bf)
        common["nsc" + nm] = (-np.sin(angc) * sc).astype(f32).astype(bf)

    def lay(m, L_l, L_k):
        return np.ascontiguousarray(m.reshape(L_l // 128, 128, L_k // NB, NB).transpose(2, 1, 0, 3)).astype(bf)

    l = np.arange(LS, dtype=np.int64)
    ang = (2 * np.pi / LS) * ((l[:, None] * l[None, :]) % LS).astype(np.float64)
    common["cms"] = lay(np.cos(ang).astype(f32), LS, LS); common["sms"] = lay(np.sin(ang).astype(f32), LS, LS)
    common["ones"] = np.ones((128, 128), f32).astype(bf)
    dft_own = {}
    for j in range(4):
        s = j * LO
        lord = np.concatenate([np.arange(0, s), np.arange(s + LO, LL), np.arange(s, s + LO)]).astype(np.int64)
        k = np.arange(s, s + LO, dtype=np.int64)
        ang = (2 * np.pi / LL) * ((lord[:, None] * k[None, :]) % LL).astype(np.float64)
        dft_own[j] = (lay(np.cos(ang).astype(f32), LL, LO), lay(np.sin(ang).astype(f32), LL, LO))
    nfb = NFB
    common["rampb"] = np.tile((TL * np.arange(NK + 1, dtype=np.float32))[None, :], (128, 1))
    common["rampf"] = np.tile((TL * (NK - 1 - np.arange(NK, dtype=np.float32)))[None, :], (128, 1))
    in_maps = []
    for i in range(8):
        b = i // 4; j = i % 4; s = j * LO
        m = dict(common)
        xs_ = x_sample[b]
        m["xf"] = _fm(np.concatenate([xs_[:s], xs_[s + LO:]], axis=0))
        m["xo"] = _fm(np.concatenate([xs_[s:s + LO], x_prompt[2 * i], x_prompt[2 * i + 1]], axis=0))
        cT = np.stack([c_ctx, c[b]], axis=1)
        m["cT"] = np.ascontiguousarray(cT.reshape(KT, 128, 2).transpose(1, 0, 2))
        h0 = np.zeros((128, 2, 16, 2), f32)
        for d in range(2):
            for ri in range(2):
                for pr in range(16):
                    for g2 in range(2):
                        h0[g2 * 64:(g2 + 1) * 64, d, pr, ri] = state_s5[b, 0, d, ri, 2 * pr + g2]
        m["h0l"] = h0.reshape(128, 64)
        capf = np.zeros((128, nfb + 1), f32); capb = np.zeros((128, nfb + 1), f32)
        if j == 0:
            capf[:, 0] = 1.0
        else:
            capf[:, 1 + (s // FB - 1)] = 1.0
        if j == 3:
            capb[:, 0] = 1.0
        else:
            capb[:, 1 + s // FB] = 1.0
        m["capf"] = capf; m["capb"] = capb
        m["cml"], m["sml"] = dft_own[j]
        in_maps.append(m)
    return in_maps


def kernel(**inputs):
    f32 = np.float32
    in_maps = make_in_maps(**inputs)
    if "nc" not in _NC_CACHE:
        _NC_CACHE["nc"] = build()
    nc = _NC_CACHE["nc"]
    res = run_bass_kernel_spmd(nc, in_maps, core_ids=list(range(8)))
    R = res.results
    y_sample = np.zeros((2, LL, D), f32)
    y_prompt = np.zeros((16, LS, D), f32)
    new_state = np.zeros((16, 1, 2, 2, 32, 64), f32)
    for i in range(8):
        b = i // 4; j = i % 4; s = j * LO
        yo = _unfm(np.asarray(R[i]["yo"], dtype=f32))
        y_sample[b, s:s + LO] = yo[:LO]
        for k in range(2):
            n = 2 * i + k
            y_prompt[n] = yo[LO + k * LS:LO + (k + 1) * LS]
            st = np.asarray(R[i]["st"][k], dtype=f32).reshape(128, 2, 16, 2)
            for d in range(2):
                for ri in range(2):
                    for pr in range(16):
                        for g2 in range(2):
                            new_state[n, 0, d, ri, 2 * pr + g2] = st[g2 * 64:(g2 + 1) * 64, d, pr, ri]
    return (y_prompt, y_sample, new_state)
```
